# Optimizing a Trainium2 kernel written in Bass

```python
import math
import jax
import jax.numpy as jnp
from jax import lax
import numpy as np

D_MODEL = 1024
BATCH = 8
SEQ = 4096
DEPTH = 2
DEC_BATCH = 8
DEC_SEQ = 32
PAST_LEN = 2048

CHUNK = 64
NORM_EPS = 1e-5
F32 = jnp.float32

SSD_INNER = D_MODEL
SSD_HEAD_DIM = 64
SSD_HEADS = SSD_INNER // SSD_HEAD_DIM
SSD_GROUPS = 2
SSD_HPG = SSD_HEADS // SSD_GROUPS
SSD_STATE = 128
SSD_CONV = 4
SSD_CONV_DIM = SSD_INNER + 2 * SSD_GROUPS * SSD_STATE

GLA_HEADS = 4
GLA_KEY_DIM = D_MODEL // 2
GLA_VAL_DIM = D_MODEL
GLA_DK = GLA_KEY_DIM // GLA_HEADS
GLA_DV = GLA_VAL_DIM // GLA_HEADS
GLA_GATE_RANK = 16
GLA_GATE_NORMALIZER = 16.0

RWKV_HEAD = 64
RWKV_DIM = D_MODEL
RWKV_HEADS = RWKV_DIM // RWKV_HEAD
RWKV_DECAY_RANK = 64
RWKV_ICLR_RANK = 64
RWKV_SHIFT_COLS = 3 * RWKV_DIM + RWKV_DECAY_RANK + RWKV_ICLR_RANK
RWKV_LN_EPS = 64e-5

MEM_LEN = 256
XA_HEADS = 4
XA_HEAD_DIM = D_MODEL // XA_HEADS

N_BRANCHES = 3
IN_SPLITS = (SSD_INNER, SSD_CONV_DIM, SSD_HEADS,
             GLA_KEY_DIM, GLA_KEY_DIM, GLA_VAL_DIM, GLA_VAL_DIM, GLA_GATE_RANK,
             RWKV_SHIFT_COLS, RWKV_DIM,
             N_BRANCHES * D_MODEL)
IN_COLS = sum(IN_SPLITS)

kernel_name = 'hybrid_ssd_gla_rwkv7_stream_step'


def _split(t, sizes):
    idx = []
    acc = 0
    for s in sizes[:-1]:
        acc += s
        idx.append(acc)
    return jnp.split(t, idx, axis=-1)


def _rmsnorm(x, g):
    xf = x.astype(F32)
    y = xf * lax.rsqrt(jnp.mean(xf * xf, axis=-1, keepdims=True) + NORM_EPS)
    return (y * g.astype(F32)).astype(x.dtype)


def _gated_group_rmsnorm(y, z, g, groups):
    b, L, W = y.shape
    t = (y * jax.nn.silu(z)).astype(F32).reshape(b, L, groups, W // groups)
    t = t * lax.rsqrt(jnp.mean(t * t, axis=-1, keepdims=True) + NORM_EPS)
    return (t.reshape(b, L, W) * g.astype(F32)).astype(y.dtype)


def _head_rmsnorm(x, g):
    xf = x.astype(F32)
    y = xf * lax.rsqrt(jnp.mean(xf * xf, axis=-1, keepdims=True) + NORM_EPS)
    return (y * g.astype(F32)).astype(x.dtype)


def _head_layernorm(x, g, bias):
    b, L, H, N = x.shape
    xf = x.astype(F32)
    mu = jnp.mean(xf, axis=-1, keepdims=True)
    var = jnp.mean(jnp.square(xf - mu), axis=-1, keepdims=True)
    y = ((xf - mu) * lax.rsqrt(var + RWKV_LN_EPS)).reshape(b, L, H * N)
    return (y * g.astype(F32) + bias.astype(F32)).astype(x.dtype)


def _causal_dwconv(x, buf, w, bias):
    L, C = x.shape[1], x.shape[2]
    xf = jnp.concatenate([buf, x], axis=1)
    out = lax.conv_general_dilated(xf, w[:, None, :], window_strides=(1,), padding='VALID',
                                   dimension_numbers=('NWC', 'WIO', 'NWC'), feature_group_count=C)
    return out + bias, xf[:, L:]


def _ssd_chunked(x, dt, A, Bm, Cm, h0):
    b, L = x.shape[0], x.shape[1]
    cs = min(CHUNK, L)
    nc = L // cs
    G, E, P, N = SSD_GROUPS, SSD_HPG, SSD_HEAD_DIM, SSD_STATE
    dtype = x.dtype
    xc = (x * dt.astype(dtype)[..., None]).reshape(b, nc, cs, G, E, P)
    Bc = Bm.reshape(b, nc, cs, G, N)
    Cc = Cm.reshape(b, nc, cs, G, N)
    acs = jnp.cumsum((dt * A).reshape(b, nc, cs, G, E), axis=2)
    causal = jnp.tril(jnp.ones((cs, cs), bool))[:, :, None, None]
    seg = acs[:, :, :, None] - acs[:, :, None, :]
    Lmat = jnp.exp(jnp.where(causal, seg, -jnp.inf)).astype(dtype)
    CB = jnp.einsum('bclgn,bcsgn->bclsg', Cc, Bc)
    y_diag = jnp.einsum('bclsg,bclsge,bcsgep->bclgep', CB, Lmat, xc)
    to_end = jnp.exp(acs[:, :, -1:] - acs).astype(dtype)
    chunk_states = jnp.einsum('bclgn,bclge,bclgep->bcgepn', Bc, to_end, xc)
    chunk_decay = jnp.exp(acs[:, :, -1])

    def step(h, inp):
        st, dec = inp
        return (h * dec[..., None, None] + st).astype(h.dtype), h

    hT, h_in = lax.scan(step, h0.reshape(b, G, E, P, N),
                        (jnp.moveaxis(chunk_states, 1, 0), jnp.moveaxis(chunk_decay, 1, 0)))
    h_in = jnp.moveaxis(h_in, 0, 1).astype(dtype)
    y_off = jnp.einsum('bclgn,bcgepn,bclge->bclgep', Cc, h_in, jnp.exp(acs).astype(dtype))
    y = (y_diag + y_off).reshape(b, L, G * E, P)
    return y, hT.reshape(b, G * E, P, N)


def _gla_chunked(q, k, v, glog, h0):
    b, L, H, K = q.shape
    V = v.shape[-1]
    cs = min(CHUNK, L)
    nc = L // cs
    dtype = q.dtype
    qc = q.reshape(b, nc, cs, H, K) * (K ** -0.5)
    kc = k.reshape(b, nc, cs, H, K)
    vc = v.reshape(b, nc, cs, H, V)
    gcs = jnp.cumsum(glog.reshape(b, nc, cs, H, K), axis=2)
    q_e = (qc * jnp.exp(gcs)).astype(dtype)
    k_e = (kc * jnp.exp(-gcs)).astype(dtype)
    causal = jnp.tril(jnp.ones((cs, cs), bool))
    A = jnp.einsum('bclhk,bcshk->bchls', q_e, k_e)
    A = jnp.where(causal, A, 0).astype(dtype)
    o_intra = jnp.einsum('bchls,bcshv->bclhv', A, vc)
    k_end = (kc * jnp.exp(gcs[:, :, -1:] - gcs)).astype(dtype)
    chunk_states = jnp.einsum('bclhk,bclhv->bchkv', k_end, vc)
    chunk_decay = jnp.exp(gcs[:, :, -1])

    def step(hs, inp):
        st, dec = inp
        return (hs * dec[..., None] + st).astype(hs.dtype), hs

    hT, h_in = lax.scan(step, h0, (jnp.moveaxis(chunk_states, 1, 0), jnp.moveaxis(chunk_decay, 1, 0)))
    h_in = jnp.moveaxis(h_in, 0, 1).astype(dtype)
    o_inter = jnp.einsum('bclhk,bchkv->bclhv', q_e, h_in)
    return (o_intra + o_inter).reshape(b, L, H, V), hT


def _rwkv7_scan(r, w, k, v, kk, a, S0):
    def step(S, inp):
        r_t, w_t, k_t, v_t, kk_t, a_t = inp
        s_kk = jnp.einsum('bhvk,bhk->bhv', S, kk_t)
        S_new = (S * w_t[:, :, None, :] - s_kk[..., None] * (kk_t * a_t)[:, :, None, :]
                 + v_t[..., None] * k_t[:, :, None, :]).astype(S.dtype)
        return S_new, jnp.einsum('bhvk,bhk->bhv', S_new, r_t)

    xs = tuple(jnp.moveaxis(t, 1, 0) for t in (r, w, k, v, kk, a))
    ST, o = lax.scan(step, S0, xs)
    return jnp.moveaxis(o, 0, 1), ST


def _mixer(u, lp, ssd_h, conv_buf, gla_h, rwkv_h, shift_buf):
    b, L, _ = u.shape
    dtype = u.dtype
    proj = jnp.einsum('bld,dc->blc', u, lp['w_in'])
    (z, xbc, dt_raw, gq, gk, gv, ggate, glr, rf, rgate, merge) = _split(proj, IN_SPLITS)

    xbc, conv_new = _causal_dwconv(xbc, conv_buf, lp['ssd_conv_w'], lp['ssd_conv_b'])
    xbc = jax.nn.silu(xbc)
    xs, Bm, Cm = _split(xbc, (SSD_INNER, SSD_GROUPS * SSD_STATE, SSD_GROUPS * SSD_STATE))
    xs = xs.reshape(b, L, SSD_HEADS, SSD_HEAD_DIM)
    Bm = Bm.reshape(b, L, SSD_GROUPS, SSD_STATE)
    Cm = Cm.reshape(b, L, SSD_GROUPS, SSD_STATE)
    dt = jax.nn.softplus(dt_raw.astype(F32) + lp['ssd_dt_bias'].astype(F32))
    A = -jnp.exp(lp['ssd_A_log'].astype(F32))
    y, ssd_new = _ssd_chunked(xs, dt, A, Bm, Cm, ssd_h)
    y = y + xs * lp['ssd_D'][:, None]
    o_ssd = _gated_group_rmsnorm(y.reshape(b, L, SSD_INNER), z, lp['ssd_norm'], SSD_GROUPS)

    q = gq.reshape(b, L, GLA_HEADS, GLA_DK)
    k = gk.reshape(b, L, GLA_HEADS, GLA_DK)
    v = gv.reshape(b, L, GLA_HEADS, GLA_DV)
    glog = jax.nn.log_sigmoid((glr @ lp['gla_gk_w2'] + lp['gla_gk_b']).astype(F32)) / GLA_GATE_NORMALIZER
    o, gla_new = _gla_chunked(q, k, v, glog.reshape(b, L, GLA_HEADS, GLA_DK), gla_h)
    o_gla = (_head_rmsnorm(o, lp['gla_norm']) * jax.nn.silu(ggate.reshape(b, L, GLA_HEADS, GLA_DV))).reshape(b, L, GLA_VAL_DIM)

    rf_all = jnp.concatenate([shift_buf, rf], axis=1)
    shift_new = rf_all[:, L:]
    rf = rf + (rf_all[:, :L] - rf) * lp['rwkv_mu']
    r7, k7, v7, wl, al = _split(rf, (RWKV_DIM, RWKV_DIM, RWKV_DIM, RWKV_DECAY_RANK, RWKV_ICLR_RANK))
    w_pre = (lp['rwkv_w0'] + jnp.tanh(wl) @ lp['rwkv_w2']).astype(F32)
    decay = jnp.exp(-jnp.exp(-jax.nn.softplus(-w_pre) - 0.5))
    a = jax.nn.sigmoid((lp['rwkv_a0'] + al @ lp['rwkv_a2']).astype(F32)).astype(dtype)
    hd = lambda t: t.reshape(b, L, RWKV_HEADS, RWKV_HEAD)
    kkf = hd(k7 * lp['rwkv_k_k']).astype(F32)
    kk = (kkf / jnp.maximum(jnp.linalg.norm(kkf, axis=-1, keepdims=True), 1e-12)).astype(dtype)
    k7 = k7 * (1 + (a - 1) * lp['rwkv_k_a'])
    r_h, k_h, v_h, a_h = hd(r7), hd(k7), hd(v7), hd(a)
    o7, rwkv_new = _rwkv7_scan(r_h, hd(decay), k_h, v_h, kk, a_h, rwkv_h)
    bonus = jnp.sum(r_h * k_h * lp['rwkv_r_k'], axis=-1, keepdims=True) * v_h
    o_rwkv = (_head_layernorm(o7, lp['rwkv_ln_w'], lp['rwkv_ln_b']) + bonus.reshape(b, L, RWKV_DIM)) * jax.nn.silu(rgate)

    s = jax.nn.sigmoid((merge.reshape(b, L, N_BRANCHES, D_MODEL) + lp['b_merge']).astype(F32)).astype(dtype)
    m = (s[:, :, 0] * (o_ssd @ lp['w_proj_ssd'])
         + s[:, :, 1] * (o_gla @ lp['w_proj_gla'])
         + s[:, :, 2] * (o_rwkv @ lp['w_proj_rwkv']))
    out = m @ lp['w_out']
    return out, (ssd_new, conv_new, gla_new, rwkv_new, shift_new)


def _cross_attn(u, mk, mv, wq, wo):
    b, L, _ = u.shape
    q = (u @ wq).reshape(b, L, XA_HEADS, XA_HEAD_DIM)
    sc = jnp.einsum('blhd,bmhd->bhlm', q, mk).astype(F32) * (XA_HEAD_DIM ** -0.5)
    p = jax.nn.softmax(sc, axis=-1).astype(u.dtype)
    o = jnp.einsum('bhlm,bmhd->blhd', p, mv).reshape(b, L, D_MODEL)
    return o @ wo


def _mem_kv(mem, g, wk, wv):
    b, M, _ = mem.shape
    mn = _rmsnorm(mem, g)
    return ((mn @ wk).reshape(b, M, XA_HEADS, XA_HEAD_DIM), (mn @ wv).reshape(b, M, XA_HEADS, XA_HEAD_DIM))


def _trunk(h, mem_k, mem_v, ssd_h, conv_buf, gla_h, rwkv_h, shift_buf, P):
    new = ([], [], [], [], [])
    for l in range(DEPTH):
        lp = {name: arr[l] for name, arr in P.items()}
        mix, st = _mixer(_rmsnorm(h, lp['norm_mix']), lp, ssd_h[l], conv_buf[l], gla_h[l], rwkv_h[l], shift_buf[l])
        h = h + mix
        h = h + _cross_attn(_rmsnorm(h, lp['norm_xattn']), mem_k[l], mem_v[l], lp['xa_wq'], lp['xa_wo'])
        for lst, s_ in zip(new, st):
            lst.append(s_)
    return h, tuple(jnp.stack(lst, axis=0) for lst in new)


def setup_inputs(seed: int = 0) -> dict:
    key = jax.random.key(seed)
    ks = iter(jax.random.split(key, 64))

    def nrm(shape, scale):
        return scale * jax.random.normal(next(ks), shape, F32)

    def gain(shape):
        return 1.0 + nrm(shape, 0.02)

    Dm = D_MODEL
    x_prompt = nrm((BATCH, SEQ, Dm), 1.0)
    x_sample = nrm((DEC_BATCH, DEC_SEQ, Dm), 1.0)
    mem_prompt = nrm((BATCH, MEM_LEN, Dm), 1.0)
    state_ssd = nrm((DEPTH, DEC_BATCH, SSD_HEADS, SSD_HEAD_DIM, SSD_STATE), 0.1)
    state_ssd_conv = nrm((DEPTH, DEC_BATCH, SSD_CONV - 1, SSD_CONV_DIM), 1.0)
    state_gla = nrm((DEPTH, DEC_BATCH, GLA_HEADS, GLA_DK, GLA_DV), 0.1)
    state_rwkv = nrm((DEPTH, DEC_BATCH, RWKV_HEADS, RWKV_HEAD, RWKV_HEAD), 0.3)
    state_rwkv_shift = nrm((DEPTH, DEC_BATCH, 1, RWKV_SHIFT_COLS), 1.0)
    cache_mem_k = nrm((DEPTH, DEC_BATCH, MEM_LEN, XA_HEADS, XA_HEAD_DIM), 1.0)
    cache_mem_v = nrm((DEPTH, DEC_BATCH, MEM_LEN, XA_HEADS, XA_HEAD_DIM), 1.0)
    norm_mix = gain((DEPTH, Dm))
    w_in = nrm((DEPTH, Dm, IN_COLS), Dm ** -0.5)
    ssd_conv_w = nrm((DEPTH, SSD_CONV, SSD_CONV_DIM), 0.5)
    ssd_conv_b = nrm((DEPTH, SSD_CONV_DIM), 0.02)
    dt0 = jnp.exp(jax.random.uniform(next(ks), (DEPTH, SSD_HEADS), F32, math.log(1e-3), math.log(1e-1)))
    ssd_dt_bias = dt0 + jnp.log(-jnp.expm1(-dt0))
    ssd_A_log = jnp.log(jax.random.uniform(next(ks), (DEPTH, SSD_HEADS), F32, 1.0, 16.0))
    ssd_D = 1.0 + nrm((DEPTH, SSD_HEADS), 0.1)
    ssd_norm = gain((DEPTH, SSD_INNER))
    w_proj_ssd = nrm((DEPTH, SSD_INNER, Dm), SSD_INNER ** -0.5)
    gla_gk_w2 = nrm((DEPTH, GLA_GATE_RANK, GLA_KEY_DIM), GLA_GATE_RANK ** -0.5)
    gla_gk_b = nrm((DEPTH, GLA_KEY_DIM), 0.1)
    gla_norm = gain((DEPTH, GLA_DV))
    w_proj_gla = nrm((DEPTH, GLA_VAL_DIM, Dm), GLA_VAL_DIM ** -0.5)
    rwkv_mu = jax.random.uniform(next(ks), (DEPTH, RWKV_SHIFT_COLS), F32)
    rwkv_w0 = jax.random.uniform(next(ks), (DEPTH, RWKV_DIM), F32, -6.0, 1.0)
    rwkv_w2 = nrm((DEPTH, RWKV_DECAY_RANK, RWKV_DIM), 0.1)
    rwkv_a0 = nrm((DEPTH, RWKV_DIM), 0.1)
    rwkv_a2 = nrm((DEPTH, RWKV_ICLR_RANK, RWKV_DIM), 0.1)
    rwkv_k_k = 0.85 + nrm((DEPTH, RWKV_DIM), 0.05)
    rwkv_k_a = 1.0 + nrm((DEPTH, RWKV_DIM), 0.05)
    rwkv_r_k = nrm((DEPTH, RWKV_HEADS, RWKV_HEAD), 0.1)
    rwkv_ln_w = gain((DEPTH, RWKV_DIM))
    rwkv_ln_b = nrm((DEPTH, RWKV_DIM), 0.02)
    w_proj_rwkv = nrm((DEPTH, RWKV_DIM, Dm), RWKV_DIM ** -0.5)
    b_merge = nrm((DEPTH, N_BRANCHES, Dm), 0.1)
    w_out = nrm((DEPTH, Dm, Dm), Dm ** -0.5)
    norm_xattn = gain((DEPTH, Dm))
    xa_wq = nrm((DEPTH, Dm, Dm), Dm ** -0.5)
    xa_wo = nrm((DEPTH, Dm, Dm), Dm ** -0.5)
    norm_mem = gain((DEPTH, Dm))
    xa_wk = nrm((DEPTH, Dm, Dm), Dm ** -0.5)
    xa_wv = nrm((DEPTH, Dm, Dm), Dm ** -0.5)
    norm_final = gain((Dm,))
    return {'x_prompt': x_prompt, 'x_sample': x_sample, 'mem_prompt': mem_prompt,
            'state_ssd': state_ssd, 'state_ssd_conv': state_ssd_conv, 'state_gla': state_gla,
            'state_rwkv': state_rwkv, 'state_rwkv_shift': state_rwkv_shift,
            'cache_mem_k': cache_mem_k, 'cache_mem_v': cache_mem_v,
            'norm_mix': norm_mix, 'w_in': w_in, 'ssd_conv_w': ssd_conv_w, 'ssd_conv_b': ssd_conv_b,
            'ssd_dt_bias': ssd_dt_bias, 'ssd_A_log': ssd_A_log, 'ssd_D': ssd_D, 'ssd_norm': ssd_norm,
            'w_proj_ssd': w_proj_ssd, 'gla_gk_w2': gla_gk_w2, 'gla_gk_b': gla_gk_b, 'gla_norm': gla_norm,
            'w_proj_gla': w_proj_gla, 'rwkv_mu': rwkv_mu, 'rwkv_w0': rwkv_w0, 'rwkv_w2': rwkv_w2,
            'rwkv_a0': rwkv_a0, 'rwkv_a2': rwkv_a2, 'rwkv_k_k': rwkv_k_k, 'rwkv_k_a': rwkv_k_a,
            'rwkv_r_k': rwkv_r_k, 'rwkv_ln_w': rwkv_ln_w, 'rwkv_ln_b': rwkv_ln_b, 'w_proj_rwkv': w_proj_rwkv,
            'b_merge': b_merge, 'w_out': w_out, 'norm_xattn': norm_xattn, 'xa_wq': xa_wq, 'xa_wo': xa_wo,
            'norm_mem': norm_mem, 'xa_wk': xa_wk, 'xa_wv': xa_wv, 'norm_final': norm_final}


def reference(x_prompt, x_sample, mem_prompt, state_ssd, state_ssd_conv, state_gla, state_rwkv, state_rwkv_shift,
              cache_mem_k, cache_mem_v, norm_mix, w_in, ssd_conv_w, ssd_conv_b, ssd_dt_bias, ssd_A_log, ssd_D,
              ssd_norm, w_proj_ssd, gla_gk_w2, gla_gk_b, gla_norm, w_proj_gla, rwkv_mu, rwkv_w0, rwkv_w2,
              rwkv_a0, rwkv_a2, rwkv_k_k, rwkv_k_a, rwkv_r_k, rwkv_ln_w, rwkv_ln_b, w_proj_rwkv, b_merge, w_out,
              norm_xattn, xa_wq, xa_wo, norm_mem, xa_wk, xa_wv, norm_final):
    P = dict(norm_mix=norm_mix, w_in=w_in, ssd_conv_w=ssd_conv_w, ssd_conv_b=ssd_conv_b, ssd_dt_bias=ssd_dt_bias,
             ssd_A_log=ssd_A_log, ssd_D=ssd_D, ssd_norm=ssd_norm, w_proj_ssd=w_proj_ssd, gla_gk_w2=gla_gk_w2,
             gla_gk_b=gla_gk_b, gla_norm=gla_norm, w_proj_gla=w_proj_gla, rwkv_mu=rwkv_mu, rwkv_w0=rwkv_w0,
             rwkv_w2=rwkv_w2, rwkv_a0=rwkv_a0, rwkv_a2=rwkv_a2, rwkv_k_k=rwkv_k_k, rwkv_k_a=rwkv_k_a,
             rwkv_r_k=rwkv_r_k, rwkv_ln_w=rwkv_ln_w, rwkv_ln_b=rwkv_ln_b, w_proj_rwkv=w_proj_rwkv,
             b_merge=b_merge, w_out=w_out, norm_xattn=norm_xattn, xa_wq=xa_wq, xa_wo=xa_wo)

    kv = [_mem_kv(mem_prompt, norm_mem[l], xa_wk[l], xa_wv[l]) for l in range(DEPTH)]
    mem_k_p = jnp.stack([t[0] for t in kv], axis=0)
    mem_v_p = jnp.stack([t[1] for t in kv], axis=0)
    bp = x_prompt.shape[0]
    zeros = lambda shape: jnp.zeros((DEPTH, bp) + shape, x_prompt.dtype)
    hp, (p_ssd, p_conv, p_gla, p_rwkv, p_shift) = _trunk(
        x_prompt, mem_k_p, mem_v_p,
        zeros((SSD_HEADS, SSD_HEAD_DIM, SSD_STATE)), zeros((SSD_CONV - 1, SSD_CONV_DIM)),
        zeros((GLA_HEADS, GLA_DK, GLA_DV)), zeros((RWKV_HEADS, RWKV_HEAD, RWKV_HEAD)),
        zeros((1, RWKV_SHIFT_COLS)), P)
    y_prompt = _rmsnorm(hp, norm_final)

    hs, (s_ssd, s_conv, s_gla, s_rwkv, s_shift) = _trunk(
        x_sample, cache_mem_k, cache_mem_v, state_ssd, state_ssd_conv, state_gla, state_rwkv, state_rwkv_shift, P)
    y_sample = _rmsnorm(hs, norm_final)

    return (y_prompt, y_sample, p_ssd, p_conv, p_gla, p_rwkv, p_shift, mem_k_p, mem_v_p,
            s_ssd, s_conv, s_gla, s_rwkv, s_shift)
```

```python
import numpy as np
from contextlib import ExitStack
import concourse.bass as bass
import concourse.mybir as mybir
from concourse.bass_utils import run_bass_kernel_spmd

F32 = mybir.dt.float32
BF16 = mybir.dt.bfloat16
AF = mybir.ActivationFunctionType
ALU = mybir.AluOpType
AX = mybir.AxisListType

D = 1024
IN_COLS = 12960
EPS = 1e-5
LN_EPS = 64e-5
LAM = float(np.exp(-0.5))
C_Z, C_XBC, C_DT, C_GQ, C_GK, C_GV, C_GG, C_GLR, C_RF, C_RG, C_MG = (
    0, 1024, 2560, 2576, 3088, 3600, 4624, 5648, 5664, 8864, 9888)


class View:
    def __init__(self, buf, ap):
        self.buf = buf
        self.ap = ap

    def __getitem__(self, idx):
        return View(self.buf, self.ap[idx])

    def rr(self, pat, **kw):
        return View(self.buf, self.ap.rearrange(pat, **kw))

    def bc(self, shape):
        return View(self.buf, self.ap.to_broadcast(list(shape)))

    def unsq(self, ax):
        return View(self.buf, self.ap.unsqueeze(ax))


class Buf:
    def __init__(self, t, name):
        self.t = t
        self.name = name
        self.lw = None
        self.rd = {}
        self.box = None

    def __getitem__(self, idx):
        return View(self, self.t[idx])

    def v(self):
        return View(self, self.t[:])


def bufs_of(v):
    b = v.buf
    if b is None:
        return []
    return b if isinstance(b, (list, tuple)) else [b]


class PsBuf:
    def __init__(self, t, name):
        self.t = t
        self.banks = [Buf(None, name + "_b0"), Buf(None, name + "_b1")]
        for b in self.banks:
            b.is_psum = True

    def __getitem__(self, idx):
        cols = idx[1] if isinstance(idx, tuple) and len(idx) > 1 else slice(None)
        lo = cols.start or 0
        hi = 1024 if cols.stop is None else cols.stop
        bs = [self.banks[i] for i in range(2) if lo < (i + 1) * 512 and hi > i * 512]
        return View(bs, self.t[idx])

    def v(self):
        return View(list(self.banks), self.t[:])


class Eng:
    def __init__(self, name, sem):
        self.name = name
        self.sem = sem
        self.n = 0
        self.waited = {}
        self.prog = []
        self.ops = []
        self.rank = []


class Kern:
    def __init__(self, nc, es):
        self.nc = nc
        self.es = es
        self.nsem = 0
        self.pe = Eng("pe", self.sem("pe"))
        self.act = Eng("act", self.sem("act"))
        self.dve = Eng("dve", self.sem("dve"))
        self.pool = Eng("pool", self.sem("pool"))
        self.sp = Eng("sp", self.sem("sp"))
        self.final = []
        self.flip = 0
        self.nbuf = 0

    def sem(self, name):
        self.nsem += 1
        return self.es.enter_context(self.nc.semaphore(name + str(self.nsem)))

    def sb(self, shape, dt, name=None):
        self.nbuf += 1
        name = (name or "b") + "_" + str(self.nbuf)
        return Buf(self.es.enter_context(self.nc.sbuf_tensor(name, list(shape), dt)), name)

    def psb(self, shape, name=None):
        self.nbuf += 1
        name = (name or "ps") + "_" + str(self.nbuf)
        return PsBuf(self.es.enter_context(self.nc.psum_tensor(name, list(shape), F32)), name)

    def dram(self, name, shape, dt, kind):
        return Buf(self.nc.dram_tensor(name, list(shape), dt, kind=kind).ap(), name)

    def _sync(self, E, reads, writes):
        deps = []
        for v in reads:
            for b in bufs_of(v):
                if b.lw is not None:
                    deps.append(b.lw)
                if getattr(b, "is_psum", False):
                    deps.extend(t for t in b.rd.values() if t[2] is not E)
        for v in writes:
            for b in bufs_of(v):
                if b.lw is not None:
                    deps.append(b.lw)
                deps.extend(b.rd.values())
        for (sem, val, eng) in deps:
            if eng is E and E.name == "pe":
                continue
            k = id(sem)
            if E.waited.get(k, 0) >= val:
                continue
            E.waited[k] = val
            if eng is None:
                E.prog.append(("waitD", sem, val))
            else:
                eng.ops[val - 1][1] = True
                E.prog.append(("waitE", eng, val))

    def emit(self, E, fn, reads, writes):
        reads = [r for r in reads if isinstance(r, View)]
        self._sync(E, reads, writes)
        E.ops.append([fn, False])
        n = len(E.ops)
        E.prog.append(("op", n - 1))
        tok = (E.sem, n, E)
        for v in reads:
            for b in bufs_of(v):
                b.rd[id(E.sem)] = tok
        for v in writes:
            for b in bufs_of(v):
                b.lw = tok
                b.rd = {}

    def assemble(self, E, e):
        for item in E.prog:
            kind = item[0]
            if kind == "op":
                fn, mark = E.ops[item[1]]
                ins = fn(e)
                if mark:
                    ins.then_inc(E.sem, 1)
            elif kind == "waitE":
                eng, val = item[1], item[2]
                e.wait_ge(eng.sem, eng.rank[val - 1])
            elif kind == "waitD":
                e.wait_ge(item[1], item[2])
            else:
                item[1](e)

    def finalize_ranks(self):
        for E in (self.pe, self.act, self.dve, self.pool, self.sp):
            r, c = [], 0
            for (fn, mark) in E.ops:
                if mark:
                    c += 1
                r.append(c)
            E.rank = r

    def dma(self, E, out, in_, final=False, noncontig=False, nosync=False):
        if not nosync:
            self._sync(E, [in_], [out])
        ob, ib = out.buf, in_.buf
        if final:
            if ib.box is None:
                ib.box = SemBox()
            bx = ib.box
            if bx.ssem is None:
                bx.ssem = self.sem("s")
            bx.scnt += 16
            sem, val = bx.ssem, bx.scnt
            ib.rd[id(sem)] = (sem, val, None)
            self.final = [f for f in self.final if f[0] is not sem] + [(sem, val)]
        else:
            if ob.box is None:
                ob.box = SemBox()
            bx = ob.box
            if bx.dsem is None:
                bx.dsem = self.sem("d")
            bx.dcnt += 16
            sem, val = bx.dsem, bx.dcnt
            ob.lw = (sem, val, None)
            ob.rd = {}
        nc = self.nc
        if noncontig:
            def f(e, o=out.ap, i=in_.ap, sem=sem):
                with nc.allow_non_contiguous_dma(reason="small strided state/const transfer"):
                    e.dma_start(out=o, in_=i).then_inc(sem, 16)
        else:
            def f(e, o=out.ap, i=in_.ap, sem=sem):
                e.dma_start(out=o, in_=i).then_inc(sem, 16)
        E.prog.append(("raw", f))

    def mm(self, out, lhsT, rhs, start=True, stop=True):
        self.emit(self.pe, lambda e: e.matmul(out.ap, lhsT.ap, rhs.ap, start=start, stop=stop),
                  [lhsT, rhs], [out])

    def tr(self, out, in_, ident):
        self.emit(self.pe, lambda e: e.transpose(out.ap, in_.ap, ident.ap), [in_, ident], [out])

    def actf(self, out, in_, func, bias=None, scale=None, accum=None):
        kw = {}
        if bias is not None:
            kw["bias"] = bias.ap if isinstance(bias, View) else bias
        if scale is not None:
            kw["scale"] = scale.ap if isinstance(scale, View) else scale
        if accum is not None:
            kw["accum_out"] = accum.ap
        w = [out] + ([accum] if accum is not None else [])
        self.emit(self.act, lambda e: e.activation(out=out.ap, in_=in_.ap, func=func, **kw),
                  [in_, bias, scale], w)

    def tt(self, E, out, a, b, op):
        self.emit(E, lambda e: e.tensor_tensor(out=out.ap, in0=a.ap, in1=b.ap, op=op), [a, b], [out])

    def ts(self, E, out, a, s1, op0, s2=None, op1=None):
        x1 = s1.ap if isinstance(s1, View) else s1
        x2 = s2.ap if isinstance(s2, View) else s2
        if op1 is None:
            self.emit(E, lambda e: e.tensor_scalar(out=out.ap, in0=a.ap, scalar1=x1, scalar2=None, op0=op0),
                      [a, s1], [out])
        else:
            self.emit(E, lambda e: e.tensor_scalar(out=out.ap, in0=a.ap, scalar1=x1, scalar2=x2, op0=op0, op1=op1),
                      [a, s1, s2], [out])

    def stt(self, E, out, a, s, b, op0, op1):
        x = s.ap if isinstance(s, View) else s
        self.emit(E, lambda e: e.scalar_tensor_tensor(out=out.ap, in0=a.ap, scalar=x, in1=b.ap, op0=op0, op1=op1),
                  [a, s, b], [out])

    def cp(self, E, out, a):
        if E is self.act:
            self.actf(out, a, AF.Copy)
        else:
            self.emit(E, lambda e: e.tensor_copy(out=out.ap, in_=a.ap), [a], [out])

    def evac(self, out, a):
        self.flip ^= 1
        self.cp(self.act if self.flip else self.dve, out, a)

    def red(self, E, out, a, op):
        self.emit(E, lambda e: e.tensor_reduce(out=out.ap, in_=a.ap, axis=AX.X, op=op), [a], [out])

    def recip(self, out, a):
        self.emit(self.dve, lambda e: e.reciprocal(out=out.ap, in_=a.ap), [a], [out])

    def memset(self, E, out, val):
        self.emit(E, lambda e: e.memset(out.ap, val), [], [out])

    def aselect(self, out, in_, pattern, op, fill, base, cm):
        self.emit(self.pool, lambda e: e.affine_select(out=out.ap, in_=in_.ap, pattern=pattern, compare_op=op,
                                                       fill=fill, base=base, channel_multiplier=cm), [in_], [out])


class SemBox:
    def __init__(self):
        self.dsem = None
        self.dcnt = 0
        self.ssem = None
        self.scnt = 0


class Slot:
    def __init__(self, K, i):
        self.t = K.es.enter_context(K.nc.sbuf_tensor("wslot%d" % i, [128, 1024], F32))
        self.hist = {}
        self.box = [SemBox(), SemBox()]
        self.used = [False, False]
        self.i = i


class Pool:
    def __init__(self, K, nslots):
        self.K = K
        self.slots = [Slot(K, i) for i in range(nslots)]
        self.stack = []
        self.n = 0

    def push(self):
        self.stack.append([])

    def pop(self):
        for (slot, halves, b) in self.stack.pop():
            self._free(slot, halves, b)

    def _free(self, slot, halves, b):
        for tok in ([b.lw] if b.lw is not None else []) + list(b.rd.values()):
            k = id(tok[0])
            if k not in slot.hist or slot.hist[k][1] < tok[1]:
                slot.hist[k] = tok
        for h in halves:
            slot.used[h] = False

    def release(self, b):
        for fr in reversed(self.stack):
            for i, (slot, halves, bb) in enumerate(fr):
                if bb is b:
                    self._free(slot, halves, bb)
                    del fr[i]
                    return
        raise AssertionError("release: buffer not found")

    def F(self):
        for slot in self.slots:
            if not slot.used[0] and not slot.used[1]:
                slot.used = [True, True]
                self.n += 1
                b = Buf(slot.t, "F%d_%d" % (slot.i, self.n))
                b.box = slot.box[0]
                b.rd = dict(slot.hist)
                self.stack[-1].append((slot, (0, 1), b))
                return b
        raise AssertionError("work pool exhausted (F)")

    def H(self, scope=-1):
        cand = [sl for sl in self.slots if sl.used[0] != sl.used[1]] + \
               [sl for sl in self.slots if not sl.used[0] and not sl.used[1]]
        assert cand, "work pool exhausted (H)"
        slot = cand[0]
        h = 0 if not slot.used[0] else 1
        slot.used[h] = True
        self.n += 1
        b = Buf(slot.t[:].bitcast(BF16)[:, h * 1024:(h + 1) * 1024], "H%d_%d_%d" % (slot.i, h, self.n))
        b.box = slot.box[h]
        b.rd = dict(slot.hist)
        self.stack[scope].append((slot, (h,), b))
        return b


def build(NT, NSLOTS=18, stage=9):
    nc = bass.Bass("TRN2", target_bir_lowering=False)
    es = ExitStack()
    K = Kern(nc, es)
    pe, act, dve, pool, sp = K.pe, K.act, K.dve, K.pool, K.sp
    T = NT * 128

    def din(name, shape):
        return K.dram(name, shape, F32, "ExternalInput")

    def dout(name, shape):
        b = K.dram(name, shape, F32, "ExternalOutput")
        b.is_out = True
        return b

    x_prompt = din("x_prompt", [T, D])
    x_sample = din("x_sample", [32, D])
    mem_prompt = din("mem_prompt", [256, D])
    st_ssd = din("state_ssd", [2, 16, 64, 128])
    st_conv = din("state_ssd_conv", [2, 3, 1536])
    st_gla = din("state_gla", [2, 4, 128, 256])
    st_rwkv = din("state_rwkv", [2, 16, 64, 64])
    st_shift = din("state_rwkv_shift", [2, 3200])
    c_mk = din("cache_mem_k", [2, 256, D])
    c_mv = din("cache_mem_v", [2, 256, D])
    W = {}
    for nm, shp in [("norm_mix", [2, D]), ("w_in", [2, D, IN_COLS]), ("ssd_conv_w", [2, 4, 1536]),
                    ("ssd_conv_b", [2, 1536]), ("ssd_dt_bias", [2, 16]), ("ssd_A_log", [2, 16]),
                    ("ssd_D", [2, 16]), ("ssd_norm", [2, D]), ("w_proj_ssd", [2, D, D]),
                    ("gla_gk_w2", [2, 16, 512]), ("gla_gk_b", [2, 512]), ("gla_norm", [2, 256]),
                    ("w_proj_gla", [2, D, D]), ("rwkv_mu", [2, 3200]), ("rwkv_w0", [2, D]),
                    ("rwkv_w2", [2, 64, D]), ("rwkv_a0", [2, D]), ("rwkv_a2", [2, 64, D]),
                    ("rwkv_k_k", [2, D]), ("rwkv_k_a", [2, D]), ("rwkv_r_k", [2, D]),
                    ("rwkv_ln_w", [2, D]), ("rwkv_ln_b", [2, D]), ("w_proj_rwkv", [2, D, D]),
                    ("b_merge", [2, 3 * D]), ("w_out", [2, D, D]), ("norm_xattn", [2, D]),
                    ("xa_wq", [2, D, D]), ("xa_wo", [2, D, D]), ("norm_mem", [2, D]),
                    ("xa_wk", [2, D, D]), ("xa_wv", [2, D, D]), ("norm_final", [D])]:
        W[nm] = din(nm, shp)

    y_prompt = dout("y_prompt", [T, D])
    y_sample = dout("y_sample", [32, D])
    O = {}
    for g in ("p", "s"):
        O[g + "_ssd"] = dout(g + "_ssd", [2, 16, 64, 128])
        O[g + "_conv"] = dout(g + "_conv", [2, 3, 1536])
        O[g + "_gla"] = dout(g + "_gla", [2, 4, 128, 256])
        O[g + "_rwkv"] = dout(g + "_rwkv", [2, 16, 64, 64])
        O[g + "_shift"] = dout(g + "_shift", [2, 3200])
    mem_k_o = dout("mem_k", [2, 256, D])
    mem_v_o = dout("mem_v", [2, 256, D])

    big = ["xa_wk", "xa_wv", "w_in", "w_proj_ssd", "w_proj_gla", "w_proj_rwkv", "w_out", "xa_wq", "xa_wo"]
    WB = {}

    def convert_weights():
        for l in range(2):
            for nm in big:
                cols = IN_COLS if nm == "w_in" else D
                WB[(nm, l)] = K.dram("%s_bf%d" % (nm, l), [D, cols], BF16, "Internal")
                for k in range(0, 8, 2):
                    K.dma(pool, WB[(nm, l)][k * 128:(k + 2) * 128, :], W[nm][l, k * 128:(k + 2) * 128, :], nosync=True)

    ident = K.sb([128, 128], F32, "ident")
    tri_i = K.sb([128, 128], F32, "tri_i")
    tri_s = K.sb([128, 128], F32, "tri_s")
    tri_l = K.sb([128, 128], F32, "tri_l")
    onesf = K.sb([128, 128], F32, "onesf")
    mk4 = K.sb([128, 4, 128], F32, "mk4")
    identb = K.sb([128, 128], BF16, "identb")
    blkb = K.sb([128, 128], BF16, "blkb")
    indb = K.sb([128, 2], BF16, "indb")
    ones1 = K.sb([1, 128], BF16, "ones1")
    K.memset(pool, ident.v(), 0.0)
    K.aselect(ident.v(), ident.v(), [[-1, 128]], ALU.not_equal, 1.0, 0, 1)
    K.memset(pool, onesf.v(), 1.0)
    K.aselect(tri_i.v(), onesf.v(), [[1, 128]], ALU.is_ge, 0.0, 0, -1)
    K.aselect(tri_s.v(), onesf.v(), [[1, 128]], ALU.is_gt, 0.0, 0, -1)
    K.aselect(tri_l.v(), onesf.v(), [[-1, 128]], ALU.is_gt, 0.0, 0, 1)
    for q in range(4):
        K.cp(pool, mk4[:, q, :], (tri_s if q % 2 == 0 else tri_i).v())
    K.cp(pool, identb.v(), ident.v())
    K.memset(pool, blkb.v(), 0.0)
    K.memset(pool, blkb[0:64, 0:64], 1.0)
    K.memset(pool, blkb[64:128, 64:128], 1.0)
    K.memset(pool, indb.v(), 0.0)
    K.memset(pool, indb[0:64, 0:1], 1.0)
    K.memset(pool, indb[64:128, 1:2], 1.0)
    K.memset(pool, ones1.v(), 1.0)
    trr = K.sb([128, 256], F32, "trr")
    K.cp(pool, trr[:, 0:128], tri_i.v())
    K.cp(pool, trr[:, 128:256], tri_s.v())
    junk = K.sb([128, 1024], BF16, "junk")
    convert_weights()

    WP = Pool(K, NSLOTS)
    WP.push()

    ST = []
    for l in range(2):
        s = dict(
            ssd=K.sb([128, 1024], F32, "Sssd"), ssd_b=K.sb([128, 1024], BF16, "Sssdb"),
            gla=K.sb([128, 1024], F32, "Sgla"), gla_b=K.sb([128, 1024], BF16, "Sglab"),
            rw=K.sb([128, 8, 64], F32, "Srw"), rw_b=K.sb([128, 8, 64], BF16, "Srwb"),
            conv=K.sb([128, 12, 3], F32, "convst"), shift=K.sb([128, 25], F32, "shiftst"),
            KT=K.sb([128, 8, 256], BF16, "KT"), Vm=K.sb([128, 2, 1024], BF16, "Vm"))
        ST.append(s)

    xin = K.sb([128, 12, 131], F32, "xin")
    hbuf = [K.sb([128, 1024], F32, "h") for _ in range(2)]
    mbuf = K.sb([128, 1024], F32, "m")
    uT = K.sb([128, 8, 128], BF16, "uT")
    stat = K.sb([128, 4], F32, "stat")
    st_g = K.sb([128, 8], F32, "st_g")
    st_cdec = K.sb([128, 4], F32, "st_cdec")
    st_EC = K.sb([128, 8], F32, "st_EC")
    st_bonus = K.sb([128, 16], F32, "st_bonus")
    st_mv = K.sb([128, 16], F32, "st_mv")
    st_mx = K.sb([128, 4], F32, "st_mx")
    st_rs = K.sb([128, 4], F32, "st_rs")
    st_dt = K.sb([128, 16], F32, "st_dt")
    st_dtA = K.sb([128, 16], F32, "st_dtA")
    st_edec = K.sb([128, 16], F32, "st_edec")
    rfT8 = K.sb([128, 8, 129], F32, "rfT8")
    NW = 4
    wring = [K.sb([128, 8, 512], BF16, "wblk") for _ in range(NW)]
    wri = [0]
    NPS = 4
    psring = [K.psb([128, 1024], "ps") for _ in range(NPS)]
    psi = [0]

    def PS():
        psi[0] = (psi[0] + 1) % NPS
        return psring[psi[0]]

    def load_fm(dst, src1d, J):
        WP.push()
        t = WP.F()
        K.dma(sp, t[0:J, 0:128], src1d.rr("(j p) -> j p", p=128))
        ps = PS()
        K.tr(ps[:, 0:J], t[0:J, 0:128], ident[0:J, 0:J])
        K.cp(dve, dst, ps[:, 0:J])
        WP.pop()

    def store_fm(dst1d, src, J):
        WP.push()
        t = WP.F()
        ps = PS()
        K.tr(ps[0:J, 0:128], src, ident.v())
        K.cp(dve, t[0:J, 0:128], ps[0:J, 0:128])
        K.dma(pool, dst1d.rr("(j p) -> j p", p=128), t[0:J, 0:128], final=True)
        WP.pop()

    def loadw(nm, l, c0, ncols):
        wri[0] = (wri[0] + 1) % NW
        wb = wring[wri[0]]
        K.dma(sp, wb[:, :, 0:ncols], WB[(nm, l)].v().rr("(k p) c -> p k c", p=128)[:, :, c0:c0 + ncols])
        return wb

    LC = []
    for l in range(2):
        c = {}

        def fm(nm, j, key=None, src=None):
            b = K.sb([128, j], F32, nm)
            load_fm(b.v(), W[nm][l], j)
            c[key or nm] = b
        fm("norm_mix", 8)
        fm("norm_xattn", 8)
        fm("norm_mem", 8)
        fm("ssd_norm", 8)
        fm("gla_norm", 2)
        fm("ssd_conv_b", 12)
        fm("rwkv_mu", 25)
        fm("rwkv_a0", 8)
        fm("rwkv_k_k", 8)
        fm("rwkv_k_a", 8)
        fm("rwkv_r_k", 8)
        fm("b_merge", 24)
        g8 = K.sb([128, 8], F32, "gn8")
        for j in range(8):
            K.cp(pool, g8[:, j:j + 1], c["gla_norm"][:, (j % 2):(j % 2) + 1])
        c["gla_norm8"] = g8
        cw = K.sb([128, 12, 4], F32, "convw")
        for k in range(4):
            load_fm(cw[:, :, k], W["ssd_conv_w"][l, k], 12)
        c["conv_w"] = cw
        a_t = K.sb([128, 16], F32, "A")
        K.dma(sp, a_t.v(), View(None, W["ssd_A_log"].t[l].partition_broadcast(128)))
        K.actf(a_t.v(), a_t.v(), AF.Exp)
        K.ts(dve, a_t.v(), a_t.v(), -1.0, ALU.mult)
        c["A"] = a_t
        d_t = K.sb([128, 16], F32, "Dsk")
        K.dma(sp, d_t.v(), View(None, W["ssd_D"].t[l].partition_broadcast(128)))
        c["Dsk"] = d_t
        def brow(nm, n, key):
            tmp = WP.F()
            hi = K.sb([1, n], BF16, key + "hi")
            lo = K.sb([1, n], BF16, key + "lo")
            K.dma(sp, tmp[0:1, 0:n], W[nm][l:l + 1, :] if len(W[nm].t.shape) == 2 else W[nm][l:l + 1])
            K.cp(dve, hi.v(), tmp[0:1, 0:n])
            K.tt(dve, tmp[0:1, 0:n], tmp[0:1, 0:n], hi.v(), ALU.subtract)
            K.cp(dve, lo.v(), tmp[0:1, 0:n])
            c[key] = (hi, lo)
        WP.push()
        brow("ssd_dt_bias", 16, "dtb")
        brow("gla_gk_b", 512, "gkb")
        brow("rwkv_w0", 1024, "w0")
        t1 = WP.F()
        gk2 = K.sb([128, 512], BF16, "gk2")
        K.memset(pool, t1.v(), 0.0)
        K.dma(sp, t1[0:16, 0:512], W["gla_gk_w2"][l])
        K.cp(dve, gk2.v(), t1[:, 0:512])
        c["gk2"] = gk2
        t2 = WP.F()
        w2b = K.sb([128, 1024], BF16, "w2a2")
        K.dma(sp, t2[0:64, :], W["rwkv_w2"][l])
        K.dma(sp, t2[64:128, :], W["rwkv_a2"][l])
        K.cp(dve, w2b.v(), t2.v())
        c["w2a2"] = w2b
        WP.pop()
        LC.append(c)

    def blocks(c0, n):
        out = []
        while n > 0:
            b = min(512, n)
            out.append((c0, b))
            c0 += b
            n -= b
        return out

    def proj_tm(nm, l, nt, lhs, c0, n, ps, pcol0=0, bias=None):
        pc = pcol0
        for (cc, b) in blocks(c0, n):
            wb = loadw(nm, l, cc, b)
            segs = []
            s0 = 0
            while s0 < b:
                e0 = min(b, s0 + (512 - (pc + s0) % 512))
                segs.append((s0, e0))
                s0 = e0
            for (s0, e0) in segs:
                o = ps[0:nt, pc + s0:pc + e0]
                for k in range(8):
                    K.mm(o, lhs[:, k, 0:nt], wb[:, k, s0:e0], start=(k == 0), stop=(k == 7 and bias is None))
                if bias is not None:
                    hi, lo, b0 = bias
                    off = b0 + (cc - c0) + s0
                    K.mm(o, ones1[0:1, 0:nt], hi[0:1, off:off + (e0 - s0)], start=False, stop=False)
                    K.mm(o, ones1[0:1, 0:nt], lo[0:1, off:off + (e0 - s0)], start=False, stop=True)
            pc += b

    def proj_fm(nm, l, nt, rhs, c0, n, sink):
        j = 0
        for (cc, b) in blocks(c0, n):
            wb = loadw(nm, l, cc, b)
            ps = PS()
            nch = (b + 127) // 128
            for q in range(nch):
                w_ = min(128, b - q * 128)
                o = ps[0:w_, q * 128:q * 128 + nt]
                for k in range(8):
                    K.mm(o, wb[:, k, q * 128:q * 128 + w_], rhs[:, k, 0:nt], start=(k == 0), stop=(k == 7))
            sink(j, ps, nch, b)
            j += nch

    def proj_fm_g(nm, l, nt, rhs, c0, n, sink):
        j = 0
        for (cc, b) in blocks(c0, n):
            wb = loadw(nm, l, cc, b)
            ps = PS()
            nch = (b + 127) // 128
            for q in range(nch):
                w_ = min(128, b - q * 128)
                o = ps[0:w_, q * 128:q * 128 + nt]
                for k in range(8):
                    K.mm(o, wb[:, k, q * 128:q * 128 + w_], rhs[:, k, 0:nt], start=(k == 0), stop=(k == 7))
            sink(j, ps, nch, b)
            yield
            j += nch

    def rms_stats(src, nt, ncols, col, scale_n, stat=stat):
        K.actf(junk[0:nt, 0:ncols], src, AF.Square, scale=float(scale_n ** -0.5), accum=stat[0:nt, col:col + 1])
        K.actf(stat[0:nt, col:col + 1], stat[0:nt, col:col + 1], AF.Ln, bias=EPS, scale=1.0)
        K.actf(stat[0:nt, col:col + 1], stat[0:nt, col:col + 1], AF.Exp, scale=-0.5)

    def to_fm(src, nt, dst, gvec=None):
        for half in range(2):
            ps = PS()
            for q in range(4):
                k = half * 4 + q
                K.tr(ps[:, q * 128:q * 128 + nt], src[:, k * 128:(k + 1) * 128], ident[0:nt, 0:nt])
            pv = ps[:, 0:512].rr("p (q t) -> p q t", q=4)[:, :, 0:nt]
            if gvec is None:
                K.evac(dst[:, half * 4:half * 4 + 4, 0:nt], pv)
            else:
                K.tt(dve, dst[:, half * 4:half * 4 + 4, 0:nt], pv,
                     gvec[:, half * 4:half * 4 + 4].unsq(2).bc([128, 4, nt]), ALU.mult)

    def norm_to_uT(h, nt, gkey, l):
        WP.push()
        rms_stats(h[0:nt, :], nt, 1024, 0, 1024)
        hn = WP.F()
        K.ts(dve, hn[0:nt, :], h[0:nt, :], stat[0:nt, 0:1], ALU.mult)
        to_fm(hn[0:nt, :], nt, uT, LC[l][gkey])
        WP.pop()

    def gate_accum(l, nt, oTv, wname, bidx, first):
        for half in range(2):
            WP.push()
            psm = PS()
            psp = PS()
            wbm = loadw("w_in", l, C_MG + bidx * 1024 + half * 512, 512)
            for q in range(4):
                for k in range(8):
                    K.mm(psm[:, q * 128:q * 128 + nt], wbm[:, k, q * 128:(q + 1) * 128], uT[:, k, 0:nt],
                         start=(k == 0), stop=(k == 7))
            wbp = loadw(wname, l, half * 512, 512)
            for q in range(4):
                for k in range(8):
                    K.mm(psp[:, q * 128:q * 128 + nt], wbp[:, k, q * 128:(q + 1) * 128], oTv[:, k, 0:nt],
                         start=(k == 0), stop=(k == 7))
            sg = WP.F()
            for q in range(4):
                j = half * 4 + q
                K.actf(sg[:, q * 128:q * 128 + nt], psm[:, q * 128:q * 128 + nt], AF.Sigmoid,
                       bias=LC[l]["b_merge"][:, bidx * 8 + j:bidx * 8 + j + 1], scale=1.0)
            mv = mbuf.v().rr("p (j t) -> p j t", j=8)[:, half * 4:half * 4 + 4, 0:nt]
            sgv = sg[:, 0:512].rr("p (q t) -> p q t", q=4)[:, :, 0:nt]
            pv = psp[:, 0:512].rr("p (q t) -> p q t", q=4)[:, :, 0:nt]
            if first:
                K.tt(dve, mv, pv, sgv, ALU.mult)
            else:
                K.tt(dve, sgv, pv, sgv, ALU.mult)
                K.tt(pool, mv, mv, sgv, ALU.add)
            WP.pop()

    def ssd_branch(l, nt):
        c, s = LC[l], ST[l]
        WP.push()
        sz = WP.F()
        psz = PS()
        proj_tm("w_in", l, nt, uT, C_Z, 1024, psz)
        K.actf(sz[0:nt, :], psz[0:nt, :], AF.Silu)
        yield
        psd = PS()
        proj_tm("w_in", l, nt, uT, C_DT, 16, psd, 0, bias=(c["dtb"][0], c["dtb"][1], 0))
        dt = st_dt[0:nt, 0:16]
        K.actf(dt, psd[0:nt, 0:16], AF.Exp)
        K.actf(dt, dt, AF.Ln, bias=1.0, scale=1.0)
        dtA = st_dtA[0:nt, 0:16]
        K.tt(dve, dtA, dt, c["A"][0:nt, :], ALU.mult)
        yield
        K.cp(dve, xin[:, :, 0:3], s["conv"].v())

        def sink(j, ps, nch, b):
            K.evac(xin[:, j:j + nch, 3:3 + nt], ps[:, 0:nch * 128].rr("p (q t) -> p q t", q=nch)[:, :, 0:nt])
        yield from proj_fm_g("w_in", l, nt, uT, C_XBC, 1536, sink)
        yield "P"
        xc = [WP.F(), WP.F()]

        def xcv(j):
            return xc[j // 8][:, (j % 8) * 128:(j % 8) * 128 + nt]
        for j in range(12):
            e = dve
            K.ts(e, xcv(j), xin[:, j, 0:nt], c["conv_w"][:, j, 0:1], ALU.mult, c["ssd_conv_b"][:, j:j + 1], ALU.add)
            for k in range(1, 4):
                K.stt(e, xcv(j), xin[:, j, k:k + nt], c["conv_w"][:, j, k:k + 1], xcv(j), ALU.mult, ALU.add)
        K.cp(dve, s["conv"].v(), xin[:, :, nt:nt + 3])
        yield
        for q in range(2):
            v = xc[q].v().rr("p (j t) -> p j t", j=8)
            nj = 8 if q == 0 else 4
            K.actf(v[:, 0:nj, 0:nt], v[:, 0:nj, 0:nt], AF.Silu)
        bcT = WP.H()
        bcv = bcT.v().rr("p (j t) -> p j t", j=8)
        K.cp(dve, bcv[:, 0:4, 0:nt], xc[1].v().rr("p (j t) -> p j t", j=8)[:, 0:4, 0:nt])
        yield
        xdt = WP.H()
        xD = WP.F()
        for half in range(2):
            ps = PS()
            for q in range(4):
                K.tr(ps[0:nt, q * 128:(q + 1) * 128], xcv(half * 4 + q), ident.v())
            pv = ps[0:nt, 0:512].rr("p (h d) -> p h d", h=8)
            hs = slice(half * 8, half * 8 + 8)
            K.tt(dve, xdt[0:nt, half * 512:(half + 1) * 512].rr("p (h d) -> p h d", h=8), pv,
                 dt[:, hs].unsq(2).bc([nt, 8, 64]), ALU.mult)
            K.tt(dve, xD[0:nt, half * 512:(half + 1) * 512].rr("p (h d) -> p h d", h=8), pv,
                 c["Dsk"][0:nt, hs].unsq(2).bc([nt, 8, 64]), ALU.mult)
            yield
        Btm = WP.H()
        ps = PS()
        for g in range(2):
            K.tr(ps[0:nt, g * 128:(g + 1) * 128], xcv(8 + g), ident.v())
        K.evac(Btm[0:nt, 0:256], ps[0:nt, 0:256])
        WP.release(xc[0])
        WP.release(xc[1])
        yield
        ps = PS()
        K.mm(ps[0:nt, 0:16], tri_i[0:nt, 0:nt], dtA)
        K.mm(ps[:, 16:32], onesf[0:nt, :], dtA)
        edec = st_edec[:, 0:16]
        K.actf(edec, ps[:, 16:32], AF.Exp)
        indec = WP.F()
        K.actf(indec[0:nt, 0:16], ps[0:nt, 0:16], AF.Exp)
        K.cp(dve, indec[0:nt, 32:48], ps[0:nt, 0:16])
        K.tt(dve, indec[0:nt, 16:32], ps[0:nt, 16:32], indec[0:nt, 32:48], ALU.subtract)
        K.actf(indec[0:nt, 16:32], indec[0:nt, 16:32], AF.Exp)
        xdte = WP.H()
        K.tt(pool, xdte[0:nt, :].rr("p (h d) -> p h d", h=16), xdt[0:nt, :].rr("p (h d) -> p h d", h=16),
             indec[0:nt, 16:32].unsq(2).bc([nt, 16, 64]), ALU.mult)
        yield
        cbm = indec[:, 512:1024]
        ps = PS()
        for g in range(2):
            K.mm(ps[0:nt, g * 128:g * 128 + nt], bcv[:, g, 0:nt], bcv[:, 2 + g, 0:nt])
        K.tt(dve, cbm[0:nt, 0:256].rr("p (g t) -> p g t", g=2)[:, :, 0:nt],
             ps[0:nt, 0:256].rr("p (g t) -> p g t", g=2)[:, :, 0:nt],
             tri_i[0:nt, 0:nt].unsq(1).bc([nt, 2, nt]), ALU.mult)
        yield
        rhsd = [None, None]
        Mh = [WP.H(), WP.H()]
        for g in range(2):
            rhsd[g] = WP.F()
            K.tt(dve if g == 0 else pool, rhsd[g][0:nt, :].rr("p (h t) -> p h t", h=8)[:, :, 0:nt],
                 tri_i[0:nt, 0:nt].unsq(1).bc([nt, 8, nt]),
                 dtA[:, g * 8:(g + 1) * 8].unsq(2).bc([nt, 8, nt]), ALU.mult)
            ps = PS()
            for q in range(2):
                K.mm(ps[0:nt, q * 512:(q + 1) * 512], tri_l[0:nt, 0:nt], rhsd[g][0:nt, q * 512:(q + 1) * 512])
            ex = rhsd[g]
            K.actf(ex[0:nt, :], ps[0:nt, :], AF.Exp)
            K.tt(dve, Mh[g][0:nt, :].rr("p (h t) -> p h t", h=8)[:, :, 0:nt],
                 ex[0:nt, :].rr("p (h t) -> p h t", h=8)[:, :, 0:nt],
                 cbm[0:nt, g * 128:g * 128 + nt].unsq(1).bc([nt, 8, nt]), ALU.mult)
            WP.release(rhsd[g])
            yield
        psy = PS()
        for g in range(2):
            K.mm(psy[0:nt, g * 512:(g + 1) * 512], bcv[:, 2 + g, 0:nt], s["ssd_b"][:, g * 512:(g + 1) * 512])
        y = WP.F()
        K.tt(dve, y[0:nt, :].rr("p (h d) -> p h d", h=16), psy[0:nt, :].rr("p (h d) -> p h d", h=16),
             indec[0:nt, 0:16].unsq(2).bc([nt, 16, 64]), ALU.mult)
        K.tt(pool, y[0:nt, :], y[0:nt, :], xD[0:nt, :], ALU.add)
        WP.release(xD)
        yield
        psd2 = PS()
        for h in range(16):
            g = h // 8
            K.mm(psd2[0:nt, h * 64:(h + 1) * 64], Mh[g][0:nt, (h % 8) * 128:(h % 8) * 128 + nt],
                 xdt[0:nt, h * 64:(h + 1) * 64])
        K.tt(dve, y[0:nt, :], y[0:nt, :], psd2[0:nt, :], ALU.add)
        WP.release(Mh[0])
        WP.release(Mh[1])
        yield
        pss = PS()
        for g in range(2):
            K.mm(pss[:, g * 512:(g + 1) * 512], Btm[0:nt, g * 128:(g + 1) * 128], xdte[0:nt, g * 512:(g + 1) * 512])
        K.tt(dve, s["ssd"].v().rr("p (h d) -> p h d", h=16), s["ssd"].v().rr("p (h d) -> p h d", h=16),
             edec.unsq(2).bc([128, 16, 64]), ALU.mult)
        K.tt(dve, s["ssd"].v(), s["ssd"].v(), pss.v(), ALU.add)
        K.cp(act, s["ssd_b"].v(), s["ssd"].v())
        yield
        K.tt(dve, y[0:nt, :], y[0:nt, :], sz[0:nt, :], ALU.mult)
        for g in range(2):
            rms_stats(y[0:nt, g * 512:(g + 1) * 512], nt, 512, 2 + g, 512, stat=st_g)
        K.tt(dve, y[0:nt, :].rr("p (g d) -> p g d", g=2), y[0:nt, :].rr("p (g d) -> p g d", g=2),
             st_g[0:nt, 2:4].unsq(2).bc([nt, 2, 512]), ALU.mult)
        oT = WP.H()
        oTv = oT.v().rr("p (j t) -> p j t", j=8)
        to_fm(y[0:nt, :], nt, oTv, c["ssd_norm"])
        yield
        gate_accum(l, nt, oTv, "w_proj_ssd", 0, True)
        WP.pop()

    def gla_branch(l, nt):
        c, s = LC[l], ST[l]
        WP.push()
        glrT = WP.H()

        def sink(j, ps, nch, b):
            K.evac(glrT[:, 0:nt], ps[:, 0:nt])
        yield from proj_fm_g("w_in", l, nt, uT, C_GLR, 128, sink)
        ps = PS()
        K.mm(ps[0:nt, 0:512], glrT[:, 0:nt], c["gk2"].v(), start=True, stop=False)
        K.mm(ps[0:nt, 0:512], ones1[0:1, 0:nt], c["gkb"][0].v(), start=False, stop=False)
        K.mm(ps[0:nt, 0:512], ones1[0:1, 0:nt], c["gkb"][1].v(), start=False, stop=True)
        gl = WP.F()
        K.actf(gl[0:nt, 0:512], ps[0:nt, 0:512], AF.Exp, scale=-1.0)
        K.actf(gl[0:nt, 0:512], gl[0:nt, 0:512], AF.Ln, bias=1.0, scale=1.0)
        yield
        ps = PS()
        K.mm(ps[0:nt, 0:512], tri_i[0:nt, 0:nt], gl[0:nt, 0:512])
        K.mm(ps[0:nt, 512:1024], onesf[0:nt, 0:nt], gl[0:nt, 0:512])
        E = WP.F()
        K.actf(E[0:nt, 0:512], ps[0:nt, 0:512], AF.Exp, scale=-1.0 / 16)
        K.actf(E[0:nt, 512:1024], ps[0:nt, 0:512], AF.Exp, scale=1.0 / 16)
        K.actf(gl[0:nt, 512:1024], ps[0:nt, 512:1024], AF.Exp, scale=-1.0 / 16)
        yield
        pst = PS()
        for h in range(4):
            K.mm(pst[:, h * 16:(h + 1) * 16], gl[0:nt, h * 128:(h + 1) * 128], onesf[0:nt, 0:16])
        cdec = st_cdec[:, 0:4]
        K.actf(cdec, pst[:, 0:64].rr("p (h x) -> p h x", x=16)[:, :, 0], AF.Exp, scale=-1.0 / 16)
        yield
        psq = PS()
        proj_tm("w_in", l, nt, uT, C_GQ, 1024, psq)
        qk = WP.F()
        K.stt(dve, qk[0:nt, 0:512], psq[0:nt, 0:512], float(128 ** -0.5), E[0:nt, 0:512], ALU.mult, ALU.mult)
        K.tt(dve, qk[0:nt, 512:1024], psq[0:nt, 512:1024], E[0:nt, 512:1024], ALU.mult)
        kend = WP.H()
        K.tt(dve, kend[0:nt, 0:512], qk[0:nt, 512:1024], gl[0:nt, 512:1024], ALU.mult)
        WP.release(gl)
        WP.release(E)
        yield
        qkT = WP.H()
        qkTv = qkT.v().rr("p (j t) -> p j t", j=8)
        to_fm(qk[0:nt, :], nt, qkTv)
        WP.release(qk)
        yield
        psv = PS()
        proj_tm("w_in", l, nt, uT, C_GV, 1024, psv)
        vb = WP.H()
        K.evac(vb[0:nt, :], psv[0:nt, :])
        yield
        psg = PS()
        proj_tm("w_in", l, nt, uT, C_GG, 1024, psg)
        gs = WP.F()
        K.actf(gs[0:nt, :], psg[0:nt, :], AF.Silu)
        yield "P"
        ps = PS()
        for h in range(4):
            K.mm(ps[0:nt, h * 128:h * 128 + nt], qkTv[:, 4 + h, 0:nt], qkTv[:, h, 0:nt])
        Am = WP.H()
        K.tt(dve, Am[0:nt, 0:512].rr("p (h t) -> p h t", h=4)[:, :, 0:nt],
             ps[0:nt, 0:512].rr("p (h t) -> p h t", h=4)[:, :, 0:nt],
             tri_i[0:nt, 0:nt].unsq(1).bc([nt, 4, nt]), ALU.mult)
        yield
        pso = PS()
        for h in range(4):
            for q in range(1):
                o = pso[0:nt, h * 256:(h + 1) * 256]
                K.mm(o, Am[0:nt, h * 128:h * 128 + nt], vb[0:nt, h * 256:(h + 1) * 256], start=True, stop=False)
                K.mm(o, qkTv[:, h, 0:nt], s["gla_b"][:, h * 256:(h + 1) * 256], start=False, stop=True)
        pss = PS()
        for h in range(4):
            K.mm(pss[:, h * 256:(h + 1) * 256], kend[0:nt, h * 128:(h + 1) * 128], vb[0:nt, h * 256:(h + 1) * 256])
        K.tt(dve, s["gla"].v().rr("p (h d) -> p h d", h=4), s["gla"].v().rr("p (h d) -> p h d", h=4),
             cdec.unsq(2).bc([128, 4, 256]), ALU.mult)
        K.tt(dve, s["gla"].v(), s["gla"].v(), pss.v(), ALU.add)
        K.cp(act, s["gla_b"].v(), s["gla"].v())
        o = WP.F()
        K.cp(act, o[0:nt, :], pso[0:nt, :])
        yield
        for h in range(4):
            rms_stats(o[0:nt, h * 256:(h + 1) * 256], nt, 256, 4 + h, 256, stat=st_g)
        K.tt(dve, o[0:nt, :].rr("p (h d) -> p h d", h=4), o[0:nt, :].rr("p (h d) -> p h d", h=4),
             st_g[0:nt, 4:8].unsq(2).bc([nt, 4, 256]), ALU.mult)
        K.tt(dve, o[0:nt, :], o[0:nt, :], gs[0:nt, :], ALU.mult)
        oT = WP.H()
        oTv = oT.v().rr("p (j t) -> p j t", j=8)
        to_fm(o[0:nt, :], nt, oTv, c["gla_norm8"])
        yield
        gate_accum(l, nt, oTv, "w_proj_gla", 1, False)
        WP.pop()

    def rwkv_branch(l, nt):
        c, s = LC[l], ST[l]
        WP.push()
        tw, prod = WP.H(), WP.H()
        ktT, btT = WP.H(), WP.H()
        Vtm = WP.H()
        gsr = WP.F()
        arM = [[None, None], [None, None]]
        ktmM, btmM = [None, None], [None, None]
        prv = prod.v().rr("p (j t) -> p j t", j=8)
        ktTv = ktT.v().rr("p (j t) -> p j t", j=8)
        btTv = btT.v().rr("p (j t) -> p j t", j=8)

        def arv(j, par):
            return arM[par][j // 4].v().rr("p (q x t) -> p q x t", q=4, x=2)[:, j % 4]
        WP.push()
        dd = [WP.F(), WP.F(), WP.F(), WP.F()]
        for q in range(4):
            nj = 8 if q < 3 else 1
            K.cp(dve, rfT8[:, 0:nj, 0], s["shift"][:, q * 8:q * 8 + nj])

            def sink(j, ps, nch, b):
                K.evac(rfT8[:, j:j + nch, 1:1 + nt], ps[:, 0:nch * 128].rr("p (q t) -> p q t", q=nch)[:, :, 0:nt])
            yield from proj_fm_g("w_in", l, nt, uT, C_RF + q * 1024, nj * 128, sink)
            K.cp(dve, s["shift"][:, q * 8:q * 8 + nj], rfT8[:, 0:nj, nt])
            dv = dd[q].v().rr("p (j t) -> p j t", j=8)[:, 0:nj, 0:nt]
            e = pool if q % 2 == 0 else dve
            K.tt(e, dv, rfT8[:, 0:nj, 0:nt], rfT8[:, 0:nj, 1:1 + nt], ALU.subtract)
            K.tt(e, dv, dv, c["rwkv_mu"][:, q * 8:q * 8 + nj].unsq(2).bc([128, nj, nt]), ALU.mult)
            K.tt(e, dv, dv, rfT8[:, 0:nj, 1:1 + nt], ALU.add)
            yield
        psg = PS()
        proj_tm("w_in", l, nt, uT, C_RG, 1024, psg)
        K.actf(gsr[0:nt, :], psg[0:nt, :], AF.Silu)
        yield "P"
        rT = dd[0].v().rr("p (j t) -> p j t", j=8)
        kT = dd[1].v().rr("p (j t) -> p j t", j=8)
        vT = dd[2].v().rr("p (j t) -> p j t", j=8)
        wa = dd[3].v().rr("p (j t) -> p j t", j=8)
        K.actf(tw[0:64, 0:nt], wa[0:64, 0, 0:nt], AF.Tanh)
        K.cp(dve, tw[64:128, 0:nt], wa[64:128, 0, 0:nt])
        yield
        for half in range(2):
            ps = PS()
            for q in range(4):
                K.tr(ps[0:nt, q * 128:(q + 1) * 128], vT[:, half * 4 + q, 0:nt], ident.v())
            K.evac(Vtm[0:nt, half * 512:(half + 1) * 512], ps[0:nt, 0:512])
            yield
        psw = PS()
        for q in range(2):
            o = psw[0:nt, q * 512:(q + 1) * 512]
            K.mm(o, tw[0:64, 0:nt], c["w2a2"][0:64, q * 512:(q + 1) * 512], start=True, stop=False)
            K.mm(o, ones1[0:1, 0:nt], c["w0"][0][0:1, q * 512:(q + 1) * 512], start=False, stop=False)
            K.mm(o, ones1[0:1, 0:nt], c["w0"][1][0:1, q * 512:(q + 1) * 512], start=False, stop=True)
        sgT = dd[3]
        K.actf(sgT[0:nt, :], psw[0:nt, :], AF.Sigmoid)
        yield
        E1, E2, E3 = WP.F(), WP.F(), WP.F()
        psc = PS()
        for half in range(2):
            ps = PS()
            for q in range(4):
                j = half * 4 + q
                K.mm(ps[:, q * 256:(q + 1) * 256], sgT[0:nt, j * 128:(j + 1) * 128], trr[0:nt, 0:256])
                K.mm(psc[:, j * 16:(j + 1) * 16], sgT[0:nt, j * 128:(j + 1) * 128], onesf[0:nt, 0:16])
            pv = ps.v().rr("p (q x t) -> p q x t", q=4, x=2)
            sl = slice(half * 512, (half + 1) * 512)
            K.actf(E1[:, sl].rr("p (q t) -> p q t", q=4)[:, :, 0:nt], pv[:, :, 0, 0:nt], AF.Exp, scale=-LAM)
            K.actf(E2[:, sl].rr("p (q t) -> p q t", q=4)[:, :, 0:nt], pv[:, :, 0, 0:nt], AF.Exp, scale=LAM)
            K.actf(E3[:, sl].rr("p (q t) -> p q t", q=4)[:, :, 0:nt], pv[:, :, 1, 0:nt], AF.Exp, scale=-LAM)
        EC = st_EC[:, 0:8]
        K.actf(EC, psc[:, 0:128].rr("p (j x) -> p j x", x=16)[:, :, 0], AF.Exp, scale=-LAM)
        yield
        E1v = E1.v().rr("p (j t) -> p j t", j=8)
        E2v = E2.v().rr("p (j t) -> p j t", j=8)
        E3v = E3.v().rr("p (j t) -> p j t", j=8)
        aT = WP.F()
        aTv = aT.v().rr("p (j t) -> p j t", j=8)
        for half in range(2):
            ps = PS()
            for q in range(4):
                j = half * 4 + q
                K.mm(ps[:, q * 128:q * 128 + nt], c["w2a2"][64:128, j * 128:(j + 1) * 128], tw[64:128, 0:nt])
            for q in range(4):
                j = half * 4 + q
                K.actf(aTv[:, j, 0:nt], ps[:, q * 128:q * 128 + nt], AF.Sigmoid, bias=c["rwkv_a0"][:, j:j + 1], scale=1.0)
            yield
        kk = WP.F()
        kkv = kk.v().rr("p (j t) -> p j t", j=8)
        K.tt(dve, kkv[:, :, 0:nt], kT[:, :, 0:nt], c["rwkv_k_k"].v().unsq(2).bc([128, 8, nt]), ALU.mult)
        sq = WP.H()
        sqv = sq.v().rr("p (j t) -> p j t", j=8)
        K.tt(pool, sqv[:, :, 0:nt], kkv[:, :, 0:nt], kkv[:, :, 0:nt], ALU.mult)
        ps = PS()
        for half in range(2):
            for q in range(4):
                j = half * 4 + q
                K.mm(ps[:, j * 128:j * 128 + nt], blkb.v(), sqv[:, j, 0:nt])
        nrm = WP.F()
        nrv = nrm.v().rr("p (j t) -> p j t", j=8)
        K.ts(dve, nrv[:, :, 0:nt], ps.v().rr("p (j t) -> p j t", j=8)[:, :, 0:nt], 1e-24, ALU.max)
        K.actf(nrv[:, :, 0:nt], nrv[:, :, 0:nt], AF.Ln)
        K.actf(nrv[:, :, 0:nt], nrv[:, :, 0:nt], AF.Exp, scale=-0.5)
        K.tt(dve, kkv[:, :, 0:nt], kkv[:, :, 0:nt], nrv[:, :, 0:nt], ALU.mult)
        yield
        k7v = nrv
        K.stt(dve, k7v[:, :, 0:nt], aTv[:, :, 0:nt], -1.0, c["rwkv_k_a"].v().unsq(2).bc([128, 8, nt]), ALU.add, ALU.mult)
        K.stt(dve, k7v[:, :, 0:nt], k7v[:, :, 0:nt], 1.0, kT[:, :, 0:nt], ALU.add, ALU.mult)
        tmpf = WP.F()
        tfv = tmpf.v().rr("p (j t) -> p j t", j=8)
        K.tt(pool, tfv[:, :, 0:nt], rT[:, :, 0:nt], k7v[:, :, 0:nt], ALU.mult)
        K.tt(pool, prv[:, :, 0:nt], tfv[:, :, 0:nt], c["rwkv_r_k"].v().unsq(2).bc([128, 8, nt]), ALU.mult)
        yield
        WP.release(tmpf)
        for par in range(2):
            for half in range(2):
                arM[par][half] = WP.H(scope=-2)
                K.memset(pool, arM[par][half][(1 - par) * 64:(2 - par) * 64, :], 0.0)
        for half in range(2):
            js = slice(half * 4, half * 4 + 4)
            for par in range(2):
                rows = slice(par * 64, par * 64 + 64)
                a4 = arM[par][half].v().rr("p (q x t) -> p q x t", q=4, x=2)
                K.stt(dve, a4[rows, :, 0, 0:nt], kkv[rows, js, 0:nt], -1.0, E3v[rows, js, 0:nt], ALU.mult, ALU.mult)
                K.tt(pool, a4[rows, :, 1, 0:nt], rT[rows, js, 0:nt], E1v[rows, js, 0:nt], ALU.mult)
        ktfv = E1v
        K.tt(dve, ktfv[:, :, 0:nt], k7v[:, :, 0:nt], E2v[:, :, 0:nt], ALU.mult)
        btfv = E3v
        K.tt(pool, btfv[:, :, 0:nt], kkv[:, :, 0:nt], aTv[:, :, 0:nt], ALU.mult)
        K.tt(dve, btfv[:, :, 0:nt], btfv[:, :, 0:nt], E2v[:, :, 0:nt], ALU.mult)
        K.cp(act, ktTv[:, :, 0:nt], ktfv[:, :, 0:nt])
        K.cp(act, btTv[:, :, 0:nt], btfv[:, :, 0:nt])
        yield
        WP.release(aT)
        WP.release(kk)
        WP.release(nrm)
        WP.release(sq)
        for par in range(2):
            ktmM[par] = WP.H(scope=-2)
            btmM[par] = WP.H(scope=-2)
        for (src, dstM) in ((ktfv, ktmM), (btfv, btmM)):
            for half in range(2):
                ps = PS()
                for q in range(4):
                    K.tr(ps[0:nt, q * 128:(q + 1) * 128], src[:, half * 4 + q, 0:nt], ident.v())
                pv = ps[0:nt, 0:512].rr("p (q h k) -> p q h k", q=4, h=2)
                for par in range(2):
                    dv = dstM[par][0:nt, half * 512:(half + 1) * 512].rr("p (q h k) -> p q h k", q=4, h=2)
                    K.memset(pool, dv[:, :, 1 - par, :], 0.0)
                    K.cp(act if par == 0 else dve, dv[:, :, par, :], pv[:, :, par, :])
                yield
        psb = PS()
        for j in range(8):
            K.mm(psb[0:nt, 2 * j:2 * j + 2], prv[:, j, 0:nt], indb.v())
        bonus = st_bonus[0:nt, 0:16]
        K.cp(dve, bonus, psb[0:nt, 0:16])
        yield
        WP.pop()
        WP.release(tw)
        WP.release(prod)
        WP.push()
        o7 = WP.F()
        nlev = 6 if nt > 64 else (5 if nt > 32 else 4)
        def group_gen(g, SCg, Pb, PTb, Wb):
            XU = Pb
            scg = [SCg[0].v().rr("p (h k t) -> p h k t", h=2, k=4), SCg[1].v().rr("p (h k t) -> p h k t", h=2, k=4)]

            def sc(hh, kind):
                return scg[hh // 2][0:nt, hh % 2, kind, 0:nt]

            def v4(b, par):
                return b[0:nt, par * 512:(par + 1) * 512].rr("p (h t) -> p h t", h=4)[:, :, 0:nt]
            psn = PS()
            ps = None
            for hh in range(4):
                h = g * 4 + hh
                j, hp = h // 2, (h % 2) * 64
                par = h % 2
                av = arv(j, par)
                if hh % 2 == 0:
                    ps = PS()
                base = (hh % 2) * 512
                if nt == 128:
                    K.mm(ps[0:nt, base:base + 256], btTv[:, j, 0:nt],
                         arM[par][j // 4][:, (j % 4) * 256:(j % 4 + 1) * 256])
                    K.mm(ps[0:nt, base + 256:base + 512], ktTv[:, j, 0:nt],
                         arM[par][j // 4][:, (j % 4) * 256:(j % 4 + 1) * 256])
                else:
                    for x in range(2):
                        K.mm(ps[0:nt, base + x * 128:base + x * 128 + nt], btTv[:, j, 0:nt], av[:, x, 0:nt])
                        K.mm(ps[0:nt, base + 256 + x * 128:base + 256 + x * 128 + nt], ktTv[:, j, 0:nt], av[:, x, 0:nt])
                K.mm(psn[0:nt, hh * 128:hh * 128 + nt], av[:, 0, 0:nt], btTv[:, j, 0:nt])
                if hh % 2 == 1:
                    for h2 in range(2):
                        K.tt(dve, scg[hh // 2][0:nt, h2, :, 0:nt],
                             ps[0:nt, h2 * 512:(h2 + 1) * 512].rr("p (k t) -> p k t", k=4)[:, :, 0:nt],
                             mk4[0:nt, :, 0:nt], ALU.mult)
            K.tt(dve, v4(PTb, 0), psn[0:nt, 0:512].rr("p (h t) -> p h t", h=4)[:, :, 0:nt],
                 tri_l[0:nt, 0:nt].unsq(1).bc([nt, 4, nt]), ALU.mult)
            for hh in range(4):
                K.cp(pool, v4(Pb, 0)[:, hh, :], sc(hh, 0))
                K.tt(pool, v4(Wb, 0)[:, hh, :], sc(hh, 0), identb[0:nt, 0:nt], ALU.add)
            yield
            cur = 0
            for lev in range(1, nlev + 1):
                nxt = 1 - cur
                last = (lev == nlev)
                psP = PS()
                for hh in range(4):
                    Pc, PTc = v4(Pb, cur)[:, hh, :], v4(PTb, cur)[:, hh, :]
                    if not last:
                        K.mm(psP[0:nt, hh * 128:hh * 128 + nt], PTc, Pc)
                    K.mm(psP[0:nt, 512 + hh * 128:512 + hh * 128 + nt], Pc, PTc)
                if not last:
                    K.evac(v4(Pb, nxt), psP[0:nt, 0:512].rr("p (h t) -> p h t", h=4)[:, :, 0:nt])
                K.evac(v4(PTb, nxt), psP[0:nt, 512:1024].rr("p (h t) -> p h t", h=4)[:, :, 0:nt])
                yield
                psW = PS()
                for hh in range(4):
                    Wc = v4(Wb, cur)[:, hh, :]
                    o = psW[0:nt, hh * 128:hh * 128 + nt]
                    K.mm(o, v4(PTb, nxt)[:, hh, :], Wc, start=True, stop=False)
                    K.mm(o, identb[0:nt, 0:nt], Wc, start=False, stop=True)
                K.evac(v4(Wb, nxt), psW[0:nt, 0:512].rr("p (h t) -> p h t", h=4)[:, :, 0:nt])
                cur = nxt
                yield
            Wf = v4(Wb, cur)
            psX = PS()
            for hh in range(4):
                h = g * 4 + hh
                j, par = h // 2, h % 2
                o = psX[0:nt, hh * 64:(hh + 1) * 64]
                K.mm(o, arv(j, par)[:, 0, 0:nt], s["rw_b"][:, j, :], start=True, stop=False)
                K.mm(o, sc(hh, 2), Vtm[0:nt, h * 64:(h + 1) * 64], start=False, stop=True)
            Xb = XU[0:nt, 0:256]
            K.evac(Xb, psX[0:nt, 0:256])
            yield
            psU = PS()
            for hh in range(4):
                K.mm(psU[0:nt, hh * 64:(hh + 1) * 64], Wf[:, hh, :], Xb[:, hh * 64:(hh + 1) * 64])
            Ub = XU[0:nt, 512:768]
            K.evac(Ub, psU[0:nt, 0:256])
            yield
            psO = PS()
            for hh in range(4):
                h = g * 4 + hh
                j, par = h // 2, h % 2
                o = psO[0:nt, hh * 64:(hh + 1) * 64]
                K.mm(o, arv(j, par)[:, 1, 0:nt], s["rw_b"][:, j, :], start=True, stop=False)
                K.mm(o, sc(hh, 1), Ub[:, hh * 64:(hh + 1) * 64], start=False, stop=False)
                K.mm(o, sc(hh, 3), Vtm[0:nt, h * 64:(h + 1) * 64], start=False, stop=True)
            for jj in range(2):
                j = 2 * g + jj
                o2 = psO[:, 512 + jj * 64:512 + (jj + 1) * 64]
                for par in range(2):
                    hh = 2 * jj + par
                    h = g * 4 + hh
                    K.mm(o2, btmM[par][0:nt, j * 128:(j + 1) * 128], Ub[:, hh * 64:(hh + 1) * 64],
                         start=(par == 0), stop=False)
                    K.mm(o2, ktmM[par][0:nt, j * 128:(j + 1) * 128], Vtm[0:nt, h * 64:(h + 1) * 64],
                         start=False, stop=(par == 1))
            K.cp(act, o7[0:nt, g * 256:(g + 1) * 256], psO[0:nt, 0:256])
            rwg = s["rw"][:, 2 * g:2 * g + 2, :]
            K.tt(dve, rwg, rwg, psO[:, 512:640].rr("p (j v) -> p j v", j=2), ALU.add)
            K.tt(dve, rwg, rwg, EC[:, 2 * g:2 * g + 2].unsq(2).bc([128, 2, 64]), ALU.mult)
            K.cp(act, s["rw_b"][:, 2 * g:2 * g + 2, :], rwg)
            yield

        WP.push()
        gens = []
        for g in range(4):
            bufs = ([WP.H(), WP.H()], WP.H(), WP.H(), WP.H())
            gens.append(group_gen(g, *bufs))
        live = list(gens)
        while live:
            for gn in list(live):
                try:
                    next(gn)
                except StopIteration:
                    live.remove(gn)
            yield
        WP.pop()
        o7h = o7[0:nt, :].rr("p (h d) -> p h d", h=16)
        mean = st_mv[0:nt, 0:16]
        K.red(dve, mean, o7h, ALU.add)
        K.ts(dve, mean, mean, 1.0 / 64, ALU.mult)
        K.tt(dve, o7h, o7h, mean.unsq(2).bc([nt, 16, 64]), ALU.subtract)
        sq2 = WP.F()
        K.tt(pool, sq2[0:nt, :], o7[0:nt, :], o7[0:nt, :], ALU.mult)
        var = st_mv[0:nt, 0:16]
        K.red(dve, var, sq2[0:nt, :].rr("p (h d) -> p h d", h=16), ALU.add)
        K.actf(var, var, AF.Ln, bias=LN_EPS, scale=1.0 / 64)
        K.actf(var, var, AF.Exp, scale=-0.5)
        K.tt(dve, o7h, o7h, var.unsq(2).bc([nt, 16, 64]), ALU.mult)
        yield
        lnw, lnb = WP.F(), WP.F()
        K.dma(sp, lnw[0:nt, :], View(None, W["rwkv_ln_w"].t[l].partition_broadcast(nt)))
        K.dma(sp, lnb[0:nt, :], View(None, W["rwkv_ln_b"].t[l].partition_broadcast(nt)))
        K.tt(dve, o7[0:nt, :], o7[0:nt, :], lnw[0:nt, :], ALU.mult)
        K.tt(dve, o7[0:nt, :], o7[0:nt, :], lnb[0:nt, :], ALU.add)
        bv = sq2
        K.tt(pool, bv[0:nt, :].rr("p (h d) -> p h d", h=16), Vtm[0:nt, :].rr("p (h d) -> p h d", h=16),
             bonus.unsq(2).bc([nt, 16, 64]), ALU.mult)
        K.tt(dve, o7[0:nt, :], o7[0:nt, :], bv[0:nt, :], ALU.add)
        yield
        K.tt(dve, o7[0:nt, :], o7[0:nt, :], gsr[0:nt, :], ALU.mult)
        oT = WP.H()
        oTv = oT.v().rr("p (j t) -> p j t", j=8)
        to_fm(o7[0:nt, :], nt, oTv)
        yield
        gate_accum(l, nt, oTv, "w_proj_rwkv", 2, False)
        WP.pop()
        WP.pop()

    def out_and_xattn(l, nt, h):
        s = ST[l]
        WP.push()
        mT = WP.H()
        mTv = mT.v().rr("p (j t) -> p j t", j=8)
        K.cp(act, mTv[:, :, 0:nt], mbuf.v().rr("p (j t) -> p j t", j=8)[:, :, 0:nt])
        ps = PS()
        proj_tm("w_out", l, nt, mTv, 0, 1024, ps)
        K.tt(dve, h[0:nt, :], h[0:nt, :], ps[0:nt, :], ALU.add)
        norm_to_uT(h, nt, "norm_xattn", l)
        qT = WP.H()
        qTv = qT.v().rr("p (j t) -> p j t", j=8)

        def sink(j, ps, nch, b):
            K.evac(qTv[:, j:j + nch, 0:nt], ps[:, 0:nch * 128].rr("p (q t) -> p q t", q=nch)[:, :, 0:nt])
        proj_fm("xa_wq", l, nt, uT, 0, 1024, sink)
        pss = PS()
        for hd in range(4):
            o = pss[0:nt, hd * 256:(hd + 1) * 256]
            for cc in range(2):
                K.mm(o, qTv[:, 2 * hd + cc, 0:nt], s["KT"][:, 2 * hd + cc, :], start=(cc == 0), stop=(cc == 1))
        mx = st_mx[0:nt, 0:4]
        K.red(dve, mx, pss[0:nt, :].rr("p (h m) -> p h m", h=4), ALU.max)
        K.ts(dve, mx, mx, -1.0 / 16, ALU.mult)
        e = WP.F()
        rs = st_rs[0:nt, 0:4]
        for hd in range(4):
            K.actf(e[0:nt, hd * 256:(hd + 1) * 256], pss[0:nt, hd * 256:(hd + 1) * 256], AF.Exp,
                   bias=mx[:, hd:hd + 1], scale=1.0 / 16, accum=rs[:, hd:hd + 1])
        K.recip(rs, rs)
        K.tt(dve, e[0:nt, :].rr("p (h m) -> p h m", h=4), e[0:nt, :].rr("p (h m) -> p h m", h=4),
             rs.unsq(2).bc([nt, 4, 256]), ALU.mult)
        pT = WP.H()
        pTv = pT.v().rr("p (j t) -> p j t", j=8)
        to_fm(e[0:nt, :], nt, pTv)
        oT = WP.H()
        oTv = oT.v().rr("p (j t) -> p j t", j=8)
        for half in range(2):
            ps = PS()
            for q in range(4):
                j = half * 4 + q
                hd = j // 2
                for mc in range(2):
                    K.mm(ps[:, q * 128:q * 128 + nt], s["Vm"][:, mc, j * 128:(j + 1) * 128], pTv[:, hd * 2 + mc, 0:nt],
                         start=(mc == 0), stop=(mc == 1))
            K.evac(oTv[:, half * 4:half * 4 + 4, 0:nt], ps[:, 0:512].rr("p (q t) -> p q t", q=4)[:, :, 0:nt])
        ps = PS()
        proj_tm("xa_wo", l, nt, oTv, 0, 1024, ps)
        K.tt(dve, h[0:nt, :], h[0:nt, :], ps[0:nt, :], ALU.add)
        WP.pop()

    def final_norm(h, nt, ydst):
        WP.push()
        rms_stats(h[0:nt, :], nt, 1024, 0, 1024)
        g = WP.F()
        K.dma(sp, g[0:nt, :], View(None, W["norm_final"].t.partition_broadcast(nt)))
        y = WP.F()
        K.stt(dve, y[0:nt, :], h[0:nt, :], stat[0:nt, 0:1], g[0:nt, :], ALU.mult, ALU.mult)
        K.dma(pool, ydst, y[0:nt, :], final=True)
        WP.pop()

    def mem_kv(l):
        s = ST[l]
        WP.push()
        mT = WP.H()
        mT2 = WP.H()
        mTs = [mT.v().rr("p (j t) -> p j t", j=8), mT2.v().rr("p (j t) -> p j t", j=8)]
        for mt in range(2):
            WP.push()
            x = WP.F()
            K.dma(sp, x.v(), mem_prompt[mt * 128:(mt + 1) * 128, :])
            rms_stats(x.v(), 128, 1024, 0, 1024)
            K.ts(dve, x.v(), x.v(), stat[:, 0:1], ALU.mult)
            to_fm(x.v(), 128, mTs[mt], LC[l]["norm_mem"])
            WP.pop()
        for mt in range(2):
            for (nm, dst) in (("xa_wk", mem_k_o), ("xa_wv", mem_v_o)):
                WP.push()
                ps = PS()
                proj_tm(nm, l, 128, mTs[mt], 0, 1024, ps)
                o = WP.F()
                K.cp(act, o.v(), ps.v())
                K.dma(pool, dst[l, mt * 128:(mt + 1) * 128, :], o.v(), final=True)
                if nm == "xa_wv":
                    K.cp(dve, s["Vm"][:, mt, :], o.v())
                else:
                    to_fm(o.v(), 128, s["KT"][:, :, mt * 128:(mt + 1) * 128])
                WP.pop()
        WP.pop()

    def load_cache_kv(l):
        s = ST[l]
        for mt in range(2):
            WP.push()
            x = WP.F()
            K.dma(sp, x.v(), c_mk[l, mt * 128:(mt + 1) * 128, :])
            to_fm(x.v(), 128, s["KT"][:, :, mt * 128:(mt + 1) * 128])
            x2 = WP.F()
            K.dma(sp, x2.v(), c_mv[l, mt * 128:(mt + 1) * 128, :])
            K.cp(dve, s["Vm"][:, mt, :], x2.v())
            WP.pop()

    def zero_states(l):
        s = ST[l]
        for k in ("ssd", "ssd_b", "gla", "gla_b", "rw", "rw_b", "conv", "shift"):
            K.memset(pool, s[k].v(), 0.0)

    def load_states(l):
        s = ST[l]
        WP.push()
        t = WP.F()
        tv = t.v().rr("p (j n) -> p j n", j=8)
        K.dma(sp, tv, st_ssd[l].rr("h p n -> (h p) n").rr("(j q) n -> q j n", q=128))
        for half in range(2):
            ps = PS()
            for q in range(4):
                K.tr(ps[:, q * 128:(q + 1) * 128], tv[:, half * 4 + q, :], ident.v())
            K.evac(s["ssd"][:, half * 512:(half + 1) * 512], ps[:, 0:512])
        K.cp(act, s["ssd_b"].v(), s["ssd"].v())
        K.dma(sp, s["gla"].v().rr("p (h v) -> p h v", h=4), st_gla[l].rr("h k v -> k h v"))
        K.cp(act, s["gla_b"].v(), s["gla"].v())
        t2 = WP.F()
        t2v = t2[0:64, :].rr("p (h k) -> p h k", h=16)
        K.dma(sp, t2v, st_rwkv[l].rr("h v k -> v h k"))
        ps = PS()
        for j in range(8):
            K.tr(ps[:, j * 64:(j + 1) * 64], t2[0:64, j * 128:(j + 1) * 128], ident[0:64, 0:64])
        K.evac(s["rw"].v(), ps[:, 0:512].rr("p (j v) -> p j v", j=8))
        K.cp(act, s["rw_b"].v(), s["rw"].v())
        for r in range(3):
            load_fm(s["conv"][:, :, r], st_conv[l, r], 12)
        load_fm(s["shift"].v(), st_shift[l], 25)
        WP.pop()

    def store_states(l, g):
        s = ST[l]
        WP.push()
        t = WP.F()
        tv = t.v().rr("p (j n) -> p j n", j=8)
        for half in range(2):
            ps = PS()
            for q in range(4):
                K.tr(ps[:, q * 128:(q + 1) * 128], s["ssd"][:, (half * 4 + q) * 128:(half * 4 + q + 1) * 128], ident.v())
            K.evac(tv[:, half * 4:half * 4 + 4, :], ps[:, 0:512].rr("p (q n) -> p q n", q=4))
        K.dma(pool, O[g + "_ssd"][l].rr("h p n -> (h p) n").rr("(j q) n -> q j n", q=128), tv, final=True)
        K.dma(pool, O[g + "_gla"][l].rr("h k v -> k h v"), s["gla"].v().rr("p (h v) -> p h v", h=4), final=True)
        t2 = WP.F()
        ps = PS()
        for j in range(8):
            K.tr(ps[0:64, j * 128:(j + 1) * 128], s["rw"][:, j, :], ident.v())
        K.evac(t2[0:64, :], ps[0:64, :])
        K.dma(pool, O[g + "_rwkv"][l].rr("h v k -> v h k"), t2[0:64, :].rr("p (h k) -> p h k", h=16), final=True)
        for r in range(3):
            store_fm(O[g + "_conv"][l, r], s["conv"][:, :, r], 12)
        store_fm(O[g + "_shift"][l], s["shift"].v(), 25)
        WP.pop()

    def run_branches(l, nt):
        gens = [ssd_branch(l, nt), gla_branch(l, nt), rwkv_branch(l, nt)]
        stacks = [[], [], []]
        base = WP.stack

        def step(i):
            WP.stack = stacks[i]
            try:
                return next(gens[i])
            except StopIteration:
                return "END"
            finally:
                WP.stack = base
        while step(0) not in ("P", "END"):
            pass
        for i in range(3):
            nxt = i + 1 if i + 1 < 3 else None
            cur_done, nxt_ready = False, nxt is None
            while not (cur_done and nxt_ready):
                if not cur_done and step(i) == "END":
                    cur_done = True
                if not nxt_ready and step(nxt) in ("P", "END"):
                    nxt_ready = True

    def run_tile(src, nt, ydst, hi):
        h = hbuf[hi]
        K.dma(sp, h[0:nt, :], src)
        for l in range(2):
            norm_to_uT(h, nt, "norm_mix", l)
            run_branches(l, nt)
            if stage >= 2.4:
                out_and_xattn(l, nt, h)
        final_norm(h, nt, ydst)

    for l in range(2):
        if stage >= 1:
            mem_kv(l)
            zero_states(l)
    for i in range(NT):
        if stage >= 2:
            run_tile(x_prompt[i * 128:(i + 1) * 128, :], 128, y_prompt[i * 128:(i + 1) * 128, :], i % 2)
    for l in range(2):
        if stage >= 2:
            store_states(l, "p")
    for l in range(2):
        if stage >= 4:
            load_cache_kv(l)
            load_states(l)
    if stage >= 5:
        run_tile(x_sample[0:32, :], 32, y_sample[0:32, :], NT % 2)
        for l in range(2):
            store_states(l, "s")
    for (sem, val) in K.final:
        sp.prog.append(("waitD", sem, val))

    K.finalize_ranks()
    with nc.Block() as block:
        @block.tensor
        def _(e):
            K.assemble(pe, e)

        @block.scalar
        def _(e):
            K.assemble(act, e)

        @block.vector
        def _(e):
            K.assemble(dve, e)

        @block.gpsimd
        def _(e):
            K.assemble(pool, e)

        @block.sync
        def _(e):
            K.assemble(sp, e)
    es.close()
    return nc


WNAMES = ["norm_mix", "w_in", "ssd_conv_w", "ssd_conv_b", "ssd_dt_bias", "ssd_A_log", "ssd_D", "ssd_norm",
          "w_proj_ssd", "gla_gk_w2", "gla_gk_b", "gla_norm", "w_proj_gla", "rwkv_mu", "rwkv_w0", "rwkv_w2",
          "rwkv_a0", "rwkv_a2", "rwkv_k_k", "rwkv_k_a", "rwkv_r_k", "rwkv_ln_w", "rwkv_ln_b", "w_proj_rwkv",
          "b_merge", "w_out", "norm_xattn", "xa_wq", "xa_wo", "norm_mem", "xa_wk", "xa_wv", "norm_final"]


def run(inputs, NT=32, ncores=8, trace=False, stage=9):
    f = lambda a: np.ascontiguousarray(np.asarray(a, dtype=np.float32))
    nc = build(NT, stage=stage)
    shared = {}
    for nm in WNAMES:
        a = f(inputs[nm])
        if nm == "rwkv_r_k":
            a = a.reshape(2, 1024)
        if nm == "b_merge":
            a = a.reshape(2, 3072)
        shared[nm] = a
    in_maps = []
    for c in range(ncores):
        m = dict(shared)
        m["x_prompt"] = f(inputs["x_prompt"][c, :NT * 128])
        m["x_sample"] = f(inputs["x_sample"][c])
        m["mem_prompt"] = f(inputs["mem_prompt"][c])
        m["state_ssd"] = f(inputs["state_ssd"][:, c])
        m["state_ssd_conv"] = f(inputs["state_ssd_conv"][:, c])
        m["state_gla"] = f(inputs["state_gla"][:, c])
        m["state_rwkv"] = f(inputs["state_rwkv"][:, c])
        m["state_rwkv_shift"] = f(inputs["state_rwkv_shift"][:, c]).reshape(2, 3200)
        m["cache_mem_k"] = f(inputs["cache_mem_k"][:, c]).reshape(2, 256, 1024)
        m["cache_mem_v"] = f(inputs["cache_mem_v"][:, c]).reshape(2, 256, 1024)
        in_maps.append(m)
    res = run_bass_kernel_spmd(nc, in_maps, core_ids=list(range(ncores)), trace=trace)
    R = res.results
    st = lambda k: np.stack([r[k] for r in R], axis=1)
    outs = (
        np.stack([r["y_prompt"] for r in R], 0),
        np.stack([r["y_sample"] for r in R], 0),
        st("p_ssd"), st("p_conv"), st("p_gla"), st("p_rwkv"), st("p_shift").reshape(2, ncores, 1, 3200),
        st("mem_k").reshape(2, ncores, 256, 4, 256), st("mem_v").reshape(2, ncores, 256, 4, 256),
        st("s_ssd"), st("s_conv"), st("s_gla"), st("s_rwkv"), st("s_shift").reshape(2, ncores, 1, 3200),
    )
    return tuple(np.ascontiguousarray(o.astype(np.float32)) for o in outs), res


def kernel(**inputs):
    outs, _ = run(inputs, NT=32, ncores=8)
    return outs
```

```python
import numpy as np
from contextlib import ExitStack
import concourse.bass as bass
import concourse.mybir as mybir
from concourse.bass_utils import run_bass_kernel_spmd

F32 = mybir.dt.float32
BF16 = mybir.dt.bfloat16
AF = mybir.ActivationFunctionType
ALU = mybir.AluOpType
AX = mybir.AxisListType

D = 1024
IN_COLS = 12960
EPS = 1e-5
LN_EPS = 64e-5
LAM = float(np.exp(-0.5))
C_Z, C_XBC, C_DT, C_GQ, C_GK, C_GV, C_GG, C_GLR, C_RF, C_RG, C_MG = (
    0, 1024, 2560, 2576, 3088, 3600, 4624, 5648, 5664, 8864, 9888)


class View:
    def __init__(self, buf, ap):
        self.buf = buf
        self.ap = ap

    def __getitem__(self, idx):
        return View(self.buf, self.ap[idx])

    def rr(self, pat, **kw):
        return View(self.buf, self.ap.rearrange(pat, **kw))

    def bc(self, shape):
        return View(self.buf, self.ap.to_broadcast(list(shape)))

    def unsq(self, ax):
        return View(self.buf, self.ap.unsqueeze(ax))


class Buf:
    def __init__(self, t, name):
        self.t = t
        self.name = name
        self.lw = None
        self.rd = {}
        self.box = None

    def __getitem__(self, idx):
        return View(self, self.t[idx])

    def v(self):
        return View(self, self.t[:])


def bufs_of(v):
    b = v.buf
    if b is None:
        return []
    return b if isinstance(b, (list, tuple)) else [b]


class PsBuf:
    def __init__(self, t, name):
        self.t = t
        self.banks = [Buf(None, name + "_b0"), Buf(None, name + "_b1")]
        for b in self.banks:
            b.is_psum = True

    def __getitem__(self, idx):
        cols = idx[1] if isinstance(idx, tuple) and len(idx) > 1 else slice(None)
        lo = cols.start or 0
        hi = 1024 if cols.stop is None else cols.stop
        bs = [self.banks[i] for i in range(2) if lo < (i + 1) * 512 and hi > i * 512]
        return View(bs, self.t[idx])

    def v(self):
        return View(list(self.banks), self.t[:])


class PsHalf:
    def __init__(self, ps, bank):
        self.ps = ps
        self.off = bank * 512

    def __getitem__(self, idx):
        rows, cols = idx
        lo = (cols.start or 0) + self.off
        hi = (512 if cols.stop is None else cols.stop) + self.off
        assert hi <= self.off + 512
        return self.ps[rows, lo:hi]

    def v(self):
        return self.ps[:, self.off:self.off + 512]


class Eng:
    def __init__(self, name, sem):
        self.name = name
        self.sem = sem
        self.n = 0
        self.waited = {}
        self.prog = []
        self.ops = []
        self.rank = []


class Kern:
    def __init__(self, nc, es):
        self.nc = nc
        self.es = es
        self.nsem = 0
        self.pe = Eng("pe", self.sem("pe"))
        self.act = Eng("act", self.sem("act"))
        self.dve = Eng("dve", self.sem("dve"))
        self.pool = Eng("pool", self.sem("pool"))
        self.sp = Eng("sp", self.sem("sp"))
        self.final = []
        self.flip = 0
        self.nbuf = 0

    def sem(self, name):
        self.nsem += 1
        return self.es.enter_context(self.nc.semaphore(name + str(self.nsem)))

    def sb(self, shape, dt, name=None):
        self.nbuf += 1
        name = (name or "b") + "_" + str(self.nbuf)
        return Buf(self.es.enter_context(self.nc.sbuf_tensor(name, list(shape), dt)), name)

    def psb(self, shape, name=None):
        self.nbuf += 1
        name = (name or "ps") + "_" + str(self.nbuf)
        return PsBuf(self.es.enter_context(self.nc.psum_tensor(name, list(shape), F32)), name)

    def dram(self, name, shape, dt, kind):
        return Buf(self.nc.dram_tensor(name, list(shape), dt, kind=kind).ap(), name)

    def _sync(self, E, reads, writes):
        deps = []
        for v in reads:
            for b in bufs_of(v):
                if b.lw is not None:
                    deps.append(b.lw)
                if getattr(b, "is_psum", False):
                    deps.extend(t for t in b.rd.values() if t[2] is not E)
        for v in writes:
            for b in bufs_of(v):
                if b.lw is not None:
                    deps.append(b.lw)
                deps.extend(b.rd.values())
        for (sem, val, eng) in deps:
            if eng is E and E.name == "pe":
                continue
            k = id(sem)
            if E.waited.get(k, 0) >= val:
                continue
            E.waited[k] = val
            if eng is None:
                E.prog.append(("waitD", sem, val))
            else:
                eng.ops[val - 1][1] = True
                E.prog.append(("waitE", eng, val))

    def emit(self, E, fn, reads, writes):
        reads = [r for r in reads if isinstance(r, View)]
        self._sync(E, reads, writes)
        E.ops.append([fn, False])
        n = len(E.ops)
        E.prog.append(("op", n - 1))
        tok = (E.sem, n, E)
        for v in reads:
            for b in bufs_of(v):
                b.rd[id(E.sem)] = tok
        for v in writes:
            for b in bufs_of(v):
                b.lw = tok
                b.rd = {}

    def assemble(self, E, e):
        for item in E.prog:
            kind = item[0]
            if kind == "op":
                fn, mark = E.ops[item[1]]
                ins = fn(e)
                if mark:
                    ins.then_inc(E.sem, 1)
            elif kind == "waitE":
                eng, val = item[1], item[2]
                e.wait_ge(eng.sem, eng.rank[val - 1])
            elif kind == "waitD":
                e.wait_ge(item[1], item[2])
            else:
                item[1](e)

    def finalize_ranks(self):
        for E in (self.pe, self.act, self.dve, self.pool, self.sp):
            r, c = [], 0
            for (fn, mark) in E.ops:
                if mark:
                    c += 1
                r.append(c)
            E.rank = r

    def dma(self, E, out, in_, final=False, noncontig=False, nosync=False):
        if not nosync:
            self._sync(E, [in_], [out])
        ob, ib = out.buf, in_.buf
        if final:
            if ib.box is None:
                ib.box = SemBox()
            bx = ib.box
            if bx.ssem is None:
                bx.ssem = self.sem("s")
            bx.scnt += 16
            sem, val = bx.ssem, bx.scnt
            ib.rd[id(sem)] = (sem, val, None)
            self.final = [f for f in self.final if f[0] is not sem] + [(sem, val)]
        else:
            if ob.box is None:
                ob.box = SemBox()
            bx = ob.box
            if bx.dsem is None:
                bx.dsem = self.sem("d")
            bx.dcnt += 16
            sem, val = bx.dsem, bx.dcnt
            ob.lw = (sem, val, None)
            ob.rd = {}
        nc = self.nc
        if noncontig:
            def f(e, o=out.ap, i=in_.ap, sem=sem):
                with nc.allow_non_contiguous_dma(reason="small strided state/const transfer"):
                    e.dma_start(out=o, in_=i).then_inc(sem, 16)
        else:
            def f(e, o=out.ap, i=in_.ap, sem=sem):
                e.dma_start(out=o, in_=i).then_inc(sem, 16)
        E.prog.append(("raw", f))

    def mm(self, out, lhsT, rhs, start=True, stop=True):
        self.emit(self.pe, lambda e: e.matmul(out.ap, lhsT.ap, rhs.ap, start=start, stop=stop),
                  [lhsT, rhs], [out])

    def tr(self, out, in_, ident):
        self.emit(self.pe, lambda e: e.transpose(out.ap, in_.ap, ident.ap), [in_, ident], [out])

    def actf(self, out, in_, func, bias=None, scale=None, accum=None):
        kw = {}
        if bias is not None:
            kw["bias"] = bias.ap if isinstance(bias, View) else bias
        if scale is not None:
            kw["scale"] = scale.ap if isinstance(scale, View) else scale
        if accum is not None:
            kw["accum_out"] = accum.ap
        w = [out] + ([accum] if accum is not None else [])
        self.emit(self.act, lambda e: e.activation(out=out.ap, in_=in_.ap, func=func, **kw),
                  [in_, bias, scale], w)

    def tt(self, E, out, a, b, op):
        self.emit(E, lambda e: e.tensor_tensor(out=out.ap, in0=a.ap, in1=b.ap, op=op), [a, b], [out])

    def ts(self, E, out, a, s1, op0, s2=None, op1=None):
        x1 = s1.ap if isinstance(s1, View) else s1
        x2 = s2.ap if isinstance(s2, View) else s2
        if op1 is None:
            self.emit(E, lambda e: e.tensor_scalar(out=out.ap, in0=a.ap, scalar1=x1, scalar2=None, op0=op0),
                      [a, s1], [out])
        else:
            self.emit(E, lambda e: e.tensor_scalar(out=out.ap, in0=a.ap, scalar1=x1, scalar2=x2, op0=op0, op1=op1),
                      [a, s1, s2], [out])

    def stt(self, E, out, a, s, b, op0, op1):
        x = s.ap if isinstance(s, View) else s
        self.emit(E, lambda e: e.scalar_tensor_tensor(out=out.ap, in0=a.ap, scalar=x, in1=b.ap, op0=op0, op1=op1),
                  [a, s, b], [out])

    def cp(self, E, out, a):
        if E is self.act:
            self.actf(out, a, AF.Copy)
        else:
            self.emit(E, lambda e: e.tensor_copy(out=out.ap, in_=a.ap), [a], [out])

    def evac(self, out, a):
        self.flip ^= 1
        self.cp(self.act if self.flip else self.dve, out, a)

    def red(self, E, out, a, op):
        self.emit(E, lambda e: e.tensor_reduce(out=out.ap, in_=a.ap, axis=AX.X, op=op), [a], [out])

    def recip(self, out, a):
        self.emit(self.dve, lambda e: e.reciprocal(out=out.ap, in_=a.ap), [a], [out])

    def memset(self, E, out, val):
        self.emit(E, lambda e: e.memset(out.ap, val), [], [out])

    def aselect(self, out, in_, pattern, op, fill, base, cm):
        self.emit(self.pool, lambda e: e.affine_select(out=out.ap, in_=in_.ap, pattern=pattern, compare_op=op,
                                                       fill=fill, base=base, channel_multiplier=cm), [in_], [out])


class SemBox:
    def __init__(self):
        self.dsem = None
        self.dcnt = 0
        self.ssem = None
        self.scnt = 0


class Slot:
    def __init__(self, K, i):
        self.t = K.es.enter_context(K.nc.sbuf_tensor("wslot%d" % i, [128, 1024], F32))
        self.hist = {}
        self.box = [SemBox(), SemBox()]
        self.used = [False, False]
        self.i = i


class Pool:
    def __init__(self, K, nslots):
        self.K = K
        self.slots = [Slot(K, i) for i in range(nslots)]
        self.stack = []
        self.n = 0

    def push(self):
        self.stack.append([])

    def pop(self):
        for (slot, halves, b) in self.stack.pop():
            self._free(slot, halves, b)

    def _free(self, slot, halves, b):
        for tok in ([b.lw] if b.lw is not None else []) + list(b.rd.values()):
            k = id(tok[0])
            if k not in slot.hist or slot.hist[k][1] < tok[1]:
                slot.hist[k] = tok
        for h in halves:
            slot.used[h] = False

    def release(self, b):
        for fr in reversed(self.stack):
            for i, (slot, halves, bb) in enumerate(fr):
                if bb is b:
                    self._free(slot, halves, bb)
                    del fr[i]
                    return
        raise AssertionError("release: buffer not found")

    def F(self):
        for slot in self.slots:
            if not slot.used[0] and not slot.used[1]:
                slot.used = [True, True]
                self.n += 1
                b = Buf(slot.t, "F%d_%d" % (slot.i, self.n))
                b.box = slot.box[0]
                b.rd = dict(slot.hist)
                self.stack[-1].append((slot, (0, 1), b))
                return b
        raise AssertionError("work pool exhausted (F)")

    def H(self, scope=-1):
        cand = [sl for sl in self.slots if sl.used[0] != sl.used[1]] + \
               [sl for sl in self.slots if not sl.used[0] and not sl.used[1]]
        assert cand, "work pool exhausted (H)"
        slot = cand[0]
        h = 0 if not slot.used[0] else 1
        slot.used[h] = True
        self.n += 1
        b = Buf(slot.t[:].bitcast(BF16)[:, h * 1024:(h + 1) * 1024], "H%d_%d_%d" % (slot.i, h, self.n))
        b.box = slot.box[h]
        b.rd = dict(slot.hist)
        self.stack[scope].append((slot, (h,), b))
        return b


def build(NT, NSLOTS=18, stage=9):
    nc = bass.Bass("TRN2", target_bir_lowering=False)
    es = ExitStack()
    K = Kern(nc, es)
    pe, act, dve, pool, sp = K.pe, K.act, K.dve, K.pool, K.sp
    T = NT * 128

    def din(name, shape):
        return K.dram(name, shape, F32, "ExternalInput")

    def dout(name, shape):
        b = K.dram(name, shape, F32, "ExternalOutput")
        b.is_out = True
        return b

    x_prompt = din("x_prompt", [T, D])
    x_sample = din("x_sample", [32, D])
    mem_prompt = din("mem_prompt", [256, D])
    st_ssd = din("state_ssd", [2, 16, 64, 128])
    st_conv = din("state_ssd_conv", [2, 3, 1536])
    st_gla = din("state_gla", [2, 4, 128, 256])
    st_rwkv = din("state_rwkv", [2, 16, 64, 64])
    st_shift = din("state_rwkv_shift", [2, 3200])
    c_mk = din("cache_mem_k", [2, 256, D])
    c_mv = din("cache_mem_v", [2, 256, D])
    W = {}
    for nm, shp in [("norm_mix", [2, D]), ("w_in", [2, D, IN_COLS]), ("ssd_conv_w", [2, 4, 1536]),
                    ("ssd_conv_b", [2, 1536]), ("ssd_dt_bias", [2, 16]), ("ssd_A_log", [2, 16]),
                    ("ssd_D", [2, 16]), ("ssd_norm", [2, D]), ("w_proj_ssd", [2, D, D]),
                    ("gla_gk_w2", [2, 16, 512]), ("gla_gk_b", [2, 512]), ("gla_norm", [2, 256]),
                    ("w_proj_gla", [2, D, D]), ("rwkv_mu", [2, 3200]), ("rwkv_w0", [2, D]),
                    ("rwkv_w2", [2, 64, D]), ("rwkv_a0", [2, D]), ("rwkv_a2", [2, 64, D]),
                    ("rwkv_k_k", [2, D]), ("rwkv_k_a", [2, D]), ("rwkv_r_k", [2, D]),
                    ("rwkv_ln_w", [2, D]), ("rwkv_ln_b", [2, D]), ("w_proj_rwkv", [2, D, D]),
                    ("b_merge", [2, 3 * D]), ("w_out", [2, D, D]), ("norm_xattn", [2, D]),
                    ("xa_wq", [2, D, D]), ("xa_wo", [2, D, D]), ("norm_mem", [2, D]),
                    ("xa_wk", [2, D, D]), ("xa_wv", [2, D, D]), ("norm_final", [D])]:
        W[nm] = din(nm, shp)

    y_prompt = dout("y_prompt", [T, D])
    y_sample = dout("y_sample", [32, D])
    O = {}
    for g in ("p", "s"):
        O[g + "_ssd"] = dout(g + "_ssd", [2, 16, 64, 128])
        O[g + "_conv"] = dout(g + "_conv", [2, 3, 1536])
        O[g + "_gla"] = dout(g + "_gla", [2, 4, 128, 256])
        O[g + "_rwkv"] = dout(g + "_rwkv", [2, 16, 64, 64])
        O[g + "_shift"] = dout(g + "_shift", [2, 3200])
    mem_k_o = dout("mem_k", [2, 256, D])
    mem_v_o = dout("mem_v", [2, 256, D])

    big = ["xa_wk", "xa_wv", "w_in", "w_proj_ssd", "w_proj_gla", "w_proj_rwkv", "w_out", "xa_wq", "xa_wo"]
    WB = {}

    def convert_weights():
        for l in range(2):
            for nm in big:
                cols = IN_COLS if nm == "w_in" else D
                WB[(nm, l)] = K.dram("%s_bf%d" % (nm, l), [D, cols], BF16, "Internal")
                for k in range(0, 8, 2):
                    K.dma(pool, WB[(nm, l)][k * 128:(k + 2) * 128, :], W[nm][l, k * 128:(k + 2) * 128, :], nosync=True)

    ident = K.sb([128, 128], F32, "ident")
    tri_i = K.sb([128, 128], F32, "tri_i")
    tri_s = K.sb([128, 128], F32, "tri_s")
    tri_l = K.sb([128, 128], F32, "tri_l")
    onesf = K.sb([128, 128], F32, "onesf")
    mk4 = K.sb([128, 4, 128], F32, "mk4")
    identb = K.sb([128, 128], BF16, "identb")
    blkb = K.sb([128, 128], BF16, "blkb")
    indb = K.sb([128, 2], BF16, "indb")
    ones1 = K.sb([1, 128], BF16, "ones1")
    K.memset(pool, ident.v(), 0.0)
    K.aselect(ident.v(), ident.v(), [[-1, 128]], ALU.not_equal, 1.0, 0, 1)
    K.memset(pool, onesf.v(), 1.0)
    K.aselect(tri_i.v(), onesf.v(), [[1, 128]], ALU.is_ge, 0.0, 0, -1)
    K.aselect(tri_s.v(), onesf.v(), [[1, 128]], ALU.is_gt, 0.0, 0, -1)
    K.aselect(tri_l.v(), onesf.v(), [[-1, 128]], ALU.is_gt, 0.0, 0, 1)
    for q in range(4):
        K.cp(pool, mk4[:, q, :], (tri_s if q % 2 == 0 else tri_i).v())
    K.cp(pool, identb.v(), ident.v())
    K.memset(pool, blkb.v(), 0.0)
    K.memset(pool, blkb[0:64, 0:64], 1.0)
    K.memset(pool, blkb[64:128, 64:128], 1.0)
    K.memset(pool, indb.v(), 0.0)
    K.memset(pool, indb[0:64, 0:1], 1.0)
    K.memset(pool, indb[64:128, 1:2], 1.0)
    K.memset(pool, ones1.v(), 1.0)
    trr = K.sb([128, 256], F32, "trr")
    K.cp(pool, trr[:, 0:128], tri_i.v())
    K.cp(pool, trr[:, 128:256], tri_s.v())
    junk = K.sb([128, 1024], BF16, "junk")
    convert_weights()

    WP = Pool(K, NSLOTS)
    WP.push()

    ST = []
    for l in range(2):
        s = dict(
            ssd=K.sb([128, 1024], F32, "Sssd"), ssd_b=K.sb([128, 1024], BF16, "Sssdb"),
            gla=K.sb([128, 1024], F32, "Sgla"), gla_b=K.sb([128, 1024], BF16, "Sglab"),
            rw=K.sb([128, 8, 64], F32, "Srw"), rw_b=K.sb([128, 8, 64], BF16, "Srwb"),
            conv=K.sb([128, 12, 3], F32, "convst"), shift=K.sb([128, 25], F32, "shiftst"),
            KT=K.sb([128, 8, 256], BF16, "KT"), Vm=K.sb([128, 2, 1024], BF16, "Vm"))
        ST.append(s)

    xin = K.sb([128, 12, 131], F32, "xin")
    hbuf = [K.sb([128, 1024], F32, "h") for _ in range(2)]
    mbuf = K.sb([128, 1024], F32, "m")
    uT = K.sb([128, 8, 128], BF16, "uT")
    stat = K.sb([128, 4], F32, "stat")
    st_g = K.sb([128, 8], F32, "st_g")
    st_cdec = K.sb([128, 4], F32, "st_cdec")
    st_EC = K.sb([128, 8], F32, "st_EC")
    st_bonus = K.sb([128, 16], F32, "st_bonus")
    st_mv = K.sb([128, 16], F32, "st_mv")
    st_mx = K.sb([128, 4], F32, "st_mx")
    st_rs = K.sb([128, 4], F32, "st_rs")
    st_dt = K.sb([128, 16], F32, "st_dt")
    st_dtA = K.sb([128, 16], F32, "st_dtA")
    st_edec = K.sb([128, 16], F32, "st_edec")
    rfT8 = K.sb([128, 8, 129], F32, "rfT8")
    NW = 4
    wring = [K.sb([128, 8, 512], BF16, "wblk") for _ in range(NW)]
    wri = [0]
    NPS = 4
    psring = [K.psb([128, 1024], "ps") for _ in range(NPS)]
    psi = [2 * NPS - 1]

    def PS(n=2):
        if n == 1:
            psi[0] = (psi[0] + 1) % (2 * NPS)
            return PsHalf(psring[psi[0] // 2], psi[0] % 2)
        psi[0] = (psi[0] + 2 - (psi[0] % 2)) % (2 * NPS)
        b = psring[psi[0] // 2]
        psi[0] = (psi[0] + 1) % (2 * NPS)
        return b

    def load_fm(dst, src1d, J):
        WP.push()
        t = WP.F()
        K.dma(sp, t[0:J, 0:128], src1d.rr("(j p) -> j p", p=128))
        ps = PS(1)
        K.tr(ps[:, 0:J], t[0:J, 0:128], ident[0:J, 0:J])
        K.cp(dve, dst, ps[:, 0:J])
        WP.pop()

    def store_fm(dst1d, src, J):
        WP.push()
        t = WP.F()
        ps = PS(1)
        K.tr(ps[0:J, 0:128], src, ident.v())
        K.cp(dve, t[0:J, 0:128], ps[0:J, 0:128])
        K.dma(pool, dst1d.rr("(j p) -> j p", p=128), t[0:J, 0:128], final=True)
        WP.pop()

    def loadw(nm, l, c0, ncols):
        wri[0] = (wri[0] + 1) % NW
        wb = wring[wri[0]]
        K.dma(sp, wb[:, :, 0:ncols], WB[(nm, l)].v().rr("(k p) c -> p k c", p=128)[:, :, c0:c0 + ncols])
        return wb

    LC = []
    for l in range(2):
        c = {}

        def fm(nm, j, key=None, src=None):
            b = K.sb([128, j], F32, nm)
            load_fm(b.v(), W[nm][l], j)
            c[key or nm] = b
        fm("norm_mix", 8)
        fm("norm_xattn", 8)
        fm("norm_mem", 8)
        fm("ssd_norm", 8)
        fm("gla_norm", 2)
        fm("ssd_conv_b", 12)
        fm("rwkv_mu", 25)
        fm("rwkv_a0", 8)
        fm("rwkv_k_k", 8)
        fm("rwkv_k_a", 8)
        fm("rwkv_r_k", 8)
        fm("b_merge", 24)
        g8 = K.sb([128, 8], F32, "gn8")
        for j in range(8):
            K.cp(pool, g8[:, j:j + 1], c["gla_norm"][:, (j % 2):(j % 2) + 1])
        c["gla_norm8"] = g8
        cw = K.sb([128, 12, 4], F32, "convw")
        for k in range(4):
            load_fm(cw[:, :, k], W["ssd_conv_w"][l, k], 12)
        c["conv_w"] = cw
        a_t = K.sb([128, 16], F32, "A")
        K.dma(sp, a_t.v(), View(None, W["ssd_A_log"].t[l].partition_broadcast(128)))
        K.actf(a_t.v(), a_t.v(), AF.Exp)
        K.ts(dve, a_t.v(), a_t.v(), -1.0, ALU.mult)
        c["A"] = a_t
        d_t = K.sb([128, 16], F32, "Dsk")
        K.dma(sp, d_t.v(), View(None, W["ssd_D"].t[l].partition_broadcast(128)))
        c["Dsk"] = d_t
        def brow(nm, n, key):
            tmp = WP.F()
            hi = K.sb([1, n], BF16, key + "hi")
            lo = K.sb([1, n], BF16, key + "lo")
            K.dma(sp, tmp[0:1, 0:n], W[nm][l:l + 1, :] if len(W[nm].t.shape) == 2 else W[nm][l:l + 1])
            K.cp(dve, hi.v(), tmp[0:1, 0:n])
            K.tt(dve, tmp[0:1, 0:n], tmp[0:1, 0:n], hi.v(), ALU.subtract)
            K.cp(dve, lo.v(), tmp[0:1, 0:n])
            c[key] = (hi, lo)
        WP.push()
        brow("ssd_dt_bias", 16, "dtb")
        brow("gla_gk_b", 512, "gkb")
        brow("rwkv_w0", 1024, "w0")
        t1 = WP.F()
        gk2 = K.sb([128, 512], BF16, "gk2")
        K.memset(pool, t1.v(), 0.0)
        K.dma(sp, t1[0:16, 0:512], W["gla_gk_w2"][l])
        K.cp(dve, gk2.v(), t1[:, 0:512])
        c["gk2"] = gk2
        t2 = WP.F()
        w2b = K.sb([128, 1024], BF16, "w2a2")
        K.dma(sp, t2[0:64, :], W["rwkv_w2"][l])
        K.dma(sp, t2[64:128, :], W["rwkv_a2"][l])
        K.cp(dve, w2b.v(), t2.v())
        c["w2a2"] = w2b
        WP.pop()
        LC.append(c)

    def blocks(c0, n):
        out = []
        while n > 0:
            b = min(512, n)
            out.append((c0, b))
            c0 += b
            n -= b
        return out

    def proj_tm(nm, l, nt, lhs, c0, n, ps, pcol0=0, bias=None):
        pc = pcol0
        for (cc, b) in blocks(c0, n):
            wb = loadw(nm, l, cc, b)
            segs = []
            s0 = 0
            while s0 < b:
                e0 = min(b, s0 + (512 - (pc + s0) % 512))
                segs.append((s0, e0))
                s0 = e0
            for (s0, e0) in segs:
                o = ps[0:nt, pc + s0:pc + e0]
                for k in range(8):
                    K.mm(o, lhs[:, k, 0:nt], wb[:, k, s0:e0], start=(k == 0), stop=(k == 7 and bias is None))
                if bias is not None:
                    hi, lo, b0 = bias
                    off = b0 + (cc - c0) + s0
                    K.mm(o, ones1[0:1, 0:nt], hi[0:1, off:off + (e0 - s0)], start=False, stop=False)
                    K.mm(o, ones1[0:1, 0:nt], lo[0:1, off:off + (e0 - s0)], start=False, stop=True)
            pc += b

    def proj_fm(nm, l, nt, rhs, c0, n, sink):
        j = 0
        for (cc, b) in blocks(c0, n):
            wb = loadw(nm, l, cc, b)
            ps = PS(1)
            nch = (b + 127) // 128
            for q in range(nch):
                w_ = min(128, b - q * 128)
                o = ps[0:w_, q * 128:q * 128 + nt]
                for k in range(8):
                    K.mm(o, wb[:, k, q * 128:q * 128 + w_], rhs[:, k, 0:nt], start=(k == 0), stop=(k == 7))
            sink(j, ps, nch, b)
            j += nch

    def proj_fm_g(nm, l, nt, rhs, c0, n, sink):
        j = 0
        for (cc, b) in blocks(c0, n):
            wb = loadw(nm, l, cc, b)
            ps = PS(1)
            nch = (b + 127) // 128
            for q in range(nch):
                w_ = min(128, b - q * 128)
                o = ps[0:w_, q * 128:q * 128 + nt]
                for k in range(8):
                    K.mm(o, wb[:, k, q * 128:q * 128 + w_], rhs[:, k, 0:nt], start=(k == 0), stop=(k == 7))
            sink(j, ps, nch, b)
            yield
            j += nch

    def rms_stats(src, nt, ncols, col, scale_n, stat=stat):
        K.actf(junk[0:nt, 0:ncols], src, AF.Square, scale=float(scale_n ** -0.5), accum=stat[0:nt, col:col + 1])
        K.actf(stat[0:nt, col:col + 1], stat[0:nt, col:col + 1], AF.Ln, bias=EPS, scale=1.0)
        K.actf(stat[0:nt, col:col + 1], stat[0:nt, col:col + 1], AF.Exp, scale=-0.5)

    def to_fm(src, nt, dst, gvec=None):
        for half in range(2):
            ps = PS(1)
            for q in range(4):
                k = half * 4 + q
                K.tr(ps[:, q * 128:q * 128 + nt], src[:, k * 128:(k + 1) * 128], ident[0:nt, 0:nt])
            pv = ps[:, 0:512].rr("p (q t) -> p q t", q=4)[:, :, 0:nt]
            if gvec is None:
                K.evac(dst[:, half * 4:half * 4 + 4, 0:nt], pv)
            else:
                K.tt(dve, dst[:, half * 4:half * 4 + 4, 0:nt], pv,
                     gvec[:, half * 4:half * 4 + 4].unsq(2).bc([128, 4, nt]), ALU.mult)

    def norm_to_uT(h, nt, gkey, l):
        WP.push()
        rms_stats(h[0:nt, :], nt, 1024, 0, 1024)
        hn = WP.F()
        K.ts(dve, hn[0:nt, :], h[0:nt, :], stat[0:nt, 0:1], ALU.mult)
        to_fm(hn[0:nt, :], nt, uT, LC[l][gkey])
        WP.pop()

    def gate_accum(l, nt, oTv, wname, bidx, first):
        for half in range(2):
            WP.push()
            psm = PS(1)
            psp = PS(1)
            wbm = loadw("w_in", l, C_MG + bidx * 1024 + half * 512, 512)
            for q in range(4):
                for k in range(8):
                    K.mm(psm[:, q * 128:q * 128 + nt], wbm[:, k, q * 128:(q + 1) * 128], uT[:, k, 0:nt],
                         start=(k == 0), stop=(k == 7))
            wbp = loadw(wname, l, half * 512, 512)
            for q in range(4):
                for k in range(8):
                    K.mm(psp[:, q * 128:q * 128 + nt], wbp[:, k, q * 128:(q + 1) * 128], oTv[:, k, 0:nt],
                         start=(k == 0), stop=(k == 7))
            sg = WP.F()
            for q in range(4):
                j = half * 4 + q
                K.actf(sg[:, q * 128:q * 128 + nt], psm[:, q * 128:q * 128 + nt], AF.Sigmoid,
                       bias=LC[l]["b_merge"][:, bidx * 8 + j:bidx * 8 + j + 1], scale=1.0)
            mv = mbuf.v().rr("p (j t) -> p j t", j=8)[:, half * 4:half * 4 + 4, 0:nt]
            sgv = sg[:, 0:512].rr("p (q t) -> p q t", q=4)[:, :, 0:nt]
            pv = psp[:, 0:512].rr("p (q t) -> p q t", q=4)[:, :, 0:nt]
            if first:
                K.tt(dve, mv, pv, sgv, ALU.mult)
            else:
                K.tt(dve, sgv, pv, sgv, ALU.mult)
                K.tt(pool, mv, mv, sgv, ALU.add)
            WP.pop()

    def ssd_branch(l, nt):
        c, s = LC[l], ST[l]
        WP.push()
        sz = WP.F()
        psz = PS()
        proj_tm("w_in", l, nt, uT, C_Z, 1024, psz)
        K.actf(sz[0:nt, :], psz[0:nt, :], AF.Silu)
        yield
        psd = PS(1)
        proj_tm("w_in", l, nt, uT, C_DT, 16, psd, 0, bias=(c["dtb"][0], c["dtb"][1], 0))
        dt = st_dt[0:nt, 0:16]
        K.actf(dt, psd[0:nt, 0:16], AF.Exp)
        K.actf(dt, dt, AF.Ln, bias=1.0, scale=1.0)
        dtA = st_dtA[0:nt, 0:16]
        K.tt(dve, dtA, dt, c["A"][0:nt, :], ALU.mult)
        yield
        K.cp(dve, xin[:, :, 0:3], s["conv"].v())

        def sink(j, ps, nch, b):
            K.evac(xin[:, j:j + nch, 3:3 + nt], ps[:, 0:nch * 128].rr("p (q t) -> p q t", q=nch)[:, :, 0:nt])
        yield from proj_fm_g("w_in", l, nt, uT, C_XBC, 1536, sink)
        yield "P"
        xc = [WP.F(), WP.F()]

        def xcv(j):
            return xc[j // 8][:, (j % 8) * 128:(j % 8) * 128 + nt]
        for j in range(12):
            e = dve
            K.ts(e, xcv(j), xin[:, j, 0:nt], c["conv_w"][:, j, 0:1], ALU.mult, c["ssd_conv_b"][:, j:j + 1], ALU.add)
            for k in range(1, 4):
                K.stt(e, xcv(j), xin[:, j, k:k + nt], c["conv_w"][:, j, k:k + 1], xcv(j), ALU.mult, ALU.add)
        K.cp(dve, s["conv"].v(), xin[:, :, nt:nt + 3])
        yield
        for q in range(2):
            v = xc[q].v().rr("p (j t) -> p j t", j=8)
            nj = 8 if q == 0 else 4
            K.actf(v[:, 0:nj, 0:nt], v[:, 0:nj, 0:nt], AF.Silu)
        bcT = WP.H()
        bcv = bcT.v().rr("p (j t) -> p j t", j=8)
        K.cp(dve, bcv[:, 0:4, 0:nt], xc[1].v().rr("p (j t) -> p j t", j=8)[:, 0:4, 0:nt])
        yield
        xdt = WP.H()
        xD = WP.F()
        for half in range(2):
            ps = PS(1)
            for q in range(4):
                K.tr(ps[0:nt, q * 128:(q + 1) * 128], xcv(half * 4 + q), ident.v())
            pv = ps[0:nt, 0:512].rr("p (h d) -> p h d", h=8)
            hs = slice(half * 8, half * 8 + 8)
            K.tt(dve, xdt[0:nt, half * 512:(half + 1) * 512].rr("p (h d) -> p h d", h=8), pv,
                 dt[:, hs].unsq(2).bc([nt, 8, 64]), ALU.mult)
            K.tt(dve, xD[0:nt, half * 512:(half + 1) * 512].rr("p (h d) -> p h d", h=8), pv,
                 c["Dsk"][0:nt, hs].unsq(2).bc([nt, 8, 64]), ALU.mult)
            yield
        Btm = WP.H()
        ps = PS(1)
        for g in range(2):
            K.tr(ps[0:nt, g * 128:(g + 1) * 128], xcv(8 + g), ident.v())
        K.evac(Btm[0:nt, 0:256], ps[0:nt, 0:256])
        WP.release(xc[0])
        WP.release(xc[1])
        yield
        ps = PS(1)
        K.mm(ps[0:nt, 0:16], tri_i[0:nt, 0:nt], dtA)
        K.mm(ps[:, 16:32], onesf[0:nt, :], dtA)
        edec = st_edec[:, 0:16]
        K.actf(edec, ps[:, 16:32], AF.Exp)
        indec = WP.F()
        K.actf(indec[0:nt, 0:16], ps[0:nt, 0:16], AF.Exp)
        K.cp(dve, indec[0:nt, 32:48], ps[0:nt, 0:16])
        K.tt(dve, indec[0:nt, 16:32], ps[0:nt, 16:32], indec[0:nt, 32:48], ALU.subtract)
        K.actf(indec[0:nt, 16:32], indec[0:nt, 16:32], AF.Exp)
        xdte = WP.H()
        K.tt(dve, xdte[0:nt, :].rr("p (h d) -> p h d", h=16), xdt[0:nt, :].rr("p (h d) -> p h d", h=16),
             indec[0:nt, 16:32].unsq(2).bc([nt, 16, 64]), ALU.mult)
        yield
        cbm = indec[:, 512:1024]
        ps = PS(1)
        for g in range(2):
            K.mm(ps[0:nt, g * 128:g * 128 + nt], bcv[:, g, 0:nt], bcv[:, 2 + g, 0:nt])
        K.tt(dve, cbm[0:nt, 0:256].rr("p (g t) -> p g t", g=2)[:, :, 0:nt],
             ps[0:nt, 0:256].rr("p (g t) -> p g t", g=2)[:, :, 0:nt],
             tri_i[0:nt, 0:nt].unsq(1).bc([nt, 2, nt]), ALU.mult)
        yield
        rhsd = [None, None]
        Mh = [WP.H(), WP.H()]
        for g in range(2):
            rhsd[g] = WP.F()
            K.tt(dve if g == 0 else pool, rhsd[g][0:nt, :].rr("p (h t) -> p h t", h=8)[:, :, 0:nt],
                 tri_i[0:nt, 0:nt].unsq(1).bc([nt, 8, nt]),
                 dtA[:, g * 8:(g + 1) * 8].unsq(2).bc([nt, 8, nt]), ALU.mult)
            ps = PS()
            for q in range(2):
                K.mm(ps[0:nt, q * 512:(q + 1) * 512], tri_l[0:nt, 0:nt], rhsd[g][0:nt, q * 512:(q + 1) * 512])
            ex = rhsd[g]
            K.actf(ex[0:nt, :], ps[0:nt, :], AF.Exp)
            K.tt(dve, Mh[g][0:nt, :].rr("p (h t) -> p h t", h=8)[:, :, 0:nt],
                 ex[0:nt, :].rr("p (h t) -> p h t", h=8)[:, :, 0:nt],
                 cbm[0:nt, g * 128:g * 128 + nt].unsq(1).bc([nt, 8, nt]), ALU.mult)
            WP.release(rhsd[g])
            yield
        psy = PS()
        for g in range(2):
            K.mm(psy[0:nt, g * 512:(g + 1) * 512], bcv[:, 2 + g, 0:nt], s["ssd_b"][:, g * 512:(g + 1) * 512])
        y = WP.F()
        K.tt(dve, y[0:nt, :].rr("p (h d) -> p h d", h=16), psy[0:nt, :].rr("p (h d) -> p h d", h=16),
             indec[0:nt, 0:16].unsq(2).bc([nt, 16, 64]), ALU.mult)
        K.tt(pool, y[0:nt, :], y[0:nt, :], xD[0:nt, :], ALU.add)
        WP.release(xD)
        yield
        psd2 = PS()
        for h in range(16):
            g = h // 8
            K.mm(psd2[0:nt, h * 64:(h + 1) * 64], Mh[g][0:nt, (h % 8) * 128:(h % 8) * 128 + nt],
                 xdt[0:nt, h * 64:(h + 1) * 64])
        K.tt(dve, y[0:nt, :], y[0:nt, :], psd2[0:nt, :], ALU.add)
        WP.release(Mh[0])
        WP.release(Mh[1])
        yield
        pss = PS()
        for g in range(2):
            K.mm(pss[:, g * 512:(g + 1) * 512], Btm[0:nt, g * 128:(g + 1) * 128], xdte[0:nt, g * 512:(g + 1) * 512])
        K.tt(dve, s["ssd"].v().rr("p (h d) -> p h d", h=16), s["ssd"].v().rr("p (h d) -> p h d", h=16),
             edec.unsq(2).bc([128, 16, 64]), ALU.mult)
        K.tt(dve, s["ssd"].v(), s["ssd"].v(), pss.v(), ALU.add)
        K.cp(act, s["ssd_b"].v(), s["ssd"].v())
        yield
        K.tt(dve, y[0:nt, :], y[0:nt, :], sz[0:nt, :], ALU.mult)
        for g in range(2):
            rms_stats(y[0:nt, g * 512:(g + 1) * 512], nt, 512, 2 + g, 512, stat=st_g)
        K.tt(dve, y[0:nt, :].rr("p (g d) -> p g d", g=2), y[0:nt, :].rr("p (g d) -> p g d", g=2),
             st_g[0:nt, 2:4].unsq(2).bc([nt, 2, 512]), ALU.mult)
        oT = WP.H()
        oTv = oT.v().rr("p (j t) -> p j t", j=8)
        to_fm(y[0:nt, :], nt, oTv, c["ssd_norm"])
        yield
        gate_accum(l, nt, oTv, "w_proj_ssd", 0, True)
        WP.pop()

    def gla_branch(l, nt):
        c, s = LC[l], ST[l]
        WP.push()
        glrT = WP.H()

        def sink(j, ps, nch, b):
            K.evac(glrT[:, 0:nt], ps[:, 0:nt])
        yield from proj_fm_g("w_in", l, nt, uT, C_GLR, 128, sink)
        ps = PS(1)
        K.mm(ps[0:nt, 0:512], glrT[:, 0:nt], c["gk2"].v(), start=True, stop=False)
        K.mm(ps[0:nt, 0:512], ones1[0:1, 0:nt], c["gkb"][0].v(), start=False, stop=False)
        K.mm(ps[0:nt, 0:512], ones1[0:1, 0:nt], c["gkb"][1].v(), start=False, stop=True)
        gl = WP.F()
        K.actf(gl[0:nt, 0:512], ps[0:nt, 0:512], AF.Exp, scale=-1.0)
        K.actf(gl[0:nt, 0:512], gl[0:nt, 0:512], AF.Ln, bias=1.0, scale=1.0)
        yield
        ps = PS()
        K.mm(ps[0:nt, 0:512], tri_i[0:nt, 0:nt], gl[0:nt, 0:512])
        K.mm(ps[0:nt, 512:1024], onesf[0:nt, 0:nt], gl[0:nt, 0:512])
        E = WP.F()
        K.actf(E[0:nt, 0:512], ps[0:nt, 0:512], AF.Exp, scale=-1.0 / 16)
        K.actf(E[0:nt, 512:1024], ps[0:nt, 0:512], AF.Exp, scale=1.0 / 16)
        K.actf(gl[0:nt, 512:1024], ps[0:nt, 512:1024], AF.Exp, scale=-1.0 / 16)
        yield
        pst = PS(1)
        for h in range(4):
            K.mm(pst[:, h * 16:(h + 1) * 16], gl[0:nt, h * 128:(h + 1) * 128], onesf[0:nt, 0:16])
        cdec = st_cdec[:, 0:4]
        K.actf(cdec, pst[:, 0:64].rr("p (h x) -> p h x", x=16)[:, :, 0], AF.Exp, scale=-1.0 / 16)
        yield
        psq = PS()
        proj_tm("w_in", l, nt, uT, C_GQ, 1024, psq)
        qk = WP.F()
        K.stt(dve, qk[0:nt, 0:512], psq[0:nt, 0:512], float(128 ** -0.5), E[0:nt, 0:512], ALU.mult, ALU.mult)
        K.tt(dve, qk[0:nt, 512:1024], psq[0:nt, 512:1024], E[0:nt, 512:1024], ALU.mult)
        kend = WP.H()
        K.tt(dve, kend[0:nt, 0:512], qk[0:nt, 512:1024], gl[0:nt, 512:1024], ALU.mult)
        WP.release(gl)
        WP.release(E)
        yield
        qkT = WP.H()
        qkTv = qkT.v().rr("p (j t) -> p j t", j=8)
        to_fm(qk[0:nt, :], nt, qkTv)
        WP.release(qk)
        yield
        psv = PS()
        proj_tm("w_in", l, nt, uT, C_GV, 1024, psv)
        vb = WP.H()
        K.evac(vb[0:nt, :], psv[0:nt, :])
        yield
        psg = PS()
        proj_tm("w_in", l, nt, uT, C_GG, 1024, psg)
        gs = WP.F()
        K.actf(gs[0:nt, :], psg[0:nt, :], AF.Silu)
        yield "P"
        ps = PS(1)
        for h in range(4):
            K.mm(ps[0:nt, h * 128:h * 128 + nt], qkTv[:, 4 + h, 0:nt], qkTv[:, h, 0:nt])
        Am = WP.H()
        K.tt(dve, Am[0:nt, 0:512].rr("p (h t) -> p h t", h=4)[:, :, 0:nt],
             ps[0:nt, 0:512].rr("p (h t) -> p h t", h=4)[:, :, 0:nt],
             tri_i[0:nt, 0:nt].unsq(1).bc([nt, 4, nt]), ALU.mult)
        yield
        pso = PS()
        for h in range(4):
            for q in range(1):
                o = pso[0:nt, h * 256:(h + 1) * 256]
                K.mm(o, Am[0:nt, h * 128:h * 128 + nt], vb[0:nt, h * 256:(h + 1) * 256], start=True, stop=False)
                K.mm(o, qkTv[:, h, 0:nt], s["gla_b"][:, h * 256:(h + 1) * 256], start=False, stop=True)
        pss = PS()
        for h in range(4):
            K.mm(pss[:, h * 256:(h + 1) * 256], kend[0:nt, h * 128:(h + 1) * 128], vb[0:nt, h * 256:(h + 1) * 256])
        K.tt(dve, s["gla"].v().rr("p (h d) -> p h d", h=4), s["gla"].v().rr("p (h d) -> p h d", h=4),
             cdec.unsq(2).bc([128, 4, 256]), ALU.mult)
        K.tt(dve, s["gla"].v(), s["gla"].v(), pss.v(), ALU.add)
        K.cp(act, s["gla_b"].v(), s["gla"].v())
        o = WP.F()
        K.cp(act, o[0:nt, :], pso[0:nt, :])
        yield
        for h in range(4):
            rms_stats(o[0:nt, h * 256:(h + 1) * 256], nt, 256, 4 + h, 256, stat=st_g)
        K.tt(dve, o[0:nt, :].rr("p (h d) -> p h d", h=4), o[0:nt, :].rr("p (h d) -> p h d", h=4),
             st_g[0:nt, 4:8].unsq(2).bc([nt, 4, 256]), ALU.mult)
        K.tt(dve, o[0:nt, :], o[0:nt, :], gs[0:nt, :], ALU.mult)
        oT = WP.H()
        oTv = oT.v().rr("p (j t) -> p j t", j=8)
        to_fm(o[0:nt, :], nt, oTv, c["gla_norm8"])
        yield
        gate_accum(l, nt, oTv, "w_proj_gla", 1, False)
        WP.pop()

    def rwkv_branch(l, nt):
        c, s = LC[l], ST[l]
        WP.push()
        tw, prod = WP.H(), WP.H()
        ktT, btT = WP.H(), WP.H()
        Vtm = WP.H()
        gsr = WP.F()
        arM = [[None, None], [None, None]]
        ktmM, btmM = [None, None], [None, None]
        prv = prod.v().rr("p (j t) -> p j t", j=8)
        ktTv = ktT.v().rr("p (j t) -> p j t", j=8)
        btTv = btT.v().rr("p (j t) -> p j t", j=8)

        def arv(j, par):
            return arM[par][j // 4].v().rr("p (q x t) -> p q x t", q=4, x=2)[:, j % 4]
        WP.push()
        dd = [WP.F(), WP.F(), WP.F(), WP.F()]
        for q in range(4):
            nj = 8 if q < 3 else 1
            K.cp(dve, rfT8[:, 0:nj, 0], s["shift"][:, q * 8:q * 8 + nj])

            def sink(j, ps, nch, b):
                K.evac(rfT8[:, j:j + nch, 1:1 + nt], ps[:, 0:nch * 128].rr("p (q t) -> p q t", q=nch)[:, :, 0:nt])
            yield from proj_fm_g("w_in", l, nt, uT, C_RF + q * 1024, nj * 128, sink)
            K.cp(dve, s["shift"][:, q * 8:q * 8 + nj], rfT8[:, 0:nj, nt])
            dv = dd[q].v().rr("p (j t) -> p j t", j=8)[:, 0:nj, 0:nt]
            e = pool if q % 2 == 0 else dve
            K.tt(e, dv, rfT8[:, 0:nj, 0:nt], rfT8[:, 0:nj, 1:1 + nt], ALU.subtract)
            K.tt(e, dv, dv, c["rwkv_mu"][:, q * 8:q * 8 + nj].unsq(2).bc([128, nj, nt]), ALU.mult)
            K.tt(e, dv, dv, rfT8[:, 0:nj, 1:1 + nt], ALU.add)
            yield
        psg = PS()
        proj_tm("w_in", l, nt, uT, C_RG, 1024, psg)
        K.actf(gsr[0:nt, :], psg[0:nt, :], AF.Silu)
        yield "P"
        rT = dd[0].v().rr("p (j t) -> p j t", j=8)
        kT = dd[1].v().rr("p (j t) -> p j t", j=8)
        vT = dd[2].v().rr("p (j t) -> p j t", j=8)
        wa = dd[3].v().rr("p (j t) -> p j t", j=8)
        K.actf(tw[0:64, 0:nt], wa[0:64, 0, 0:nt], AF.Tanh)
        K.cp(dve, tw[64:128, 0:nt], wa[64:128, 0, 0:nt])
        yield
        for half in range(2):
            ps = PS(1)
            for q in range(4):
                K.tr(ps[0:nt, q * 128:(q + 1) * 128], vT[:, half * 4 + q, 0:nt], ident.v())
            K.evac(Vtm[0:nt, half * 512:(half + 1) * 512], ps[0:nt, 0:512])
            yield
        psw = PS()
        for q in range(2):
            o = psw[0:nt, q * 512:(q + 1) * 512]
            K.mm(o, tw[0:64, 0:nt], c["w2a2"][0:64, q * 512:(q + 1) * 512], start=True, stop=False)
            K.mm(o, ones1[0:1, 0:nt], c["w0"][0][0:1, q * 512:(q + 1) * 512], start=False, stop=False)
            K.mm(o, ones1[0:1, 0:nt], c["w0"][1][0:1, q * 512:(q + 1) * 512], start=False, stop=True)
        sgT = dd[3]
        K.actf(sgT[0:nt, :], psw[0:nt, :], AF.Sigmoid)
        yield
        E1, E2, E3 = WP.F(), WP.F(), WP.F()
        psc = PS(1)
        for half in range(2):
            ps = PS()
            for q in range(4):
                j = half * 4 + q
                K.mm(ps[:, q * 256:(q + 1) * 256], sgT[0:nt, j * 128:(j + 1) * 128], trr[0:nt, 0:256])
                K.mm(psc[:, j * 16:(j + 1) * 16], sgT[0:nt, j * 128:(j + 1) * 128], onesf[0:nt, 0:16])
            pv = ps.v().rr("p (q x t) -> p q x t", q=4, x=2)
            sl = slice(half * 512, (half + 1) * 512)
            K.actf(E1[:, sl].rr("p (q t) -> p q t", q=4)[:, :, 0:nt], pv[:, :, 0, 0:nt], AF.Exp, scale=-LAM)
            K.actf(E2[:, sl].rr("p (q t) -> p q t", q=4)[:, :, 0:nt], pv[:, :, 0, 0:nt], AF.Exp, scale=LAM)
            K.actf(E3[:, sl].rr("p (q t) -> p q t", q=4)[:, :, 0:nt], pv[:, :, 1, 0:nt], AF.Exp, scale=-LAM)
        EC = st_EC[:, 0:8]
        K.actf(EC, psc[:, 0:128].rr("p (j x) -> p j x", x=16)[:, :, 0], AF.Exp, scale=-LAM)
        yield
        E1v = E1.v().rr("p (j t) -> p j t", j=8)
        E2v = E2.v().rr("p (j t) -> p j t", j=8)
        E3v = E3.v().rr("p (j t) -> p j t", j=8)
        aT = WP.F()
        aTv = aT.v().rr("p (j t) -> p j t", j=8)
        for half in range(2):
            ps = PS(1)
            for q in range(4):
                j = half * 4 + q
                K.mm(ps[:, q * 128:q * 128 + nt], c["w2a2"][64:128, j * 128:(j + 1) * 128], tw[64:128, 0:nt])
            for q in range(4):
                j = half * 4 + q
                K.actf(aTv[:, j, 0:nt], ps[:, q * 128:q * 128 + nt], AF.Sigmoid, bias=c["rwkv_a0"][:, j:j + 1], scale=1.0)
            yield
        kk = WP.F()
        kkv = kk.v().rr("p (j t) -> p j t", j=8)
        K.tt(dve, kkv[:, :, 0:nt], kT[:, :, 0:nt], c["rwkv_k_k"].v().unsq(2).bc([128, 8, nt]), ALU.mult)
        sq = WP.H()
        sqv = sq.v().rr("p (j t) -> p j t", j=8)
        K.tt(pool, sqv[:, :, 0:nt], kkv[:, :, 0:nt], kkv[:, :, 0:nt], ALU.mult)
        ps = PS()
        for half in range(2):
            for q in range(4):
                j = half * 4 + q
                K.mm(ps[:, j * 128:j * 128 + nt], blkb.v(), sqv[:, j, 0:nt])
        nrm = WP.F()
        nrv = nrm.v().rr("p (j t) -> p j t", j=8)
        K.ts(dve, nrv[:, :, 0:nt], ps.v().rr("p (j t) -> p j t", j=8)[:, :, 0:nt], 1e-24, ALU.max)
        K.actf(nrv[:, :, 0:nt], nrv[:, :, 0:nt], AF.Ln)
        K.actf(nrv[:, :, 0:nt], nrv[:, :, 0:nt], AF.Exp, scale=-0.5)
        K.tt(dve, kkv[:, :, 0:nt], kkv[:, :, 0:nt], nrv[:, :, 0:nt], ALU.mult)
        yield
        k7v = nrv
        K.stt(dve, k7v[:, :, 0:nt], aTv[:, :, 0:nt], -1.0, c["rwkv_k_a"].v().unsq(2).bc([128, 8, nt]), ALU.add, ALU.mult)
        K.stt(dve, k7v[:, :, 0:nt], k7v[:, :, 0:nt], 1.0, kT[:, :, 0:nt], ALU.add, ALU.mult)
        tmpf = WP.F()
        tfv = tmpf.v().rr("p (j t) -> p j t", j=8)
        K.tt(pool, tfv[:, :, 0:nt], rT[:, :, 0:nt], k7v[:, :, 0:nt], ALU.mult)
        K.tt(pool, prv[:, :, 0:nt], tfv[:, :, 0:nt], c["rwkv_r_k"].v().unsq(2).bc([128, 8, nt]), ALU.mult)
        yield
        WP.release(tmpf)
        for par in range(2):
            for half in range(2):
                arM[par][half] = WP.H(scope=-2)
                K.memset(pool, arM[par][half][(1 - par) * 64:(2 - par) * 64, :], 0.0)
        for half in range(2):
            js = slice(half * 4, half * 4 + 4)
            for par in range(2):
                rows = slice(par * 64, par * 64 + 64)
                a4 = arM[par][half].v().rr("p (q x t) -> p q x t", q=4, x=2)
                K.stt(dve, a4[rows, :, 0, 0:nt], kkv[rows, js, 0:nt], -1.0, E3v[rows, js, 0:nt], ALU.mult, ALU.mult)
                K.tt(dve, a4[rows, :, 1, 0:nt], rT[rows, js, 0:nt], E1v[rows, js, 0:nt], ALU.mult)
        ktfv = E1v
        K.tt(dve, ktfv[:, :, 0:nt], k7v[:, :, 0:nt], E2v[:, :, 0:nt], ALU.mult)
        btfv = E3v
        K.tt(dve, btfv[:, :, 0:nt], kkv[:, :, 0:nt], aTv[:, :, 0:nt], ALU.mult)
        K.tt(dve, btfv[:, :, 0:nt], btfv[:, :, 0:nt], E2v[:, :, 0:nt], ALU.mult)
        K.cp(act, ktTv[:, :, 0:nt], ktfv[:, :, 0:nt])
        K.cp(act, btTv[:, :, 0:nt], btfv[:, :, 0:nt])
        yield
        WP.release(aT)
        WP.release(kk)
        WP.release(nrm)
        WP.release(sq)
        for par in range(2):
            ktmM[par] = WP.H(scope=-2)
            btmM[par] = WP.H(scope=-2)
        for (src, dstM) in ((ktfv, ktmM), (btfv, btmM)):
            for half in range(2):
                ps = PS(1)
                for q in range(4):
                    K.tr(ps[0:nt, q * 128:(q + 1) * 128], src[:, half * 4 + q, 0:nt], ident.v())
                pv = ps[0:nt, 0:512].rr("p (q h k) -> p q h k", q=4, h=2)
                for par in range(2):
                    dv = dstM[par][0:nt, half * 512:(half + 1) * 512].rr("p (q h k) -> p q h k", q=4, h=2)
                    K.memset(pool, dv[:, :, 1 - par, :], 0.0)
                    K.cp(act if par == 0 else dve, dv[:, :, par, :], pv[:, :, par, :])
                yield
        psb = PS(1)
        for j in range(8):
            K.mm(psb[0:nt, 2 * j:2 * j + 2], prv[:, j, 0:nt], indb.v())
        bonus = st_bonus[0:nt, 0:16]
        K.cp(dve, bonus, psb[0:nt, 0:16])
        yield
        WP.pop()
        WP.release(tw)
        WP.release(prod)
        WP.push()
        o7 = WP.F()
        nlev = 6 if nt > 64 else (5 if nt > 32 else 4)
        def group_gen(g, SCg, Pb, PTb, Wb):
            XU = Pb
            scg = [SCg[0].v().rr("p (h k t) -> p h k t", h=2, k=4), SCg[1].v().rr("p (h k t) -> p h k t", h=2, k=4)]

            def sc(hh, kind):
                return scg[hh // 2][0:nt, hh % 2, kind, 0:nt]

            def v4(b, par):
                return b[0:nt, par * 512:(par + 1) * 512].rr("p (h t) -> p h t", h=4)[:, :, 0:nt]
            psn = PS(1)
            ps = None
            for hh in range(4):
                h = g * 4 + hh
                j, hp = h // 2, (h % 2) * 64
                par = h % 2
                av = arv(j, par)
                if hh % 2 == 0:
                    ps = PS()
                base = (hh % 2) * 512
                if nt == 128:
                    K.mm(ps[0:nt, base:base + 256], btTv[:, j, 0:nt],
                         arM[par][j // 4][:, (j % 4) * 256:(j % 4 + 1) * 256])
                    K.mm(ps[0:nt, base + 256:base + 512], ktTv[:, j, 0:nt],
                         arM[par][j // 4][:, (j % 4) * 256:(j % 4 + 1) * 256])
                else:
                    for x in range(2):
                        K.mm(ps[0:nt, base + x * 128:base + x * 128 + nt], btTv[:, j, 0:nt], av[:, x, 0:nt])
                        K.mm(ps[0:nt, base + 256 + x * 128:base + 256 + x * 128 + nt], ktTv[:, j, 0:nt], av[:, x, 0:nt])
                K.mm(psn[0:nt, hh * 128:hh * 128 + nt], av[:, 0, 0:nt], btTv[:, j, 0:nt])
                if hh % 2 == 1:
                    for h2 in range(2):
                        K.tt(dve, scg[hh // 2][0:nt, h2, :, 0:nt],
                             ps[0:nt, h2 * 512:(h2 + 1) * 512].rr("p (k t) -> p k t", k=4)[:, :, 0:nt],
                             mk4[0:nt, :, 0:nt], ALU.mult)
            K.tt(dve, v4(PTb, 0), psn[0:nt, 0:512].rr("p (h t) -> p h t", h=4)[:, :, 0:nt],
                 tri_l[0:nt, 0:nt].unsq(1).bc([nt, 4, nt]), ALU.mult)
            for hh in range(4):
                K.cp(pool, v4(Pb, 0)[:, hh, :], sc(hh, 0))
                K.tt(pool, v4(Wb, 0)[:, hh, :], sc(hh, 0), identb[0:nt, 0:nt], ALU.add)
            yield
            cur = 0
            for lev in range(1, nlev + 1):
                nxt = 1 - cur
                last = (lev == nlev)
                psP = PS()
                for hh in range(4):
                    Pc, PTc = v4(Pb, cur)[:, hh, :], v4(PTb, cur)[:, hh, :]
                    if not last:
                        K.mm(psP[0:nt, hh * 128:hh * 128 + nt], PTc, Pc)
                    K.mm(psP[0:nt, 512 + hh * 128:512 + hh * 128 + nt], Pc, PTc)
                if not last:
                    K.evac(v4(Pb, nxt), psP[0:nt, 0:512].rr("p (h t) -> p h t", h=4)[:, :, 0:nt])
                K.evac(v4(PTb, nxt), psP[0:nt, 512:1024].rr("p (h t) -> p h t", h=4)[:, :, 0:nt])
                yield
                psW = PS(1)
                for hh in range(4):
                    Wc = v4(Wb, cur)[:, hh, :]
                    o = psW[0:nt, hh * 128:hh * 128 + nt]
                    K.mm(o, v4(PTb, nxt)[:, hh, :], Wc, start=True, stop=False)
                    K.mm(o, identb[0:nt, 0:nt], Wc, start=False, stop=True)
                K.evac(v4(Wb, nxt), psW[0:nt, 0:512].rr("p (h t) -> p h t", h=4)[:, :, 0:nt])
                cur = nxt
                yield
            Wf = v4(Wb, cur)
            psX = PS(1)
            for hh in range(4):
                h = g * 4 + hh
                j, par = h // 2, h % 2
                o = psX[0:nt, hh * 64:(hh + 1) * 64]
                K.mm(o, arv(j, par)[:, 0, 0:nt], s["rw_b"][:, j, :], start=True, stop=False)
                K.mm(o, sc(hh, 2), Vtm[0:nt, h * 64:(h + 1) * 64], start=False, stop=True)
            Xb = XU[0:nt, 0:256]
            K.evac(Xb, psX[0:nt, 0:256])
            yield
            psU = PS(1)
            for hh in range(4):
                K.mm(psU[0:nt, hh * 64:(hh + 1) * 64], Wf[:, hh, :], Xb[:, hh * 64:(hh + 1) * 64])
            Ub = XU[0:nt, 512:768]
            K.evac(Ub, psU[0:nt, 0:256])
            yield
            psO = PS()
            for hh in range(4):
                h = g * 4 + hh
                j, par = h // 2, h % 2
                o = psO[0:nt, hh * 64:(hh + 1) * 64]
                K.mm(o, arv(j, par)[:, 1, 0:nt], s["rw_b"][:, j, :], start=True, stop=False)
                K.mm(o, sc(hh, 1), Ub[:, hh * 64:(hh + 1) * 64], start=False, stop=False)
                K.mm(o, sc(hh, 3), Vtm[0:nt, h * 64:(h + 1) * 64], start=False, stop=True)
            for jj in range(2):
                j = 2 * g + jj
                o2 = psO[:, 512 + jj * 64:512 + (jj + 1) * 64]
                for par in range(2):
                    hh = 2 * jj + par
                    h = g * 4 + hh
                    K.mm(o2, btmM[par][0:nt, j * 128:(j + 1) * 128], Ub[:, hh * 64:(hh + 1) * 64],
                         start=(par == 0), stop=False)
                    K.mm(o2, ktmM[par][0:nt, j * 128:(j + 1) * 128], Vtm[0:nt, h * 64:(h + 1) * 64],
                         start=False, stop=(par == 1))
            K.cp(act, o7[0:nt, g * 256:(g + 1) * 256], psO[0:nt, 0:256])
            rwg = s["rw"][:, 2 * g:2 * g + 2, :]
            K.tt(dve, rwg, rwg, psO[:, 512:640].rr("p (j v) -> p j v", j=2), ALU.add)
            K.tt(dve, rwg, rwg, EC[:, 2 * g:2 * g + 2].unsq(2).bc([128, 2, 64]), ALU.mult)
            K.cp(act, s["rw_b"][:, 2 * g:2 * g + 2, :], rwg)
            yield

        WP.push()
        gens = []
        for g in range(4):
            bufs = ([WP.H(), WP.H()], WP.H(), WP.H(), WP.H())
            gens.append(group_gen(g, *bufs))
        live = list(gens)
        while live:
            for gn in list(live):
                try:
                    next(gn)
                except StopIteration:
                    live.remove(gn)
            yield
        WP.pop()
        o7h = o7[0:nt, :].rr("p (h d) -> p h d", h=16)
        mean = st_mv[0:nt, 0:16]
        K.red(dve, mean, o7h, ALU.add)
        K.ts(dve, mean, mean, 1.0 / 64, ALU.mult)
        K.tt(dve, o7h, o7h, mean.unsq(2).bc([nt, 16, 64]), ALU.subtract)
        sq2 = WP.F()
        K.tt(pool, sq2[0:nt, :], o7[0:nt, :], o7[0:nt, :], ALU.mult)
        var = st_mv[0:nt, 0:16]
        K.red(dve, var, sq2[0:nt, :].rr("p (h d) -> p h d", h=16), ALU.add)
        K.actf(var, var, AF.Ln, bias=LN_EPS, scale=1.0 / 64)
        K.actf(var, var, AF.Exp, scale=-0.5)
        K.tt(dve, o7h, o7h, var.unsq(2).bc([nt, 16, 64]), ALU.mult)
        yield
        lnw, lnb = WP.F(), WP.F()
        K.dma(sp, lnw[0:nt, :], View(None, W["rwkv_ln_w"].t[l].partition_broadcast(nt)))
        K.dma(sp, lnb[0:nt, :], View(None, W["rwkv_ln_b"].t[l].partition_broadcast(nt)))
        K.tt(dve, o7[0:nt, :], o7[0:nt, :], lnw[0:nt, :], ALU.mult)
        K.tt(dve, o7[0:nt, :], o7[0:nt, :], lnb[0:nt, :], ALU.add)
        bv = sq2
        K.tt(dve, bv[0:nt, :].rr("p (h d) -> p h d", h=16), Vtm[0:nt, :].rr("p (h d) -> p h d", h=16),
             bonus.unsq(2).bc([nt, 16, 64]), ALU.mult)
        K.tt(dve, o7[0:nt, :], o7[0:nt, :], bv[0:nt, :], ALU.add)
        yield
        K.tt(dve, o7[0:nt, :], o7[0:nt, :], gsr[0:nt, :], ALU.mult)
        oT = WP.H()
        oTv = oT.v().rr("p (j t) -> p j t", j=8)
        to_fm(o7[0:nt, :], nt, oTv)
        yield
        gate_accum(l, nt, oTv, "w_proj_rwkv", 2, False)
        WP.pop()
        WP.pop()

    def out_and_xattn(l, nt, h):
        s = ST[l]
        WP.push()
        mT = WP.H()
        mTv = mT.v().rr("p (j t) -> p j t", j=8)
        K.cp(act, mTv[:, :, 0:nt], mbuf.v().rr("p (j t) -> p j t", j=8)[:, :, 0:nt])
        ps = PS()
        proj_tm("w_out", l, nt, mTv, 0, 1024, ps)
        K.tt(dve, h[0:nt, :], h[0:nt, :], ps[0:nt, :], ALU.add)
        norm_to_uT(h, nt, "norm_xattn", l)
        qT = WP.H()
        qTv = qT.v().rr("p (j t) -> p j t", j=8)

        def sink(j, ps, nch, b):
            K.evac(qTv[:, j:j + nch, 0:nt], ps[:, 0:nch * 128].rr("p (q t) -> p q t", q=nch)[:, :, 0:nt])
        proj_fm("xa_wq", l, nt, uT, 0, 1024, sink)
        pss = PS()
        for hd in range(4):
            o = pss[0:nt, hd * 256:(hd + 1) * 256]
            for cc in range(2):
                K.mm(o, qTv[:, 2 * hd + cc, 0:nt], s["KT"][:, 2 * hd + cc, :], start=(cc == 0), stop=(cc == 1))
        mx = st_mx[0:nt, 0:4]
        K.red(dve, mx, pss[0:nt, :].rr("p (h m) -> p h m", h=4), ALU.max)
        K.ts(dve, mx, mx, -1.0 / 16, ALU.mult)
        e = WP.F()
        rs = st_rs[0:nt, 0:4]
        for hd in range(4):
            K.actf(e[0:nt, hd * 256:(hd + 1) * 256], pss[0:nt, hd * 256:(hd + 1) * 256], AF.Exp,
                   bias=mx[:, hd:hd + 1], scale=1.0 / 16, accum=rs[:, hd:hd + 1])
        K.recip(rs, rs)
        K.tt(dve, e[0:nt, :].rr("p (h m) -> p h m", h=4), e[0:nt, :].rr("p (h m) -> p h m", h=4),
             rs.unsq(2).bc([nt, 4, 256]), ALU.mult)
        pT = WP.H()
        pTv = pT.v().rr("p (j t) -> p j t", j=8)
        to_fm(e[0:nt, :], nt, pTv)
        oT = WP.H()
        oTv = oT.v().rr("p (j t) -> p j t", j=8)
        for half in range(2):
            ps = PS(1)
            for q in range(4):
                j = half * 4 + q
                hd = j // 2
                for mc in range(2):
                    K.mm(ps[:, q * 128:q * 128 + nt], s["Vm"][:, mc, j * 128:(j + 1) * 128], pTv[:, hd * 2 + mc, 0:nt],
                         start=(mc == 0), stop=(mc == 1))
            K.evac(oTv[:, half * 4:half * 4 + 4, 0:nt], ps[:, 0:512].rr("p (q t) -> p q t", q=4)[:, :, 0:nt])
        ps = PS()
        proj_tm("xa_wo", l, nt, oTv, 0, 1024, ps)
        K.tt(dve, h[0:nt, :], h[0:nt, :], ps[0:nt, :], ALU.add)
        WP.pop()

    def final_norm(h, nt, ydst):
        WP.push()
        rms_stats(h[0:nt, :], nt, 1024, 0, 1024)
        g = WP.F()
        K.dma(sp, g[0:nt, :], View(None, W["norm_final"].t.partition_broadcast(nt)))
        y = WP.F()
        K.stt(dve, y[0:nt, :], h[0:nt, :], stat[0:nt, 0:1], g[0:nt, :], ALU.mult, ALU.mult)
        K.dma(pool, ydst, y[0:nt, :], final=True)
        WP.pop()

    def mem_kv(l):
        s = ST[l]
        WP.push()
        mT = WP.H()
        mT2 = WP.H()
        mTs = [mT.v().rr("p (j t) -> p j t", j=8), mT2.v().rr("p (j t) -> p j t", j=8)]
        for mt in range(2):
            WP.push()
            x = WP.F()
            K.dma(sp, x.v(), mem_prompt[mt * 128:(mt + 1) * 128, :])
            rms_stats(x.v(), 128, 1024, 0, 1024)
            K.ts(dve, x.v(), x.v(), stat[:, 0:1], ALU.mult)
            to_fm(x.v(), 128, mTs[mt], LC[l]["norm_mem"])
            WP.pop()
        for mt in range(2):
            for (nm, dst) in (("xa_wk", mem_k_o), ("xa_wv", mem_v_o)):
                WP.push()
                ps = PS()
                proj_tm(nm, l, 128, mTs[mt], 0, 1024, ps)
                o = WP.F()
                K.cp(act, o.v(), ps.v())
                K.dma(pool, dst[l, mt * 128:(mt + 1) * 128, :], o.v(), final=True)
                if nm == "xa_wv":
                    K.cp(dve, s["Vm"][:, mt, :], o.v())
                else:
                    to_fm(o.v(), 128, s["KT"][:, :, mt * 128:(mt + 1) * 128])
                WP.pop()
        WP.pop()

    def load_cache_kv(l):
        s = ST[l]
        for mt in range(2):
            WP.push()
            x = WP.F()
            K.dma(sp, x.v(), c_mk[l, mt * 128:(mt + 1) * 128, :])
            to_fm(x.v(), 128, s["KT"][:, :, mt * 128:(mt + 1) * 128])
            x2 = WP.F()
            K.dma(sp, x2.v(), c_mv[l, mt * 128:(mt + 1) * 128, :])
            K.cp(dve, s["Vm"][:, mt, :], x2.v())
            WP.pop()

    def zero_states(l):
        s = ST[l]
        for k in ("ssd", "ssd_b", "gla", "gla_b", "rw", "rw_b", "conv", "shift"):
            K.memset(pool, s[k].v(), 0.0)

    def load_states(l):
        s = ST[l]
        WP.push()
        t = WP.F()
        tv = t.v().rr("p (j n) -> p j n", j=8)
        K.dma(sp, tv, st_ssd[l].rr("h p n -> (h p) n").rr("(j q) n -> q j n", q=128))
        for half in range(2):
            ps = PS(1)
            for q in range(4):
                K.tr(ps[:, q * 128:(q + 1) * 128], tv[:, half * 4 + q, :], ident.v())
            K.evac(s["ssd"][:, half * 512:(half + 1) * 512], ps[:, 0:512])
        K.cp(act, s["ssd_b"].v(), s["ssd"].v())
        K.dma(sp, s["gla"].v().rr("p (h v) -> p h v", h=4), st_gla[l].rr("h k v -> k h v"))
        K.cp(act, s["gla_b"].v(), s["gla"].v())
        t2 = WP.F()
        t2v = t2[0:64, :].rr("p (h k) -> p h k", h=16)
        K.dma(sp, t2v, st_rwkv[l].rr("h v k -> v h k"))
        ps = PS(1)
        for j in range(8):
            K.tr(ps[:, j * 64:(j + 1) * 64], t2[0:64, j * 128:(j + 1) * 128], ident[0:64, 0:64])
        K.evac(s["rw"].v(), ps[:, 0:512].rr("p (j v) -> p j v", j=8))
        K.cp(act, s["rw_b"].v(), s["rw"].v())
        for r in range(3):
            load_fm(s["conv"][:, :, r], st_conv[l, r], 12)
        load_fm(s["shift"].v(), st_shift[l], 25)
        WP.pop()

    def store_states(l, g):
        s = ST[l]
        WP.push()
        t = WP.F()
        tv = t.v().rr("p (j n) -> p j n", j=8)
        for half in range(2):
            ps = PS(1)
            for q in range(4):
                K.tr(ps[:, q * 128:(q + 1) * 128], s["ssd"][:, (half * 4 + q) * 128:(half * 4 + q + 1) * 128], ident.v())
            K.evac(tv[:, half * 4:half * 4 + 4, :], ps[:, 0:512].rr("p (q n) -> p q n", q=4))
        K.dma(pool, O[g + "_ssd"][l].rr("h p n -> (h p) n").rr("(j q) n -> q j n", q=128), tv, final=True)
        K.dma(pool, O[g + "_gla"][l].rr("h k v -> k h v"), s["gla"].v().rr("p (h v) -> p h v", h=4), final=True)
        t2 = WP.F()
        ps = PS()
        for j in range(8):
            K.tr(ps[0:64, j * 128:(j + 1) * 128], s["rw"][:, j, :], ident.v())
        K.evac(t2[0:64, :], ps[0:64, :])
        K.dma(pool, O[g + "_rwkv"][l].rr("h v k -> v h k"), t2[0:64, :].rr("p (h k) -> p h k", h=16), final=True)
        for r in range(3):
            store_fm(O[g + "_conv"][l, r], s["conv"][:, :, r], 12)
        store_fm(O[g + "_shift"][l], s["shift"].v(), 25)
        WP.pop()

    def run_branches(l, nt):
        gens = [ssd_branch(l, nt), gla_branch(l, nt), rwkv_branch(l, nt)]
        stacks = [[], [], []]
        base = WP.stack

        def step(i):
            WP.stack = stacks[i]
            try:
                return next(gens[i])
            except StopIteration:
                return "END"
            finally:
                WP.stack = base
        while step(0) not in ("P", "END"):
            pass
        for i in range(3):
            nxt = i + 1 if i + 1 < 3 else None
            cur_done, nxt_ready = False, nxt is None
            while not (cur_done and nxt_ready):
                if not cur_done and step(i) == "END":
                    cur_done = True
                if not nxt_ready and step(nxt) in ("P", "END"):
                    nxt_ready = True

    def run_tile(src, nt, ydst, hi):
        h = hbuf[hi]
        K.dma(sp, h[0:nt, :], src)
        for l in range(2):
            norm_to_uT(h, nt, "norm_mix", l)
            run_branches(l, nt)
            if stage >= 2.4:
                out_and_xattn(l, nt, h)
        final_norm(h, nt, ydst)

    for l in range(2):
        if stage >= 1:
            mem_kv(l)
            zero_states(l)
    for i in range(NT):
        if stage >= 2:
            run_tile(x_prompt[i * 128:(i + 1) * 128, :], 128, y_prompt[i * 128:(i + 1) * 128, :], i % 2)
    for l in range(2):
        if stage >= 2:
            store_states(l, "p")
    for l in range(2):
        if stage >= 4:
            load_cache_kv(l)
            load_states(l)
    if stage >= 5:
        run_tile(x_sample[0:32, :], 32, y_sample[0:32, :], NT % 2)
        for l in range(2):
            store_states(l, "s")
    for (sem, val) in K.final:
        sp.prog.append(("waitD", sem, val))

    K.finalize_ranks()
    with nc.Block() as block:
        @block.tensor
        def _(e):
            K.assemble(pe, e)

        @block.scalar
        def _(e):
            K.assemble(act, e)

        @block.vector
        def _(e):
            K.assemble(dve, e)

        @block.gpsimd
        def _(e):
            K.assemble(pool, e)

        @block.sync
        def _(e):
            K.assemble(sp, e)
    es.close()
    return nc


WNAMES = ["norm_mix", "w_in", "ssd_conv_w", "ssd_conv_b", "ssd_dt_bias", "ssd_A_log", "ssd_D", "ssd_norm",
          "w_proj_ssd", "gla_gk_w2", "gla_gk_b", "gla_norm", "w_proj_gla", "rwkv_mu", "rwkv_w0", "rwkv_w2",
          "rwkv_a0", "rwkv_a2", "rwkv_k_k", "rwkv_k_a", "rwkv_r_k", "rwkv_ln_w", "rwkv_ln_b", "w_proj_rwkv",
          "b_merge", "w_out", "norm_xattn", "xa_wq", "xa_wo", "norm_mem", "xa_wk", "xa_wv", "norm_final"]


def run(inputs, NT=32, ncores=8, trace=False, stage=9):
    f = lambda a: np.ascontiguousarray(np.asarray(a, dtype=np.float32))
    nc = build(NT, stage=stage)
    shared = {}
    for nm in WNAMES:
        a = f(inputs[nm])
        if nm == "rwkv_r_k":
            a = a.reshape(2, 1024)
        if nm == "b_merge":
            a = a.reshape(2, 3072)
        shared[nm] = a
    in_maps = []
    for c in range(ncores):
        m = dict(shared)
        m["x_prompt"] = f(inputs["x_prompt"][c, :NT * 128])
        m["x_sample"] = f(inputs["x_sample"][c])
        m["mem_prompt"] = f(inputs["mem_prompt"][c])
        m["state_ssd"] = f(inputs["state_ssd"][:, c])
        m["state_ssd_conv"] = f(inputs["state_ssd_conv"][:, c])
        m["state_gla"] = f(inputs["state_gla"][:, c])
        m["state_rwkv"] = f(inputs["state_rwkv"][:, c])
        m["state_rwkv_shift"] = f(inputs["state_rwkv_shift"][:, c]).reshape(2, 3200)
        m["cache_mem_k"] = f(inputs["cache_mem_k"][:, c]).reshape(2, 256, 1024)
        m["cache_mem_v"] = f(inputs["cache_mem_v"][:, c]).reshape(2, 256, 1024)
        in_maps.append(m)
    res = run_bass_kernel_spmd(nc, in_maps, core_ids=list(range(ncores)), trace=trace)
    R = res.results
    st = lambda k: np.stack([r[k] for r in R], axis=1)
    outs = (
        np.stack([r["y_prompt"] for r in R], 0),
        np.stack([r["y_sample"] for r in R], 0),
        st("p_ssd"), st("p_conv"), st("p_gla"), st("p_rwkv"), st("p_shift").reshape(2, ncores, 1, 3200),
        st("mem_k").reshape(2, ncores, 256, 4, 256), st("mem_v").reshape(2, ncores, 256, 4, 256),
        st("s_ssd"), st("s_conv"), st("s_gla"), st("s_rwkv"), st("s_shift").reshape(2, ncores, 1, 3200),
    )
    return tuple(np.ascontiguousarray(o.astype(np.float32)) for o in outs), res


def kernel(**inputs):
    outs, _ = run(inputs, NT=32, ncores=8)
    return outs
```

```python
import numpy as np
from contextlib import ExitStack
import concourse.bass as bass
import concourse.mybir as mybir
from concourse.bass_utils import run_bass_kernel_spmd

F32 = mybir.dt.float32
BF16 = mybir.dt.bfloat16
AF = mybir.ActivationFunctionType
ALU = mybir.AluOpType
AX = mybir.AxisListType

D = 1024
IN_COLS = 12960
EPS = 1e-5
LN_EPS = 64e-5
LAM = float(np.exp(-0.5))
C_Z, C_XBC, C_DT, C_GQ, C_GK, C_GV, C_GG, C_GLR, C_RF, C_RG, C_MG = (
    0, 1024, 2560, 2576, 3088, 3600, 4624, 5648, 5664, 8864, 9888)


class View:
    def __init__(self, buf, ap):
        self.buf = buf
        self.ap = ap

    def __getitem__(self, idx):
        return View(self.buf, self.ap[idx])

    def rr(self, pat, **kw):
        return View(self.buf, self.ap.rearrange(pat, **kw))

    def bc(self, shape):
        return View(self.buf, self.ap.to_broadcast(list(shape)))

    def unsq(self, ax):
        return View(self.buf, self.ap.unsqueeze(ax))


class Buf:
    def __init__(self, t, name):
        self.t = t
        self.name = name
        self.lw = None
        self.rd = {}
        self.box = None

    def __getitem__(self, idx):
        return View(self, self.t[idx])

    def v(self):
        return View(self, self.t[:])


def bufs_of(v):
    b = v.buf
    if b is None:
        return []
    return b if isinstance(b, (list, tuple)) else [b]


class PsBuf:
    def __init__(self, t, name):
        self.t = t
        self.banks = [Buf(None, name + "_b0"), Buf(None, name + "_b1")]
        for b in self.banks:
            b.is_psum = True

    def __getitem__(self, idx):
        cols = idx[1] if isinstance(idx, tuple) and len(idx) > 1 else slice(None)
        lo = cols.start or 0
        hi = 1024 if cols.stop is None else cols.stop
        bs = [self.banks[i] for i in range(2) if lo < (i + 1) * 512 and hi > i * 512]
        return View(bs, self.t[idx])

    def v(self):
        return View(list(self.banks), self.t[:])


class PsHalf:
    def __init__(self, ps, bank):
        self.ps = ps
        self.off = bank * 512

    def __getitem__(self, idx):
        rows, cols = idx
        lo = (cols.start or 0) + self.off
        hi = (512 if cols.stop is None else cols.stop) + self.off
        assert hi <= self.off + 512
        return self.ps[rows, lo:hi]

    def v(self):
        return self.ps[:, self.off:self.off + 512]


class Eng:
    def __init__(self, name, sem):
        self.name = name
        self.sem = sem
        self.n = 0
        self.waited = {}
        self.prog = []
        self.ops = []
        self.rank = []


class Kern:
    def __init__(self, nc, es):
        self.nc = nc
        self.es = es
        self.nsem = 0
        self.pe = Eng("pe", self.sem("pe"))
        self.act = Eng("act", self.sem("act"))
        self.dve = Eng("dve", self.sem("dve"))
        self.pool = Eng("pool", self.sem("pool"))
        self.sp = Eng("sp", self.sem("sp"))
        self.final = []
        self.flip = 0
        self.nbuf = 0

    def sem(self, name):
        self.nsem += 1
        return self.es.enter_context(self.nc.semaphore(name + str(self.nsem)))

    def sb(self, shape, dt, name=None):
        self.nbuf += 1
        name = (name or "b") + "_" + str(self.nbuf)
        return Buf(self.es.enter_context(self.nc.sbuf_tensor(name, list(shape), dt)), name)

    def psb(self, shape, name=None):
        self.nbuf += 1
        name = (name or "ps") + "_" + str(self.nbuf)
        return PsBuf(self.es.enter_context(self.nc.psum_tensor(name, list(shape), F32)), name)

    def dram(self, name, shape, dt, kind):
        return Buf(self.nc.dram_tensor(name, list(shape), dt, kind=kind).ap(), name)

    def _sync(self, E, reads, writes):
        deps = []
        for v in reads:
            for b in bufs_of(v):
                if b.lw is not None:
                    deps.append(b.lw)
                if getattr(b, "is_psum", False):
                    deps.extend(t for t in b.rd.values() if t[2] is not E)
        for v in writes:
            for b in bufs_of(v):
                if b.lw is not None:
                    deps.append(b.lw)
                deps.extend(b.rd.values())
        for (sem, val, eng) in deps:
            if eng is E and E.name == "pe":
                continue
            k = id(sem)
            if E.waited.get(k, 0) >= val:
                continue
            E.waited[k] = val
            if eng is None:
                E.prog.append(("waitD", sem, val))
            else:
                eng.ops[val - 1][1] = True
                E.prog.append(("waitE", eng, val))

    def emit(self, E, fn, reads, writes):
        reads = [r for r in reads if isinstance(r, View)]
        self._sync(E, reads, writes)
        E.ops.append([fn, False])
        n = len(E.ops)
        E.prog.append(("op", n - 1))
        tok = (E.sem, n, E)
        for v in reads:
            for b in bufs_of(v):
                b.rd[id(E.sem)] = tok
        for v in writes:
            for b in bufs_of(v):
                b.lw = tok
                b.rd = {}

    def assemble(self, E, e):
        for item in E.prog:
            kind = item[0]
            if kind == "op":
                fn, mark = E.ops[item[1]]
                ins = fn(e)
                if mark:
                    ins.then_inc(E.sem, 1)
            elif kind == "waitE":
                eng, val = item[1], item[2]
                e.wait_ge(eng.sem, eng.rank[val - 1])
            elif kind == "waitD":
                e.wait_ge(item[1], item[2])
            else:
                item[1](e)

    def finalize_ranks(self):
        for E in (self.pe, self.act, self.dve, self.pool, self.sp):
            r, c = [], 0
            for (fn, mark) in E.ops:
                if mark:
                    c += 1
                r.append(c)
            E.rank = r

    def dma(self, E, out, in_, final=False, noncontig=False, nosync=False):
        if not nosync:
            self._sync(E, [in_], [out])
        ob, ib = out.buf, in_.buf
        if final:
            if ib.box is None:
                ib.box = SemBox()
            bx = ib.box
            if bx.ssem is None:
                bx.ssem = self.sem("s")
            bx.scnt += 16
            sem, val = bx.ssem, bx.scnt
            ib.rd[id(sem)] = (sem, val, None)
            self.final = [f for f in self.final if f[0] is not sem] + [(sem, val)]
        else:
            if ob.box is None:
                ob.box = SemBox()
            bx = ob.box
            if bx.dsem is None:
                bx.dsem = self.sem("d")
            bx.dcnt += 16
            sem, val = bx.dsem, bx.dcnt
            ob.lw = (sem, val, None)
            ob.rd = {}
        nc = self.nc
        if noncontig:
            def f(e, o=out.ap, i=in_.ap, sem=sem):
                with nc.allow_non_contiguous_dma(reason="small strided state/const transfer"):
                    e.dma_start(out=o, in_=i).then_inc(sem, 16)
        else:
            def f(e, o=out.ap, i=in_.ap, sem=sem):
                e.dma_start(out=o, in_=i).then_inc(sem, 16)
        E.prog.append(("raw", f))

    def mm(self, out, lhsT, rhs, start=True, stop=True):
        self.emit(self.pe, lambda e: e.matmul(out.ap, lhsT.ap, rhs.ap, start=start, stop=stop),
                  [lhsT, rhs], [out])

    def tr(self, out, in_, ident):
        self.emit(self.pe, lambda e: e.transpose(out.ap, in_.ap, ident.ap), [in_, ident], [out])

    def actf(self, out, in_, func, bias=None, scale=None, accum=None):
        kw = {}
        if bias is not None:
            kw["bias"] = bias.ap if isinstance(bias, View) else bias
        if scale is not None:
            kw["scale"] = scale.ap if isinstance(scale, View) else scale
        if accum is not None:
            kw["accum_out"] = accum.ap
        w = [out] + ([accum] if accum is not None else [])
        self.emit(self.act, lambda e: e.activation(out=out.ap, in_=in_.ap, func=func, **kw),
                  [in_, bias, scale], w)

    def tt(self, E, out, a, b, op):
        self.emit(E, lambda e: e.tensor_tensor(out=out.ap, in0=a.ap, in1=b.ap, op=op), [a, b], [out])

    def ts(self, E, out, a, s1, op0, s2=None, op1=None):
        x1 = s1.ap if isinstance(s1, View) else s1
        x2 = s2.ap if isinstance(s2, View) else s2
        if op1 is None:
            self.emit(E, lambda e: e.tensor_scalar(out=out.ap, in0=a.ap, scalar1=x1, scalar2=None, op0=op0),
                      [a, s1], [out])
        else:
            self.emit(E, lambda e: e.tensor_scalar(out=out.ap, in0=a.ap, scalar1=x1, scalar2=x2, op0=op0, op1=op1),
                      [a, s1, s2], [out])

    def stt(self, E, out, a, s, b, op0, op1):
        x = s.ap if isinstance(s, View) else s
        self.emit(E, lambda e: e.scalar_tensor_tensor(out=out.ap, in0=a.ap, scalar=x, in1=b.ap, op0=op0, op1=op1),
                  [a, s, b], [out])

    def cp(self, E, out, a):
        if E is self.act:
            self.actf(out, a, AF.Copy)
        else:
            self.emit(E, lambda e: e.tensor_copy(out=out.ap, in_=a.ap), [a], [out])

    def evac(self, out, a):
        self.flip ^= 1
        self.cp(self.act if self.flip else self.dve, out, a)

    def red(self, E, out, a, op):
        self.emit(E, lambda e: e.tensor_reduce(out=out.ap, in_=a.ap, axis=AX.X, op=op), [a], [out])

    def recip(self, out, a):
        self.emit(self.dve, lambda e: e.reciprocal(out=out.ap, in_=a.ap), [a], [out])

    def memset(self, E, out, val):
        self.emit(E, lambda e: e.memset(out.ap, val), [], [out])

    def aselect(self, out, in_, pattern, op, fill, base, cm):
        self.emit(self.pool, lambda e: e.affine_select(out=out.ap, in_=in_.ap, pattern=pattern, compare_op=op,
                                                       fill=fill, base=base, channel_multiplier=cm), [in_], [out])


class SemBox:
    def __init__(self):
        self.dsem = None
        self.dcnt = 0
        self.ssem = None
        self.scnt = 0


class Slot:
    def __init__(self, K, i):
        self.t = K.es.enter_context(K.nc.sbuf_tensor("wslot%d" % i, [128, 1024], F32))
        self.hist = {}
        self.box = [SemBox(), SemBox()]
        self.used = [False, False]
        self.i = i


class Pool:
    def __init__(self, K, nslots):
        self.K = K
        self.slots = [Slot(K, i) for i in range(nslots)]
        self.stack = []
        self.n = 0

    def push(self):
        self.stack.append([])

    def pop(self):
        for (slot, halves, b) in self.stack.pop():
            self._free(slot, halves, b)

    def _free(self, slot, halves, b):
        for tok in ([b.lw] if b.lw is not None else []) + list(b.rd.values()):
            k = id(tok[0])
            if k not in slot.hist or slot.hist[k][1] < tok[1]:
                slot.hist[k] = tok
        for h in halves:
            slot.used[h] = False

    def release(self, b):
        for fr in reversed(self.stack):
            for i, (slot, halves, bb) in enumerate(fr):
                if bb is b:
                    self._free(slot, halves, bb)
                    del fr[i]
                    return
        raise AssertionError("release: buffer not found")

    def F(self):
        for slot in self.slots:
            if not slot.used[0] and not slot.used[1]:
                slot.used = [True, True]
                self.n += 1
                b = Buf(slot.t, "F%d_%d" % (slot.i, self.n))
                b.box = slot.box[0]
                b.rd = dict(slot.hist)
                self.stack[-1].append((slot, (0, 1), b))
                return b
        raise AssertionError("work pool exhausted (F)")

    def H(self, scope=-1):
        cand = [sl for sl in self.slots if sl.used[0] != sl.used[1]] + \
               [sl for sl in self.slots if not sl.used[0] and not sl.used[1]]
        assert cand, "work pool exhausted (H)"
        slot = cand[0]
        h = 0 if not slot.used[0] else 1
        slot.used[h] = True
        self.n += 1
        b = Buf(slot.t[:].bitcast(BF16)[:, h * 1024:(h + 1) * 1024], "H%d_%d_%d" % (slot.i, h, self.n))
        b.box = slot.box[h]
        b.rd = dict(slot.hist)
        self.stack[scope].append((slot, (h,), b))
        return b


def build(NT, NSLOTS=18, stage=9):
    nc = bass.Bass("TRN2", target_bir_lowering=False)
    es = ExitStack()
    K = Kern(nc, es)
    pe, act, dve, pool, sp = K.pe, K.act, K.dve, K.pool, K.sp
    T = NT * 128

    def din(name, shape):
        return K.dram(name, shape, F32, "ExternalInput")

    def dout(name, shape):
        b = K.dram(name, shape, F32, "ExternalOutput")
        b.is_out = True
        return b

    x_prompt = din("x_prompt", [T, D])
    x_sample = din("x_sample", [32, D])
    mem_prompt = din("mem_prompt", [256, D])
    st_ssd = din("state_ssd", [2, 16, 64, 128])
    st_conv = din("state_ssd_conv", [2, 3, 1536])
    st_gla = din("state_gla", [2, 4, 128, 256])
    st_rwkv = din("state_rwkv", [2, 16, 64, 64])
    st_shift = din("state_rwkv_shift", [2, 3200])
    c_mk = din("cache_mem_k", [2, 256, D])
    c_mv = din("cache_mem_v", [2, 256, D])
    W = {}
    for nm, shp in [("norm_mix", [2, D]), ("w_in", [2, D, IN_COLS]), ("ssd_conv_w", [2, 4, 1536]),
                    ("ssd_conv_b", [2, 1536]), ("ssd_dt_bias", [2, 16]), ("ssd_A_log", [2, 16]),
                    ("ssd_D", [2, 16]), ("ssd_norm", [2, D]), ("w_proj_ssd", [2, D, D]),
                    ("gla_gk_w2", [2, 16, 512]), ("gla_gk_b", [2, 512]), ("gla_norm", [2, 256]),
                    ("w_proj_gla", [2, D, D]), ("rwkv_mu", [2, 3200]), ("rwkv_w0", [2, D]),
                    ("rwkv_w2", [2, 64, D]), ("rwkv_a0", [2, D]), ("rwkv_a2", [2, 64, D]),
                    ("rwkv_k_k", [2, D]), ("rwkv_k_a", [2, D]), ("rwkv_r_k", [2, D]),
                    ("rwkv_ln_w", [2, D]), ("rwkv_ln_b", [2, D]), ("w_proj_rwkv", [2, D, D]),
                    ("b_merge", [2, 3 * D]), ("w_out", [2, D, D]), ("norm_xattn", [2, D]),
                    ("xa_wq", [2, D, D]), ("xa_wo", [2, D, D]), ("norm_mem", [2, D]),
                    ("xa_wk", [2, D, D]), ("xa_wv", [2, D, D]), ("norm_final", [D])]:
        W[nm] = din(nm, shp)

    y_prompt = dout("y_prompt", [T, D])
    y_sample = dout("y_sample", [32, D])
    O = {}
    for g in ("p", "s"):
        O[g + "_ssd"] = dout(g + "_ssd", [2, 16, 64, 128])
        O[g + "_conv"] = dout(g + "_conv", [2, 3, 1536])
        O[g + "_gla"] = dout(g + "_gla", [2, 4, 128, 256])
        O[g + "_rwkv"] = dout(g + "_rwkv", [2, 16, 64, 64])
        O[g + "_shift"] = dout(g + "_shift", [2, 3200])
    mem_k_o = dout("mem_k", [2, 256, D])
    mem_v_o = dout("mem_v", [2, 256, D])

    big = ["xa_wk", "xa_wv", "w_in", "w_proj_ssd", "w_proj_gla", "w_proj_rwkv", "w_out", "xa_wq", "xa_wo"]
    WB = {}

    def convert_weights():
        for l in range(2):
            for nm in big:
                cols = IN_COLS if nm == "w_in" else D
                WB[(nm, l)] = K.dram("%s_bf%d" % (nm, l), [D, cols], BF16, "Internal")
                for k in range(0, 8, 2):
                    K.dma(pool, WB[(nm, l)][k * 128:(k + 2) * 128, :], W[nm][l, k * 128:(k + 2) * 128, :], nosync=True)

    ident = K.sb([128, 128], F32, "ident")
    tri_i = K.sb([128, 128], F32, "tri_i")
    tri_s = K.sb([128, 128], F32, "tri_s")
    tri_l = K.sb([128, 128], F32, "tri_l")
    onesf = K.sb([128, 128], F32, "onesf")
    mk4 = K.sb([128, 4, 128], F32, "mk4")
    identb = K.sb([128, 128], BF16, "identb")
    blkb = K.sb([128, 128], BF16, "blkb")
    indb = K.sb([128, 2], BF16, "indb")
    ones1 = K.sb([1, 128], BF16, "ones1")
    K.memset(pool, ident.v(), 0.0)
    K.aselect(ident.v(), ident.v(), [[-1, 128]], ALU.not_equal, 1.0, 0, 1)
    K.memset(pool, onesf.v(), 1.0)
    K.aselect(tri_i.v(), onesf.v(), [[1, 128]], ALU.is_ge, 0.0, 0, -1)
    K.aselect(tri_s.v(), onesf.v(), [[1, 128]], ALU.is_gt, 0.0, 0, -1)
    K.aselect(tri_l.v(), onesf.v(), [[-1, 128]], ALU.is_gt, 0.0, 0, 1)
    for q in range(4):
        K.cp(pool, mk4[:, q, :], (tri_s if q % 2 == 0 else tri_i).v())
    K.cp(pool, identb.v(), ident.v())
    K.memset(pool, blkb.v(), 0.0)
    K.memset(pool, blkb[0:64, 0:64], 1.0)
    K.memset(pool, blkb[64:128, 64:128], 1.0)
    K.memset(pool, indb.v(), 0.0)
    K.memset(pool, indb[0:64, 0:1], 1.0)
    K.memset(pool, indb[64:128, 1:2], 1.0)
    K.memset(pool, ones1.v(), 1.0)
    trr = K.sb([128, 256], F32, "trr")
    K.cp(pool, trr[:, 0:128], tri_i.v())
    K.cp(pool, trr[:, 128:256], tri_s.v())
    junk = K.sb([128, 1024], BF16, "junk")
    convert_weights()

    WP = Pool(K, NSLOTS)
    WP.push()

    ST = []
    for l in range(2):
        s = dict(
            ssd=K.sb([128, 1024], F32, "Sssd"), ssd_b=K.sb([128, 1024], BF16, "Sssdb"),
            gla=K.sb([128, 1024], F32, "Sgla"), gla_b=K.sb([128, 1024], BF16, "Sglab"),
            rw=K.sb([128, 8, 64], F32, "Srw"), rw_b=K.sb([128, 8, 64], BF16, "Srwb"),
            conv=K.sb([128, 12, 3], F32, "convst"), shift=K.sb([128, 25], F32, "shiftst"),
            KT=K.sb([128, 8, 256], BF16, "KT"), Vm=K.sb([128, 2, 1024], BF16, "Vm"))
        ST.append(s)

    xin = K.sb([128, 12, 131], F32, "xin")
    hbuf = [K.sb([128, 1024], F32, "h") for _ in range(2)]
    mbuf = K.sb([128, 1024], F32, "m")
    uT = K.sb([128, 8, 128], BF16, "uT")
    stat = K.sb([128, 4], F32, "stat")
    st_g = K.sb([128, 8], F32, "st_g")
    st_cdec = K.sb([128, 4], F32, "st_cdec")
    st_EC = K.sb([128, 8], F32, "st_EC")
    st_bonus = K.sb([128, 16], F32, "st_bonus")
    st_mv = K.sb([128, 16], F32, "st_mv")
    st_mx = K.sb([128, 4], F32, "st_mx")
    st_rs = K.sb([128, 4], F32, "st_rs")
    st_dt = K.sb([128, 16], F32, "st_dt")
    st_dtA = K.sb([128, 16], F32, "st_dtA")
    st_edec = K.sb([128, 16], F32, "st_edec")
    rfT8 = K.sb([128, 8, 129], F32, "rfT8")
    NW = 4
    wring = [K.sb([128, 8, 512], BF16, "wblk") for _ in range(NW)]
    wri = [0]
    NPS = 4
    psring = [K.psb([128, 1024], "ps") for _ in range(NPS)]
    psi = [2 * NPS - 1]

    def PS(n=2):
        if n == 1:
            psi[0] = (psi[0] + 1) % (2 * NPS)
            return PsHalf(psring[psi[0] // 2], psi[0] % 2)
        psi[0] = (psi[0] + 2 - (psi[0] % 2)) % (2 * NPS)
        b = psring[psi[0] // 2]
        psi[0] = (psi[0] + 1) % (2 * NPS)
        return b

    def load_fm(dst, src1d, J):
        WP.push()
        t = WP.F()
        K.dma(sp, t[0:J, 0:128], src1d.rr("(j p) -> j p", p=128))
        ps = PS(1)
        K.tr(ps[:, 0:J], t[0:J, 0:128], ident[0:J, 0:J])
        K.cp(dve, dst, ps[:, 0:J])
        WP.pop()

    def store_fm(dst1d, src, J):
        WP.push()
        t = WP.F()
        ps = PS(1)
        K.tr(ps[0:J, 0:128], src, ident.v())
        K.cp(dve, t[0:J, 0:128], ps[0:J, 0:128])
        K.dma(pool, dst1d.rr("(j p) -> j p", p=128), t[0:J, 0:128], final=True)
        WP.pop()

    def loadw(nm, l, c0, ncols):
        wri[0] = (wri[0] + 1) % NW
        wb = wring[wri[0]]
        K.dma(sp, wb[:, :, 0:ncols], WB[(nm, l)].v().rr("(k p) c -> p k c", p=128)[:, :, c0:c0 + ncols])
        return wb

    LC = []
    for l in range(2):
        c = {}

        def fm(nm, j, key=None, src=None):
            b = K.sb([128, j], F32, nm)
            load_fm(b.v(), W[nm][l], j)
            c[key or nm] = b
        fm("norm_mix", 8)
        fm("norm_xattn", 8)
        fm("norm_mem", 8)
        fm("ssd_norm", 8)
        fm("gla_norm", 2)
        fm("ssd_conv_b", 12)
        fm("rwkv_mu", 25)
        fm("rwkv_a0", 8)
        fm("rwkv_k_k", 8)
        fm("rwkv_k_a", 8)
        fm("rwkv_r_k", 8)
        fm("b_merge", 24)
        g8 = K.sb([128, 8], F32, "gn8")
        for j in range(8):
            K.cp(pool, g8[:, j:j + 1], c["gla_norm"][:, (j % 2):(j % 2) + 1])
        c["gla_norm8"] = g8
        cw = K.sb([128, 12, 4], F32, "convw")
        for k in range(4):
            load_fm(cw[:, :, k], W["ssd_conv_w"][l, k], 12)
        c["conv_w"] = cw
        a_t = K.sb([128, 16], F32, "A")
        K.dma(sp, a_t.v(), View(None, W["ssd_A_log"].t[l].partition_broadcast(128)))
        K.actf(a_t.v(), a_t.v(), AF.Exp)
        K.ts(dve, a_t.v(), a_t.v(), -1.0, ALU.mult)
        c["A"] = a_t
        d_t = K.sb([128, 16], F32, "Dsk")
        K.dma(sp, d_t.v(), View(None, W["ssd_D"].t[l].partition_broadcast(128)))
        c["Dsk"] = d_t
        def brow(nm, n, key):
            tmp = WP.F()
            hi = K.sb([1, n], BF16, key + "hi")
            lo = K.sb([1, n], BF16, key + "lo")
            K.dma(sp, tmp[0:1, 0:n], W[nm][l:l + 1, :] if len(W[nm].t.shape) == 2 else W[nm][l:l + 1])
            K.cp(dve, hi.v(), tmp[0:1, 0:n])
            K.tt(dve, tmp[0:1, 0:n], tmp[0:1, 0:n], hi.v(), ALU.subtract)
            K.cp(dve, lo.v(), tmp[0:1, 0:n])
            c[key] = (hi, lo)
        WP.push()
        brow("ssd_dt_bias", 16, "dtb")
        brow("gla_gk_b", 512, "gkb")
        brow("rwkv_w0", 1024, "w0")
        t1 = WP.F()
        gk2 = K.sb([128, 512], BF16, "gk2")
        K.memset(pool, t1.v(), 0.0)
        K.dma(sp, t1[0:16, 0:512], W["gla_gk_w2"][l])
        K.cp(dve, gk2.v(), t1[:, 0:512])
        c["gk2"] = gk2
        t2 = WP.F()
        w2b = K.sb([128, 1024], BF16, "w2a2")
        K.dma(sp, t2[0:64, :], W["rwkv_w2"][l])
        K.dma(sp, t2[64:128, :], W["rwkv_a2"][l])
        K.cp(dve, w2b.v(), t2.v())
        c["w2a2"] = w2b
        WP.pop()
        LC.append(c)

    def blocks(c0, n):
        out = []
        while n > 0:
            b = min(512, n)
            out.append((c0, b))
            c0 += b
            n -= b
        return out

    def proj_tm(nm, l, nt, lhs, c0, n, ps, pcol0=0, bias=None):
        pc = pcol0
        for (cc, b) in blocks(c0, n):
            wb = loadw(nm, l, cc, b)
            segs = []
            s0 = 0
            while s0 < b:
                e0 = min(b, s0 + (512 - (pc + s0) % 512))
                segs.append((s0, e0))
                s0 = e0
            for (s0, e0) in segs:
                o = ps[0:nt, pc + s0:pc + e0]
                for k in range(8):
                    K.mm(o, lhs[:, k, 0:nt], wb[:, k, s0:e0], start=(k == 0), stop=(k == 7 and bias is None))
                if bias is not None:
                    hi, lo, b0 = bias
                    off = b0 + (cc - c0) + s0
                    K.mm(o, ones1[0:1, 0:nt], hi[0:1, off:off + (e0 - s0)], start=False, stop=False)
                    K.mm(o, ones1[0:1, 0:nt], lo[0:1, off:off + (e0 - s0)], start=False, stop=True)
            pc += b

    def proj_fm(nm, l, nt, rhs, c0, n, sink):
        j = 0
        for (cc, b) in blocks(c0, n):
            wb = loadw(nm, l, cc, b)
            ps = PS(1)
            nch = (b + 127) // 128
            for q in range(nch):
                w_ = min(128, b - q * 128)
                o = ps[0:w_, q * 128:q * 128 + nt]
                for k in range(8):
                    K.mm(o, wb[:, k, q * 128:q * 128 + w_], rhs[:, k, 0:nt], start=(k == 0), stop=(k == 7))
            sink(j, ps, nch, b)
            j += nch

    def proj_fm_g(nm, l, nt, rhs, c0, n, sink):
        j = 0
        for (cc, b) in blocks(c0, n):
            wb = loadw(nm, l, cc, b)
            ps = PS(1)
            nch = (b + 127) // 128
            for q in range(nch):
                w_ = min(128, b - q * 128)
                o = ps[0:w_, q * 128:q * 128 + nt]
                for k in range(8):
                    K.mm(o, wb[:, k, q * 128:q * 128 + w_], rhs[:, k, 0:nt], start=(k == 0), stop=(k == 7))
            sink(j, ps, nch, b)
            yield
            j += nch

    def rms_stats(src, nt, ncols, col, scale_n, stat=stat):
        K.actf(junk[0:nt, 0:ncols], src, AF.Square, scale=float(scale_n ** -0.5), accum=stat[0:nt, col:col + 1])
        K.actf(stat[0:nt, col:col + 1], stat[0:nt, col:col + 1], AF.Ln, bias=EPS, scale=1.0)
        K.actf(stat[0:nt, col:col + 1], stat[0:nt, col:col + 1], AF.Exp, scale=-0.5)

    def to_fm(src, nt, dst, gvec=None):
        for half in range(2):
            ps = PS(1)
            for q in range(4):
                k = half * 4 + q
                K.tr(ps[:, q * 128:q * 128 + nt], src[:, k * 128:(k + 1) * 128], ident[0:nt, 0:nt])
            pv = ps[:, 0:512].rr("p (q t) -> p q t", q=4)[:, :, 0:nt]
            if gvec is None:
                K.evac(dst[:, half * 4:half * 4 + 4, 0:nt], pv)
            else:
                K.tt(dve, dst[:, half * 4:half * 4 + 4, 0:nt], pv,
                     gvec[:, half * 4:half * 4 + 4].unsq(2).bc([128, 4, nt]), ALU.mult)

    def norm_to_uT(h, nt, gkey, l):
        WP.push()
        rms_stats(h[0:nt, :], nt, 1024, 0, 1024)
        hn = WP.F()
        K.ts(dve, hn[0:nt, :], h[0:nt, :], stat[0:nt, 0:1], ALU.mult)
        to_fm(hn[0:nt, :], nt, uT, LC[l][gkey])
        WP.pop()

    def gate_accum(l, nt, oTv, wname, bidx, first):
        for half in range(2):
            WP.push()
            psm = PS(1)
            psp = PS(1)
            wbm = loadw("w_in", l, C_MG + bidx * 1024 + half * 512, 512)
            for q in range(4):
                for k in range(8):
                    K.mm(psm[:, q * 128:q * 128 + nt], wbm[:, k, q * 128:(q + 1) * 128], uT[:, k, 0:nt],
                         start=(k == 0), stop=(k == 7))
            wbp = loadw(wname, l, half * 512, 512)
            for q in range(4):
                for k in range(8):
                    K.mm(psp[:, q * 128:q * 128 + nt], wbp[:, k, q * 128:(q + 1) * 128], oTv[:, k, 0:nt],
                         start=(k == 0), stop=(k == 7))
            sg = WP.F()
            for q in range(4):
                j = half * 4 + q
                K.actf(sg[:, q * 128:q * 128 + nt], psm[:, q * 128:q * 128 + nt], AF.Sigmoid,
                       bias=LC[l]["b_merge"][:, bidx * 8 + j:bidx * 8 + j + 1], scale=1.0)
            mv = mbuf.v().rr("p (j t) -> p j t", j=8)[:, half * 4:half * 4 + 4, 0:nt]
            sgv = sg[:, 0:512].rr("p (q t) -> p q t", q=4)[:, :, 0:nt]
            pv = psp[:, 0:512].rr("p (q t) -> p q t", q=4)[:, :, 0:nt]
            if first:
                K.tt(dve, mv, pv, sgv, ALU.mult)
            else:
                K.tt(dve, sgv, pv, sgv, ALU.mult)
                K.tt(pool, mv, mv, sgv, ALU.add)
            WP.pop()

    def ssd_branch(l, nt):
        c, s = LC[l], ST[l]
        WP.push()
        sz = WP.F()
        psz = PS()
        proj_tm("w_in", l, nt, uT, C_Z, 1024, psz)
        K.actf(sz[0:nt, :], psz[0:nt, :], AF.Silu)
        yield
        psd = PS(1)
        proj_tm("w_in", l, nt, uT, C_DT, 16, psd, 0, bias=(c["dtb"][0], c["dtb"][1], 0))
        dt = st_dt[0:nt, 0:16]
        K.actf(dt, psd[0:nt, 0:16], AF.Exp)
        K.actf(dt, dt, AF.Ln, bias=1.0, scale=1.0)
        dtA = st_dtA[0:nt, 0:16]
        K.tt(dve, dtA, dt, c["A"][0:nt, :], ALU.mult)
        yield
        K.cp(dve, xin[:, :, 0:3], s["conv"].v())

        def sink(j, ps, nch, b):
            K.evac(xin[:, j:j + nch, 3:3 + nt], ps[:, 0:nch * 128].rr("p (q t) -> p q t", q=nch)[:, :, 0:nt])
        yield from proj_fm_g("w_in", l, nt, uT, C_XBC, 1536, sink)
        yield "P"
        xc = [WP.F(), WP.F()]

        def xcv(j):
            return xc[j // 8][:, (j % 8) * 128:(j % 8) * 128 + nt]
        for j in range(12):
            e = dve
            K.ts(e, xcv(j), xin[:, j, 0:nt], c["conv_w"][:, j, 0:1], ALU.mult, c["ssd_conv_b"][:, j:j + 1], ALU.add)
            for k in range(1, 4):
                K.stt(e, xcv(j), xin[:, j, k:k + nt], c["conv_w"][:, j, k:k + 1], xcv(j), ALU.mult, ALU.add)
        K.cp(dve, s["conv"].v(), xin[:, :, nt:nt + 3])
        yield
        for q in range(2):
            v = xc[q].v().rr("p (j t) -> p j t", j=8)
            nj = 8 if q == 0 else 4
            K.actf(v[:, 0:nj, 0:nt], v[:, 0:nj, 0:nt], AF.Silu)
        bcT = WP.H()
        bcv = bcT.v().rr("p (j t) -> p j t", j=8)
        K.cp(dve, bcv[:, 0:4, 0:nt], xc[1].v().rr("p (j t) -> p j t", j=8)[:, 0:4, 0:nt])
        yield
        xdt = WP.H()
        xD = WP.F()
        for half in range(2):
            ps = PS(1)
            for q in range(4):
                K.tr(ps[0:nt, q * 128:(q + 1) * 128], xcv(half * 4 + q), ident.v())
            pv = ps[0:nt, 0:512].rr("p (h d) -> p h d", h=8)
            hs = slice(half * 8, half * 8 + 8)
            K.tt(dve, xdt[0:nt, half * 512:(half + 1) * 512].rr("p (h d) -> p h d", h=8), pv,
                 dt[:, hs].unsq(2).bc([nt, 8, 64]), ALU.mult)
            K.tt(dve, xD[0:nt, half * 512:(half + 1) * 512].rr("p (h d) -> p h d", h=8), pv,
                 c["Dsk"][0:nt, hs].unsq(2).bc([nt, 8, 64]), ALU.mult)
            yield
        Btm = WP.H()
        ps = PS(1)
        for g in range(2):
            K.tr(ps[0:nt, g * 128:(g + 1) * 128], xcv(8 + g), ident.v())
        K.evac(Btm[0:nt, 0:256], ps[0:nt, 0:256])
        WP.release(xc[0])
        WP.release(xc[1])
        yield
        ps = PS(1)
        K.mm(ps[0:nt, 0:16], tri_i[0:nt, 0:nt], dtA)
        K.mm(ps[:, 16:32], onesf[0:nt, :], dtA)
        edec = st_edec[:, 0:16]
        K.actf(edec, ps[:, 16:32], AF.Exp)
        indec = WP.F()
        K.actf(indec[0:nt, 0:16], ps[0:nt, 0:16], AF.Exp)
        K.cp(dve, indec[0:nt, 32:48], ps[0:nt, 0:16])
        K.tt(dve, indec[0:nt, 16:32], ps[0:nt, 16:32], indec[0:nt, 32:48], ALU.subtract)
        K.actf(indec[0:nt, 16:32], indec[0:nt, 16:32], AF.Exp)
        xdte = WP.H()
        K.tt(dve, xdte[0:nt, :].rr("p (h d) -> p h d", h=16), xdt[0:nt, :].rr("p (h d) -> p h d", h=16),
             indec[0:nt, 16:32].unsq(2).bc([nt, 16, 64]), ALU.mult)
        yield
        cbm = indec[:, 512:1024]
        ps = PS(1)
        for g in range(2):
            K.mm(ps[0:nt, g * 128:g * 128 + nt], bcv[:, g, 0:nt], bcv[:, 2 + g, 0:nt])
        K.tt(dve, cbm[0:nt, 0:256].rr("p (g t) -> p g t", g=2)[:, :, 0:nt],
             ps[0:nt, 0:256].rr("p (g t) -> p g t", g=2)[:, :, 0:nt],
             tri_i[0:nt, 0:nt].unsq(1).bc([nt, 2, nt]), ALU.mult)
        yield
        rhsd = [None, None]
        Mh = [WP.H(), WP.H()]
        for g in range(2):
            rhsd[g] = WP.F()
            K.tt(dve if g == 0 else pool, rhsd[g][0:nt, :].rr("p (h t) -> p h t", h=8)[:, :, 0:nt],
                 tri_i[0:nt, 0:nt].unsq(1).bc([nt, 8, nt]),
                 dtA[:, g * 8:(g + 1) * 8].unsq(2).bc([nt, 8, nt]), ALU.mult)
            ps = PS()
            for q in range(2):
                K.mm(ps[0:nt, q * 512:(q + 1) * 512], tri_l[0:nt, 0:nt], rhsd[g][0:nt, q * 512:(q + 1) * 512])
            ex = rhsd[g]
            K.actf(ex[0:nt, :], ps[0:nt, :], AF.Exp)
            K.tt(dve, Mh[g][0:nt, :].rr("p (h t) -> p h t", h=8)[:, :, 0:nt],
                 ex[0:nt, :].rr("p (h t) -> p h t", h=8)[:, :, 0:nt],
                 cbm[0:nt, g * 128:g * 128 + nt].unsq(1).bc([nt, 8, nt]), ALU.mult)
            WP.release(rhsd[g])
            yield
        psy = PS()
        for g in range(2):
            K.mm(psy[0:nt, g * 512:(g + 1) * 512], bcv[:, 2 + g, 0:nt], s["ssd_b"][:, g * 512:(g + 1) * 512])
        y = WP.F()
        K.tt(dve, y[0:nt, :].rr("p (h d) -> p h d", h=16), psy[0:nt, :].rr("p (h d) -> p h d", h=16),
             indec[0:nt, 0:16].unsq(2).bc([nt, 16, 64]), ALU.mult)
        K.tt(pool, y[0:nt, :], y[0:nt, :], xD[0:nt, :], ALU.add)
        WP.release(xD)
        yield
        psd2 = PS()
        for h in range(16):
            g = h // 8
            K.mm(psd2[0:nt, h * 64:(h + 1) * 64], Mh[g][0:nt, (h % 8) * 128:(h % 8) * 128 + nt],
                 xdt[0:nt, h * 64:(h + 1) * 64])
        K.tt(dve, y[0:nt, :], y[0:nt, :], psd2[0:nt, :], ALU.add)
        WP.release(Mh[0])
        WP.release(Mh[1])
        yield
        pss = PS()
        for g in range(2):
            K.mm(pss[:, g * 512:(g + 1) * 512], Btm[0:nt, g * 128:(g + 1) * 128], xdte[0:nt, g * 512:(g + 1) * 512])
        K.tt(dve, s["ssd"].v().rr("p (h d) -> p h d", h=16), s["ssd"].v().rr("p (h d) -> p h d", h=16),
             edec.unsq(2).bc([128, 16, 64]), ALU.mult)
        K.tt(dve, s["ssd"].v(), s["ssd"].v(), pss.v(), ALU.add)
        K.cp(act, s["ssd_b"].v(), s["ssd"].v())
        yield
        K.tt(dve, y[0:nt, :], y[0:nt, :], sz[0:nt, :], ALU.mult)
        for g in range(2):
            rms_stats(y[0:nt, g * 512:(g + 1) * 512], nt, 512, 2 + g, 512, stat=st_g)
        K.tt(dve, y[0:nt, :].rr("p (g d) -> p g d", g=2), y[0:nt, :].rr("p (g d) -> p g d", g=2),
             st_g[0:nt, 2:4].unsq(2).bc([nt, 2, 512]), ALU.mult)
        oT = WP.H()
        oTv = oT.v().rr("p (j t) -> p j t", j=8)
        to_fm(y[0:nt, :], nt, oTv, c["ssd_norm"])
        yield
        gate_accum(l, nt, oTv, "w_proj_ssd", 0, True)
        WP.pop()

    def gla_branch(l, nt):
        c, s = LC[l], ST[l]
        WP.push()
        glrT = WP.H()

        def sink(j, ps, nch, b):
            K.evac(glrT[:, 0:nt], ps[:, 0:nt])
        yield from proj_fm_g("w_in", l, nt, uT, C_GLR, 128, sink)
        ps = PS(1)
        K.mm(ps[0:nt, 0:512], glrT[:, 0:nt], c["gk2"].v(), start=True, stop=False)
        K.mm(ps[0:nt, 0:512], ones1[0:1, 0:nt], c["gkb"][0].v(), start=False, stop=False)
        K.mm(ps[0:nt, 0:512], ones1[0:1, 0:nt], c["gkb"][1].v(), start=False, stop=True)
        gl = WP.F()
        K.actf(gl[0:nt, 0:512], ps[0:nt, 0:512], AF.Exp, scale=-1.0)
        K.actf(gl[0:nt, 0:512], gl[0:nt, 0:512], AF.Ln, bias=1.0, scale=1.0)
        yield
        ps = PS()
        K.mm(ps[0:nt, 0:512], tri_i[0:nt, 0:nt], gl[0:nt, 0:512])
        K.mm(ps[0:nt, 512:1024], onesf[0:nt, 0:nt], gl[0:nt, 0:512])
        E = WP.F()
        K.actf(E[0:nt, 0:512], ps[0:nt, 0:512], AF.Exp, scale=-1.0 / 16)
        K.actf(E[0:nt, 512:1024], ps[0:nt, 0:512], AF.Exp, scale=1.0 / 16)
        K.actf(gl[0:nt, 512:1024], ps[0:nt, 512:1024], AF.Exp, scale=-1.0 / 16)
        yield
        pst = PS(1)
        for h in range(4):
            K.mm(pst[:, h * 16:(h + 1) * 16], gl[0:nt, h * 128:(h + 1) * 128], onesf[0:nt, 0:16])
        cdec = st_cdec[:, 0:4]
        K.actf(cdec, pst[:, 0:64].rr("p (h x) -> p h x", x=16)[:, :, 0], AF.Exp, scale=-1.0 / 16)
        yield
        psq = PS()
        proj_tm("w_in", l, nt, uT, C_GQ, 1024, psq)
        qk = WP.F()
        K.stt(dve, qk[0:nt, 0:512], psq[0:nt, 0:512], float(128 ** -0.5), E[0:nt, 0:512], ALU.mult, ALU.mult)
        K.tt(dve, qk[0:nt, 512:1024], psq[0:nt, 512:1024], E[0:nt, 512:1024], ALU.mult)
        kend = WP.H()
        K.tt(dve, kend[0:nt, 0:512], qk[0:nt, 512:1024], gl[0:nt, 512:1024], ALU.mult)
        WP.release(gl)
        WP.release(E)
        yield
        qkT = WP.H()
        qkTv = qkT.v().rr("p (j t) -> p j t", j=8)
        to_fm(qk[0:nt, :], nt, qkTv)
        WP.release(qk)
        yield
        psv = PS()
        proj_tm("w_in", l, nt, uT, C_GV, 1024, psv)
        vb = WP.H()
        K.evac(vb[0:nt, :], psv[0:nt, :])
        yield
        psg = PS()
        proj_tm("w_in", l, nt, uT, C_GG, 1024, psg)
        gs = WP.F()
        K.actf(gs[0:nt, :], psg[0:nt, :], AF.Silu)
        yield "P"
        ps = PS(1)
        for h in range(4):
            K.mm(ps[0:nt, h * 128:h * 128 + nt], qkTv[:, 4 + h, 0:nt], qkTv[:, h, 0:nt])
        Am = WP.H()
        K.tt(dve, Am[0:nt, 0:512].rr("p (h t) -> p h t", h=4)[:, :, 0:nt],
             ps[0:nt, 0:512].rr("p (h t) -> p h t", h=4)[:, :, 0:nt],
             tri_i[0:nt, 0:nt].unsq(1).bc([nt, 4, nt]), ALU.mult)
        yield
        pso = PS()
        for h in range(4):
            for q in range(1):
                o = pso[0:nt, h * 256:(h + 1) * 256]
                K.mm(o, Am[0:nt, h * 128:h * 128 + nt], vb[0:nt, h * 256:(h + 1) * 256], start=True, stop=False)
                K.mm(o, qkTv[:, h, 0:nt], s["gla_b"][:, h * 256:(h + 1) * 256], start=False, stop=True)
        pss = PS()
        for h in range(4):
            K.mm(pss[:, h * 256:(h + 1) * 256], kend[0:nt, h * 128:(h + 1) * 128], vb[0:nt, h * 256:(h + 1) * 256])
        K.tt(dve, s["gla"].v().rr("p (h d) -> p h d", h=4), s["gla"].v().rr("p (h d) -> p h d", h=4),
             cdec.unsq(2).bc([128, 4, 256]), ALU.mult)
        K.tt(dve, s["gla"].v(), s["gla"].v(), pss.v(), ALU.add)
        K.cp(act, s["gla_b"].v(), s["gla"].v())
        o = WP.F()
        K.cp(act, o[0:nt, :], pso[0:nt, :])
        yield
        for h in range(4):
            rms_stats(o[0:nt, h * 256:(h + 1) * 256], nt, 256, 4 + h, 256, stat=st_g)
        K.tt(dve, o[0:nt, :].rr("p (h d) -> p h d", h=4), o[0:nt, :].rr("p (h d) -> p h d", h=4),
             st_g[0:nt, 4:8].unsq(2).bc([nt, 4, 256]), ALU.mult)
        K.tt(dve, o[0:nt, :], o[0:nt, :], gs[0:nt, :], ALU.mult)
        oT = WP.H()
        oTv = oT.v().rr("p (j t) -> p j t", j=8)
        to_fm(o[0:nt, :], nt, oTv, c["gla_norm8"])
        yield
        gate_accum(l, nt, oTv, "w_proj_gla", 1, False)
        WP.pop()

    def rwkv_branch(l, nt):
        c, s = LC[l], ST[l]
        WP.push()
        tw, prod = WP.H(), WP.H()
        ktT, btT = WP.H(), WP.H()
        Vtm = WP.H()
        gsr = WP.F()
        arM = [[None, None], [None, None]]
        ktmM, btmM = [None, None], [None, None]
        prv = prod.v().rr("p (j t) -> p j t", j=8)
        ktTv = ktT.v().rr("p (j t) -> p j t", j=8)
        btTv = btT.v().rr("p (j t) -> p j t", j=8)

        def arv(j, par):
            return arM[par][j // 4].v().rr("p (q x t) -> p q x t", q=4, x=2)[:, j % 4]
        WP.push()
        dd = [WP.F(), WP.F(), WP.F(), WP.F()]
        for q in range(4):
            nj = 8 if q < 3 else 1
            K.cp(dve, rfT8[:, 0:nj, 0], s["shift"][:, q * 8:q * 8 + nj])

            def sink(j, ps, nch, b):
                K.evac(rfT8[:, j:j + nch, 1:1 + nt], ps[:, 0:nch * 128].rr("p (q t) -> p q t", q=nch)[:, :, 0:nt])
            yield from proj_fm_g("w_in", l, nt, uT, C_RF + q * 1024, nj * 128, sink)
            K.cp(dve, s["shift"][:, q * 8:q * 8 + nj], rfT8[:, 0:nj, nt])
            dv = dd[q].v().rr("p (j t) -> p j t", j=8)[:, 0:nj, 0:nt]
            e = pool if q % 2 == 0 else dve
            K.tt(e, dv, rfT8[:, 0:nj, 0:nt], rfT8[:, 0:nj, 1:1 + nt], ALU.subtract)
            K.tt(e, dv, dv, c["rwkv_mu"][:, q * 8:q * 8 + nj].unsq(2).bc([128, nj, nt]), ALU.mult)
            K.tt(e, dv, dv, rfT8[:, 0:nj, 1:1 + nt], ALU.add)
            yield
        psg = PS()
        proj_tm("w_in", l, nt, uT, C_RG, 1024, psg)
        K.actf(gsr[0:nt, :], psg[0:nt, :], AF.Silu)
        yield "P"
        rT = dd[0].v().rr("p (j t) -> p j t", j=8)
        kT = dd[1].v().rr("p (j t) -> p j t", j=8)
        vT = dd[2].v().rr("p (j t) -> p j t", j=8)
        wa = dd[3].v().rr("p (j t) -> p j t", j=8)
        K.actf(tw[0:64, 0:nt], wa[0:64, 0, 0:nt], AF.Tanh)
        K.cp(dve, tw[64:128, 0:nt], wa[64:128, 0, 0:nt])
        yield
        for half in range(2):
            ps = PS(1)
            for q in range(4):
                K.tr(ps[0:nt, q * 128:(q + 1) * 128], vT[:, half * 4 + q, 0:nt], ident.v())
            K.evac(Vtm[0:nt, half * 512:(half + 1) * 512], ps[0:nt, 0:512])
            yield
        psw = PS()
        for q in range(2):
            o = psw[0:nt, q * 512:(q + 1) * 512]
            K.mm(o, tw[0:64, 0:nt], c["w2a2"][0:64, q * 512:(q + 1) * 512], start=True, stop=False)
            K.mm(o, ones1[0:1, 0:nt], c["w0"][0][0:1, q * 512:(q + 1) * 512], start=False, stop=False)
            K.mm(o, ones1[0:1, 0:nt], c["w0"][1][0:1, q * 512:(q + 1) * 512], start=False, stop=True)
        sgT = dd[3]
        K.actf(sgT[0:nt, :], psw[0:nt, :], AF.Sigmoid)
        yield
        E1, E2, E3 = WP.F(), WP.F(), WP.F()
        psc = PS(1)
        for half in range(2):
            ps = PS()
            for q in range(4):
                j = half * 4 + q
                K.mm(ps[:, q * 256:(q + 1) * 256], sgT[0:nt, j * 128:(j + 1) * 128], trr[0:nt, 0:256])
                K.mm(psc[:, j * 16:(j + 1) * 16], sgT[0:nt, j * 128:(j + 1) * 128], onesf[0:nt, 0:16])
            pv = ps.v().rr("p (q x t) -> p q x t", q=4, x=2)
            sl = slice(half * 512, (half + 1) * 512)
            K.actf(E1[:, sl].rr("p (q t) -> p q t", q=4)[:, :, 0:nt], pv[:, :, 0, 0:nt], AF.Exp, scale=-LAM)
            K.actf(E2[:, sl].rr("p (q t) -> p q t", q=4)[:, :, 0:nt], pv[:, :, 0, 0:nt], AF.Exp, scale=LAM)
            K.actf(E3[:, sl].rr("p (q t) -> p q t", q=4)[:, :, 0:nt], pv[:, :, 1, 0:nt], AF.Exp, scale=-LAM)
        EC = st_EC[:, 0:8]
        K.actf(EC, psc[:, 0:128].rr("p (j x) -> p j x", x=16)[:, :, 0], AF.Exp, scale=-LAM)
        yield
        E1v = E1.v().rr("p (j t) -> p j t", j=8)
        E2v = E2.v().rr("p (j t) -> p j t", j=8)
        E3v = E3.v().rr("p (j t) -> p j t", j=8)
        aT = WP.F()
        aTv = aT.v().rr("p (j t) -> p j t", j=8)
        for half in range(2):
            ps = PS(1)
            for q in range(4):
                j = half * 4 + q
                K.mm(ps[:, q * 128:q * 128 + nt], c["w2a2"][64:128, j * 128:(j + 1) * 128], tw[64:128, 0:nt])
            for q in range(4):
                j = half * 4 + q
                K.actf(aTv[:, j, 0:nt], ps[:, q * 128:q * 128 + nt], AF.Sigmoid, bias=c["rwkv_a0"][:, j:j + 1], scale=1.0)
            yield
        kk = WP.F()
        kkv = kk.v().rr("p (j t) -> p j t", j=8)
        K.tt(dve, kkv[:, :, 0:nt], kT[:, :, 0:nt], c["rwkv_k_k"].v().unsq(2).bc([128, 8, nt]), ALU.mult)
        sq = WP.H()
        sqv = sq.v().rr("p (j t) -> p j t", j=8)
        K.tt(pool, sqv[:, :, 0:nt], kkv[:, :, 0:nt], kkv[:, :, 0:nt], ALU.mult)
        ps = PS()
        for half in range(2):
            for q in range(4):
                j = half * 4 + q
                K.mm(ps[:, j * 128:j * 128 + nt], blkb.v(), sqv[:, j, 0:nt])
        nrm = WP.F()
        nrv = nrm.v().rr("p (j t) -> p j t", j=8)
        K.ts(dve, nrv[:, :, 0:nt], ps.v().rr("p (j t) -> p j t", j=8)[:, :, 0:nt], 1e-24, ALU.max)
        K.actf(nrv[:, :, 0:nt], nrv[:, :, 0:nt], AF.Ln)
        K.actf(nrv[:, :, 0:nt], nrv[:, :, 0:nt], AF.Exp, scale=-0.5)
        K.tt(dve, kkv[:, :, 0:nt], kkv[:, :, 0:nt], nrv[:, :, 0:nt], ALU.mult)
        yield
        k7v = nrv
        K.stt(dve, k7v[:, :, 0:nt], aTv[:, :, 0:nt], -1.0, c["rwkv_k_a"].v().unsq(2).bc([128, 8, nt]), ALU.add, ALU.mult)
        K.stt(dve, k7v[:, :, 0:nt], k7v[:, :, 0:nt], 1.0, kT[:, :, 0:nt], ALU.add, ALU.mult)
        tmpf = WP.F()
        tfv = tmpf.v().rr("p (j t) -> p j t", j=8)
        K.tt(pool, tfv[:, :, 0:nt], rT[:, :, 0:nt], k7v[:, :, 0:nt], ALU.mult)
        K.tt(pool, prv[:, :, 0:nt], tfv[:, :, 0:nt], c["rwkv_r_k"].v().unsq(2).bc([128, 8, nt]), ALU.mult)
        yield
        WP.release(tmpf)
        for par in range(2):
            for half in range(2):
                arM[par][half] = WP.H(scope=-2)
                K.memset(pool, arM[par][half][(1 - par) * 64:(2 - par) * 64, :], 0.0)
        for half in range(2):
            js = slice(half * 4, half * 4 + 4)
            for par in range(2):
                rows = slice(par * 64, par * 64 + 64)
                a4 = arM[par][half].v().rr("p (q x t) -> p q x t", q=4, x=2)
                K.stt(dve, a4[rows, :, 0, 0:nt], kkv[rows, js, 0:nt], -1.0, E3v[rows, js, 0:nt], ALU.mult, ALU.mult)
                K.tt(dve, a4[rows, :, 1, 0:nt], rT[rows, js, 0:nt], E1v[rows, js, 0:nt], ALU.mult)
        ktfv = E1v
        K.tt(dve, ktfv[:, :, 0:nt], k7v[:, :, 0:nt], E2v[:, :, 0:nt], ALU.mult)
        btfv = E3v
        K.tt(dve, btfv[:, :, 0:nt], kkv[:, :, 0:nt], aTv[:, :, 0:nt], ALU.mult)
        K.tt(dve, btfv[:, :, 0:nt], btfv[:, :, 0:nt], E2v[:, :, 0:nt], ALU.mult)
        K.cp(act, ktTv[:, :, 0:nt], ktfv[:, :, 0:nt])
        K.cp(act, btTv[:, :, 0:nt], btfv[:, :, 0:nt])
        yield
        WP.release(aT)
        WP.release(kk)
        WP.release(nrm)
        WP.release(sq)
        for par in range(2):
            ktmM[par] = WP.H(scope=-2)
            btmM[par] = WP.H(scope=-2)
        for (src, dstM) in ((ktfv, ktmM), (btfv, btmM)):
            for half in range(2):
                ps = PS(1)
                for q in range(4):
                    K.tr(ps[0:nt, q * 128:(q + 1) * 128], src[:, half * 4 + q, 0:nt], ident.v())
                pv = ps[0:nt, 0:512].rr("p (q h k) -> p q h k", q=4, h=2)
                for par in range(2):
                    dv = dstM[par][0:nt, half * 512:(half + 1) * 512].rr("p (q h k) -> p q h k", q=4, h=2)
                    K.memset(pool, dv[:, :, 1 - par, :], 0.0)
                    K.cp(act if par == 0 else dve, dv[:, :, par, :], pv[:, :, par, :])
                yield
        psb = PS(1)
        for j in range(8):
            K.mm(psb[0:nt, 2 * j:2 * j + 2], prv[:, j, 0:nt], indb.v())
        bonus = st_bonus[0:nt, 0:16]
        K.cp(dve, bonus, psb[0:nt, 0:16])
        yield
        WP.pop()
        WP.release(tw)
        WP.release(prod)
        WP.push()
        o7 = WP.F()
        nlev = 6 if nt > 64 else (5 if nt > 32 else 4)
        def group_gen(g, SCg, Pb, PTb, Wb):
            XU = Pb
            scg = [SCg[0].v().rr("p (h k t) -> p h k t", h=2, k=4), SCg[1].v().rr("p (h k t) -> p h k t", h=2, k=4)]

            def sc(hh, kind):
                return scg[hh // 2][0:nt, hh % 2, kind, 0:nt]

            def v4(b, par):
                return b[0:nt, par * 512:(par + 1) * 512].rr("p (h t) -> p h t", h=4)[:, :, 0:nt]
            psn = PS(1)
            ps = None
            for hh in range(4):
                h = g * 4 + hh
                j, hp = h // 2, (h % 2) * 64
                par = h % 2
                av = arv(j, par)
                if hh % 2 == 0:
                    ps = PS()
                base = (hh % 2) * 512
                if nt == 128:
                    K.mm(ps[0:nt, base:base + 256], btTv[:, j, 0:nt],
                         arM[par][j // 4][:, (j % 4) * 256:(j % 4 + 1) * 256])
                    K.mm(ps[0:nt, base + 256:base + 512], ktTv[:, j, 0:nt],
                         arM[par][j // 4][:, (j % 4) * 256:(j % 4 + 1) * 256])
                else:
                    for x in range(2):
                        K.mm(ps[0:nt, base + x * 128:base + x * 128 + nt], btTv[:, j, 0:nt], av[:, x, 0:nt])
                        K.mm(ps[0:nt, base + 256 + x * 128:base + 256 + x * 128 + nt], ktTv[:, j, 0:nt], av[:, x, 0:nt])
                K.mm(psn[0:nt, hh * 128:hh * 128 + nt], av[:, 0, 0:nt], btTv[:, j, 0:nt])
                if hh % 2 == 1:
                    for h2 in range(2):
                        K.tt(dve, scg[hh // 2][0:nt, h2, :, 0:nt],
                             ps[0:nt, h2 * 512:(h2 + 1) * 512].rr("p (k t) -> p k t", k=4)[:, :, 0:nt],
                             mk4[0:nt, :, 0:nt], ALU.mult)
            K.tt(dve, v4(PTb, 0), psn[0:nt, 0:512].rr("p (h t) -> p h t", h=4)[:, :, 0:nt],
                 tri_l[0:nt, 0:nt].unsq(1).bc([nt, 4, nt]), ALU.mult)
            for hh in range(4):
                K.cp(pool, v4(Pb, 0)[:, hh, :], sc(hh, 0))
                K.tt(pool, v4(Wb, 0)[:, hh, :], sc(hh, 0), identb[0:nt, 0:nt], ALU.add)
            yield
            cur = 0
            for lev in range(1, nlev + 1):
                nxt = 1 - cur
                last = (lev == nlev)
                psP = PS()
                for hh in range(4):
                    Pc, PTc = v4(Pb, cur)[:, hh, :], v4(PTb, cur)[:, hh, :]
                    if not last:
                        K.mm(psP[0:nt, hh * 128:hh * 128 + nt], PTc, Pc)
                    K.mm(psP[0:nt, 512 + hh * 128:512 + hh * 128 + nt], Pc, PTc)
                if not last:
                    K.evac(v4(Pb, nxt), psP[0:nt, 0:512].rr("p (h t) -> p h t", h=4)[:, :, 0:nt])
                K.evac(v4(PTb, nxt), psP[0:nt, 512:1024].rr("p (h t) -> p h t", h=4)[:, :, 0:nt])
                yield
                psW = PS(1)
                for hh in range(4):
                    Wc = v4(Wb, cur)[:, hh, :]
                    o = psW[0:nt, hh * 128:hh * 128 + nt]
                    K.mm(o, v4(PTb, nxt)[:, hh, :], Wc, start=True, stop=False)
                    K.mm(o, identb[0:nt, 0:nt], Wc, start=False, stop=True)
                K.evac(v4(Wb, nxt), psW[0:nt, 0:512].rr("p (h t) -> p h t", h=4)[:, :, 0:nt])
                cur = nxt
                yield
            Wf = v4(Wb, cur)
            psX = PS(1)
            for hh in range(4):
                h = g * 4 + hh
                j, par = h // 2, h % 2
                o = psX[0:nt, hh * 64:(hh + 1) * 64]
                K.mm(o, arv(j, par)[:, 0, 0:nt], s["rw_b"][:, j, :], start=True, stop=False)
                K.mm(o, sc(hh, 2), Vtm[0:nt, h * 64:(h + 1) * 64], start=False, stop=True)
            Xb = XU[0:nt, 0:256]
            K.evac(Xb, psX[0:nt, 0:256])
            yield
            psU = PS(1)
            for hh in range(4):
                K.mm(psU[0:nt, hh * 64:(hh + 1) * 64], Wf[:, hh, :], Xb[:, hh * 64:(hh + 1) * 64])
            Ub = XU[0:nt, 512:768]
            K.evac(Ub, psU[0:nt, 0:256])
            yield
            psO = PS()
            for hh in range(4):
                h = g * 4 + hh
                j, par = h // 2, h % 2
                o = psO[0:nt, hh * 64:(hh + 1) * 64]
                K.mm(o, arv(j, par)[:, 1, 0:nt], s["rw_b"][:, j, :], start=True, stop=False)
                K.mm(o, sc(hh, 1), Ub[:, hh * 64:(hh + 1) * 64], start=False, stop=False)
                K.mm(o, sc(hh, 3), Vtm[0:nt, h * 64:(h + 1) * 64], start=False, stop=True)
            for jj in range(2):
                j = 2 * g + jj
                o2 = psO[:, 512 + jj * 64:512 + (jj + 1) * 64]
                for par in range(2):
                    hh = 2 * jj + par
                    h = g * 4 + hh
                    K.mm(o2, btmM[par][0:nt, j * 128:(j + 1) * 128], Ub[:, hh * 64:(hh + 1) * 64],
                         start=(par == 0), stop=False)
                    K.mm(o2, ktmM[par][0:nt, j * 128:(j + 1) * 128], Vtm[0:nt, h * 64:(h + 1) * 64],
                         start=False, stop=(par == 1))
            K.cp(act, o7[0:nt, g * 256:(g + 1) * 256], psO[0:nt, 0:256])
            rwg = s["rw"][:, 2 * g:2 * g + 2, :]
            K.tt(dve, rwg, rwg, psO[:, 512:640].rr("p (j v) -> p j v", j=2), ALU.add)
            K.tt(dve, rwg, rwg, EC[:, 2 * g:2 * g + 2].unsq(2).bc([128, 2, 64]), ALU.mult)
            K.cp(act, s["rw_b"][:, 2 * g:2 * g + 2, :], rwg)
            yield

        WP.push()
        gens = []
        for g in range(4):
            bufs = ([WP.H(), WP.H()], WP.H(), WP.H(), WP.H())
            gens.append(group_gen(g, *bufs))
        live = list(gens)
        while live:
            for gn in list(live):
                try:
                    next(gn)
                except StopIteration:
                    live.remove(gn)
            yield
        WP.pop()
        lnw, lnb = WP.F(), WP.F()
        K.dma(sp, lnw[0:nt, :], View(None, W["rwkv_ln_w"].t[l].partition_broadcast(nt)))
        K.dma(sp, lnb[0:nt, :], View(None, W["rwkv_ln_b"].t[l].partition_broadcast(nt)))
        o7h = o7[0:nt, :].rr("p (h d) -> p h d", h=16)
        mean = st_mv[0:nt, 0:16]
        K.red(dve, mean, o7h, ALU.add)
        K.ts(dve, mean, mean, 1.0 / 64, ALU.mult)
        K.tt(dve, o7h, o7h, mean.unsq(2).bc([nt, 16, 64]), ALU.subtract)
        sq2 = WP.F()
        K.tt(pool, sq2[0:nt, :], o7[0:nt, :], o7[0:nt, :], ALU.mult)
        var = st_mv[0:nt, 0:16]
        K.red(dve, var, sq2[0:nt, :].rr("p (h d) -> p h d", h=16), ALU.add)
        K.actf(var, var, AF.Ln, bias=LN_EPS, scale=1.0 / 64)
        K.actf(var, var, AF.Exp, scale=-0.5)
        K.tt(dve, o7h, o7h, var.unsq(2).bc([nt, 16, 64]), ALU.mult)
        yield
        K.tt(dve, o7[0:nt, :], o7[0:nt, :], lnw[0:nt, :], ALU.mult)
        K.tt(dve, o7[0:nt, :], o7[0:nt, :], lnb[0:nt, :], ALU.add)
        bv = sq2
        K.tt(dve, bv[0:nt, :].rr("p (h d) -> p h d", h=16), Vtm[0:nt, :].rr("p (h d) -> p h d", h=16),
             bonus.unsq(2).bc([nt, 16, 64]), ALU.mult)
        K.tt(dve, o7[0:nt, :], o7[0:nt, :], bv[0:nt, :], ALU.add)
        yield
        K.tt(dve, o7[0:nt, :], o7[0:nt, :], gsr[0:nt, :], ALU.mult)
        oT = WP.H()
        oTv = oT.v().rr("p (j t) -> p j t", j=8)
        to_fm(o7[0:nt, :], nt, oTv)
        yield
        gate_accum(l, nt, oTv, "w_proj_rwkv", 2, False)
        WP.pop()
        WP.pop()

    def out_and_xattn(l, nt, h):
        s = ST[l]
        WP.push()
        mT = WP.H()
        mTv = mT.v().rr("p (j t) -> p j t", j=8)
        K.cp(act, mTv[:, :, 0:nt], mbuf.v().rr("p (j t) -> p j t", j=8)[:, :, 0:nt])
        ps = PS()
        proj_tm("w_out", l, nt, mTv, 0, 1024, ps)
        K.tt(dve, h[0:nt, :], h[0:nt, :], ps[0:nt, :], ALU.add)
        norm_to_uT(h, nt, "norm_xattn", l)
        qT = WP.H()
        qTv = qT.v().rr("p (j t) -> p j t", j=8)

        def sink(j, ps, nch, b):
            K.evac(qTv[:, j:j + nch, 0:nt], ps[:, 0:nch * 128].rr("p (q t) -> p q t", q=nch)[:, :, 0:nt])
        proj_fm("xa_wq", l, nt, uT, 0, 1024, sink)
        pss = PS()
        for hd in range(4):
            o = pss[0:nt, hd * 256:(hd + 1) * 256]
            for cc in range(2):
                K.mm(o, qTv[:, 2 * hd + cc, 0:nt], s["KT"][:, 2 * hd + cc, :], start=(cc == 0), stop=(cc == 1))
        mx = st_mx[0:nt, 0:4]
        K.red(dve, mx, pss[0:nt, :].rr("p (h m) -> p h m", h=4), ALU.max)
        K.ts(dve, mx, mx, -1.0 / 16, ALU.mult)
        e = WP.F()
        rs = st_rs[0:nt, 0:4]
        for hd in range(4):
            K.actf(e[0:nt, hd * 256:(hd + 1) * 256], pss[0:nt, hd * 256:(hd + 1) * 256], AF.Exp,
                   bias=mx[:, hd:hd + 1], scale=1.0 / 16, accum=rs[:, hd:hd + 1])
        K.recip(rs, rs)
        K.tt(dve, e[0:nt, :].rr("p (h m) -> p h m", h=4), e[0:nt, :].rr("p (h m) -> p h m", h=4),
             rs.unsq(2).bc([nt, 4, 256]), ALU.mult)
        pT = WP.H()
        pTv = pT.v().rr("p (j t) -> p j t", j=8)
        to_fm(e[0:nt, :], nt, pTv)
        oT = WP.H()
        oTv = oT.v().rr("p (j t) -> p j t", j=8)
        for half in range(2):
            ps = PS(1)
            for q in range(4):
                j = half * 4 + q
                hd = j // 2
                for mc in range(2):
                    K.mm(ps[:, q * 128:q * 128 + nt], s["Vm"][:, mc, j * 128:(j + 1) * 128], pTv[:, hd * 2 + mc, 0:nt],
                         start=(mc == 0), stop=(mc == 1))
            K.evac(oTv[:, half * 4:half * 4 + 4, 0:nt], ps[:, 0:512].rr("p (q t) -> p q t", q=4)[:, :, 0:nt])
        ps = PS()
        proj_tm("xa_wo", l, nt, oTv, 0, 1024, ps)
        K.tt(dve, h[0:nt, :], h[0:nt, :], ps[0:nt, :], ALU.add)
        WP.pop()

    def final_norm(h, nt, ydst):
        WP.push()
        rms_stats(h[0:nt, :], nt, 1024, 0, 1024)
        g = WP.F()
        K.dma(sp, g[0:nt, :], View(None, W["norm_final"].t.partition_broadcast(nt)))
        y = WP.F()
        K.stt(dve, y[0:nt, :], h[0:nt, :], stat[0:nt, 0:1], g[0:nt, :], ALU.mult, ALU.mult)
        K.dma(pool, ydst, y[0:nt, :], final=True)
        WP.pop()

    def mem_kv(l):
        s = ST[l]
        WP.push()
        mT = WP.H()
        mT2 = WP.H()
        mTs = [mT.v().rr("p (j t) -> p j t", j=8), mT2.v().rr("p (j t) -> p j t", j=8)]
        for mt in range(2):
            WP.push()
            x = WP.F()
            K.dma(sp, x.v(), mem_prompt[mt * 128:(mt + 1) * 128, :])
            rms_stats(x.v(), 128, 1024, 0, 1024)
            K.ts(dve, x.v(), x.v(), stat[:, 0:1], ALU.mult)
            to_fm(x.v(), 128, mTs[mt], LC[l]["norm_mem"])
            WP.pop()
        for mt in range(2):
            for (nm, dst) in (("xa_wk", mem_k_o), ("xa_wv", mem_v_o)):
                WP.push()
                ps = PS()
                proj_tm(nm, l, 128, mTs[mt], 0, 1024, ps)
                o = WP.F()
                K.cp(act, o.v(), ps.v())
                K.dma(pool, dst[l, mt * 128:(mt + 1) * 128, :], o.v(), final=True)
                if nm == "xa_wv":
                    K.cp(dve, s["Vm"][:, mt, :], o.v())
                else:
                    to_fm(o.v(), 128, s["KT"][:, :, mt * 128:(mt + 1) * 128])
                WP.pop()
        WP.pop()

    def load_cache_kv(l):
        s = ST[l]
        for mt in range(2):
            WP.push()
            x = WP.F()
            K.dma(sp, x.v(), c_mk[l, mt * 128:(mt + 1) * 128, :])
            to_fm(x.v(), 128, s["KT"][:, :, mt * 128:(mt + 1) * 128])
            x2 = WP.F()
            K.dma(sp, x2.v(), c_mv[l, mt * 128:(mt + 1) * 128, :])
            K.cp(dve, s["Vm"][:, mt, :], x2.v())
            WP.pop()

    def zero_states(l):
        s = ST[l]
        for k in ("ssd", "ssd_b", "gla", "gla_b", "rw", "rw_b", "conv", "shift"):
            K.memset(pool, s[k].v(), 0.0)

    def load_states(l):
        s = ST[l]
        WP.push()
        t = WP.F()
        tv = t.v().rr("p (j n) -> p j n", j=8)
        K.dma(sp, tv, st_ssd[l].rr("h p n -> (h p) n").rr("(j q) n -> q j n", q=128))
        for half in range(2):
            ps = PS(1)
            for q in range(4):
                K.tr(ps[:, q * 128:(q + 1) * 128], tv[:, half * 4 + q, :], ident.v())
            K.evac(s["ssd"][:, half * 512:(half + 1) * 512], ps[:, 0:512])
        K.cp(act, s["ssd_b"].v(), s["ssd"].v())
        K.dma(sp, s["gla"].v().rr("p (h v) -> p h v", h=4), st_gla[l].rr("h k v -> k h v"))
        K.cp(act, s["gla_b"].v(), s["gla"].v())
        t2 = WP.F()
        t2v = t2[0:64, :].rr("p (h k) -> p h k", h=16)
        K.dma(sp, t2v, st_rwkv[l].rr("h v k -> v h k"))
        ps = PS(1)
        for j in range(8):
            K.tr(ps[:, j * 64:(j + 1) * 64], t2[0:64, j * 128:(j + 1) * 128], ident[0:64, 0:64])
        K.evac(s["rw"].v(), ps[:, 0:512].rr("p (j v) -> p j v", j=8))
        K.cp(act, s["rw_b"].v(), s["rw"].v())
        for r in range(3):
            load_fm(s["conv"][:, :, r], st_conv[l, r], 12)
        load_fm(s["shift"].v(), st_shift[l], 25)
        WP.pop()

    def store_states(l, g):
        s = ST[l]
        WP.push()
        t = WP.F()
        tv = t.v().rr("p (j n) -> p j n", j=8)
        for half in range(2):
            ps = PS(1)
            for q in range(4):
                K.tr(ps[:, q * 128:(q + 1) * 128], s["ssd"][:, (half * 4 + q) * 128:(half * 4 + q + 1) * 128], ident.v())
            K.evac(tv[:, half * 4:half * 4 + 4, :], ps[:, 0:512].rr("p (q n) -> p q n", q=4))
        K.dma(pool, O[g + "_ssd"][l].rr("h p n -> (h p) n").rr("(j q) n -> q j n", q=128), tv, final=True)
        K.dma(pool, O[g + "_gla"][l].rr("h k v -> k h v"), s["gla"].v().rr("p (h v) -> p h v", h=4), final=True)
        t2 = WP.F()
        ps = PS()
        for j in range(8):
            K.tr(ps[0:64, j * 128:(j + 1) * 128], s["rw"][:, j, :], ident.v())
        K.evac(t2[0:64, :], ps[0:64, :])
        K.dma(pool, O[g + "_rwkv"][l].rr("h v k -> v h k"), t2[0:64, :].rr("p (h k) -> p h k", h=16), final=True)
        for r in range(3):
            store_fm(O[g + "_conv"][l, r], s["conv"][:, :, r], 12)
        store_fm(O[g + "_shift"][l], s["shift"].v(), 25)
        WP.pop()

    def run_branches(l, nt):
        gens = [ssd_branch(l, nt), gla_branch(l, nt), rwkv_branch(l, nt)]
        stacks = [[] for _ in gens]
        base = WP.stack

        def step(i):
            WP.stack = stacks[i]
            try:
                return next(gens[i])
            except StopIteration:
                return "END"
            finally:
                WP.stack = base
        while step(0) not in ("P", "END"):
            pass
        for i in range(len(gens)):
            nxt = i + 1 if i + 1 < len(gens) else None
            cur_done, nxt_ready = False, nxt is None
            while not (cur_done and nxt_ready):
                if not cur_done and step(i) == "END":
                    cur_done = True
                if not nxt_ready and step(nxt) in ("P", "END"):
                    nxt_ready = True

    def run_tile(src, nt, ydst, hi):
        h = hbuf[hi]
        K.dma(sp, h[0:nt, :], src)
        for l in range(2):
            norm_to_uT(h, nt, "norm_mix", l)
            run_branches(l, nt)
            if stage >= 2.4:
                out_and_xattn(l, nt, h)
        final_norm(h, nt, ydst)

    for l in range(2):
        if stage >= 1:
            mem_kv(l)
            zero_states(l)
    for i in range(NT):
        if stage >= 2:
            run_tile(x_prompt[i * 128:(i + 1) * 128, :], 128, y_prompt[i * 128:(i + 1) * 128, :], i % 2)
    for l in range(2):
        if stage >= 2:
            store_states(l, "p")
    for l in range(2):
        if stage >= 4:
            load_cache_kv(l)
            load_states(l)
    if stage >= 5:
        run_tile(x_sample[0:32, :], 32, y_sample[0:32, :], NT % 2)
        for l in range(2):
            store_states(l, "s")
    for (sem, val) in K.final:
        sp.prog.append(("waitD", sem, val))

    K.finalize_ranks()
    with nc.Block() as block:
        @block.tensor
        def _(e):
            K.assemble(pe, e)

        @block.scalar
        def _(e):
            K.assemble(act, e)

        @block.vector
        def _(e):
            K.assemble(dve, e)

        @block.gpsimd
        def _(e):
            K.assemble(pool, e)

        @block.sync
        def _(e):
            K.assemble(sp, e)
    es.close()
    return nc


WNAMES = ["norm_mix", "w_in", "ssd_conv_w", "ssd_conv_b", "ssd_dt_bias", "ssd_A_log", "ssd_D", "ssd_norm",
          "w_proj_ssd", "gla_gk_w2", "gla_gk_b", "gla_norm", "w_proj_gla", "rwkv_mu", "rwkv_w0", "rwkv_w2",
          "rwkv_a0", "rwkv_a2", "rwkv_k_k", "rwkv_k_a", "rwkv_r_k", "rwkv_ln_w", "rwkv_ln_b", "w_proj_rwkv",
          "b_merge", "w_out", "norm_xattn", "xa_wq", "xa_wo", "norm_mem", "xa_wk", "xa_wv", "norm_final"]


def run(inputs, NT=32, ncores=8, trace=False, stage=9):
    f = lambda a: np.ascontiguousarray(np.asarray(a, dtype=np.float32))
    nc = build(NT, stage=stage)
    shared = {}
    for nm in WNAMES:
        a = f(inputs[nm])
        if nm == "rwkv_r_k":
            a = a.reshape(2, 1024)
        if nm == "b_merge":
            a = a.reshape(2, 3072)
        shared[nm] = a
    in_maps = []
    for c in range(ncores):
        m = dict(shared)
        m["x_prompt"] = f(inputs["x_prompt"][c, :NT * 128])
        m["x_sample"] = f(inputs["x_sample"][c])
        m["mem_prompt"] = f(inputs["mem_prompt"][c])
        m["state_ssd"] = f(inputs["state_ssd"][:, c])
        m["state_ssd_conv"] = f(inputs["state_ssd_conv"][:, c])
        m["state_gla"] = f(inputs["state_gla"][:, c])
        m["state_rwkv"] = f(inputs["state_rwkv"][:, c])
        m["state_rwkv_shift"] = f(inputs["state_rwkv_shift"][:, c]).reshape(2, 3200)
        m["cache_mem_k"] = f(inputs["cache_mem_k"][:, c]).reshape(2, 256, 1024)
        m["cache_mem_v"] = f(inputs["cache_mem_v"][:, c]).reshape(2, 256, 1024)
        in_maps.append(m)
    res = run_bass_kernel_spmd(nc, in_maps, core_ids=list(range(ncores)), trace=trace)
    R = res.results
    st = lambda k: np.stack([r[k] for r in R], axis=1)
    outs = (
        np.stack([r["y_prompt"] for r in R], 0),
        np.stack([r["y_sample"] for r in R], 0),
        st("p_ssd"), st("p_conv"), st("p_gla"), st("p_rwkv"), st("p_shift").reshape(2, ncores, 1, 3200),
        st("mem_k").reshape(2, ncores, 256, 4, 256), st("mem_v").reshape(2, ncores, 256, 4, 256),
        st("s_ssd"), st("s_conv"), st("s_gla"), st("s_rwkv"), st("s_shift").reshape(2, ncores, 1, 3200),
    )
    return tuple(np.ascontiguousarray(o.astype(np.float32)) for o in outs), res


def kernel(**inputs):
    outs, _ = run(inputs, NT=32, ncores=8)
    return outs
```
